# Optimizing a Trainium2 kernel written in Bass

```python
import functools
import jax, jax.numpy as jnp
from jax import lax
import numpy as np

D_MODEL = 1024
BATCH = 16
SEQ = 2048
DEPTH = 1
DEC_BATCH = 32
DEC_SEQ = 8
PAST_LEN = 16384
PAGE_SIZE = 128

D_MIX = D_MODEL
HD_A = 64
H_A = (D_MIX // 2) // HD_A
PATTERNS = ((128, 1), (512, 4), (2048, 16))
MAX_WINDOW = max(w for w, _ in PATTERNS)
WIN_BUF = min(MAX_WINDOW, PAST_LEN)
H_B = 4
DV_B = (D_MIX - H_A * HD_A) // H_B
DK_B = DV_B // 2
GATE_RANK = 16
GLA_TAU = 16.0
CHUNK = 16
D_FF = 11 * D_MODEL // 4
CONV_W = 3
EPS = 1e-6
NEG = -1e30
SPLITS = (H_A * HD_A, H_A * HD_A, H_A * HD_A, H_B * DK_B, H_B * DK_B, H_B * DV_B, H_B * DV_B, GATE_RANK)
D_IN = sum(SPLITS)

kernel_name = "hymba_dilated_gla_convffn_step"


def rmsnorm(x, g):
    xf = x.astype(jnp.float32)
    y = xf * lax.rsqrt(jnp.mean(xf * xf, axis=-1, keepdims=True) + EPS)
    return (y * g.astype(jnp.float32)).astype(x.dtype)


def dilated_prompt(q, k, v, window, dil):
    B, S, H, E = q.shape
    n = window // dil
    span = n * dil
    Sp = -(-S // span) * span
    nb = Sp // span

    def to_blocks(t):
        t = jnp.pad(t.astype(jnp.float32), ((0, 0), (0, Sp - S), (0, 0), (0, 0)))
        t = t.reshape(B, nb, n, dil, H, E)
        return t.transpose(0, 3, 4, 1, 2, 5)

    def with_prev(t):
        prev = jnp.pad(t, ((0, 0), (0, 0), (0, 0), (1, 0), (0, 0), (0, 0)))[:, :, :, :-1]
        return jnp.concatenate([prev, t], axis=4)

    qb, kb, vb = to_blocks(q), to_blocks(k), to_blocks(v)
    kk, vv = with_prev(kb), with_prev(vb)
    s = jnp.einsum('brhcie,brhcje->brhcij', qb, kk) * (E ** -0.5)
    i = jnp.arange(n)[:, None]
    j = jnp.arange(2 * n)[None, :]
    delta = n + i - j
    band = (delta >= 0) & (delta <= n)
    first = (jnp.arange(nb) == 0)[:, None, None] & (j < n)[None]
    mask = band[None] & ~first
    s = jnp.where(mask, s, NEG)
    lse = jax.nn.logsumexp(s, axis=-1)
    p = jnp.exp(s - lse[..., None])
    o = jnp.einsum('brhcij,brhcje->brhcie', p, vv)
    o = o.transpose(0, 3, 4, 1, 2, 5).reshape(B, Sp, H, E)[:, :S]
    lse = lse.transpose(0, 3, 4, 1, 2).reshape(B, Sp, H)[:, :S]
    return o, lse


def dilated_sample(q, kc, vc, window, dil, lb):
    B, T, H, E = q.shape
    n = window // dil
    idx = lb + jnp.arange(T)[:, None] - dil * jnp.arange(n + 1)[None, :]
    valid = idx >= 0
    idxc = jnp.maximum(idx, 0).reshape(-1)
    kg = jnp.take(kc, idxc, axis=1).reshape(B, T, n + 1, H, E).astype(jnp.float32)
    vg = jnp.take(vc, idxc, axis=1).reshape(B, T, n + 1, H, E).astype(jnp.float32)
    s = jnp.einsum('bthe,btjhe->bhtj', q.astype(jnp.float32), kg) * (E ** -0.5)
    s = jnp.where(valid[None, None], s, NEG)
    lse = jax.nn.logsumexp(s, axis=-1)
    p = jnp.exp(s - lse[..., None])
    o = jnp.einsum('bhtj,btjhe->bthe', p, vg)
    return o, lse.transpose(0, 2, 1)


def combine_patterns(results):
    outs = jnp.stack([o for o, _ in results], axis=0)
    lses = jnp.stack([l for _, l in results], axis=0)
    w = jax.nn.softmax(lses, axis=0)
    return jnp.sum(w[..., None] * outs, axis=0)


def prompt_attn(q, k, v):
    return combine_patterns([dilated_prompt(q, k, v, w, d) for w, d in PATTERNS])


def sample_attn(q, k, v, ck, cv):
    kc = jnp.concatenate([ck.astype(k.dtype), k], axis=1)
    vc = jnp.concatenate([cv.astype(v.dtype), v], axis=1)
    lb = ck.shape[1]
    return combine_patterns([dilated_sample(q, kc, vc, w, d, lb) for w, d in PATTERNS])


def gla_recurrent(q, k, v, logf, s0):
    B, T, H, DK = q.shape
    Tp = -(-T // CHUNK) * CHUNK
    nC = Tp // CHUNK

    def chunks(t):
        t = jnp.pad(t.astype(jnp.float32), ((0, 0), (0, Tp - T), (0, 0), (0, 0)))
        return t.reshape(B, nC, CHUNK, H, t.shape[-1]).transpose(1, 0, 3, 2, 4)

    qc = chunks(q * (DK ** -0.5))
    kc, vc, gc = chunks(k), chunks(v), chunks(logf)
    causal = jnp.tril(jnp.ones((CHUNK, CHUNK), dtype=bool))

    def step(S, inp):
        qi, ki, vi, gi = inp
        b = jnp.cumsum(gi, axis=2)
        qe = qi * jnp.exp(b)
        ke = ki * jnp.exp(-b)
        att = jnp.where(causal, jnp.einsum('bhik,bhjk->bhij', qe, ke), 0.0)
        o = jnp.einsum('bhij,bhjv->bhiv', att, vi) + jnp.einsum('bhik,bhkv->bhiv', qe, S)
        bl = b[:, :, -1:, :]
        S = jnp.exp(bl[:, :, 0, :])[..., None] * S + jnp.einsum('bhjk,bhjv->bhkv', ki * jnp.exp(bl - b), vi)
        return S, o

    S, o = lax.scan(step, s0.astype(jnp.float32), (qc, kc, vc, gc))
    o = o.transpose(1, 0, 3, 2, 4).reshape(B, Tp, H, -1)[:, :T]
    return o, S


def conv_ffn(xn, conv_buf, w_up, conv_w, conv_b, w_down):
    T = xn.shape[1]
    up = xn @ w_up
    g, u = jnp.split(up, 2, axis=-1)
    gp = jnp.concatenate([conv_buf.astype(g.dtype), g], axis=1)
    c = conv_b + sum(conv_w[j] * gp[:, j:j + T] for j in range(CONV_W))
    h = jax.nn.gelu(c, approximate=False) * u
    return h @ w_down, gp[:, T:]


def layer(x, attn_fn, s_gla, conv_buf, w_in, w_a2, b_a, g_gla_norm, w_o,
          g_pre_mix, g_post_mix, g_pre_ffn, g_post_ffn, w_up, conv_w, conv_b, w_down):
    B, T, _ = x.shape
    xn = rmsnorm(x, g_pre_mix)
    proj = xn @ w_in
    split_idx = [int(i) for i in np.cumsum(SPLITS)[:-1]]
    q_a, k_a, v_a, q_b, k_b, v_b, r_b, a_lr = jnp.split(proj, split_idx, axis=-1)
    heads = lambda t, h: t.reshape(B, T, h, -1)
    q_a, k_a, v_a = heads(q_a, H_A), heads(k_a, H_A), heads(v_a, H_A)
    o_a = attn_fn(q_a, k_a, v_a)
    logf = jax.nn.log_sigmoid((a_lr @ w_a2 + b_a).astype(jnp.float32)) / GLA_TAU
    o_b, s_new = gla_recurrent(heads(q_b, H_B), heads(k_b, H_B), heads(v_b, H_B), heads(logf, H_B), s_gla)
    o_b = rmsnorm(o_b, g_gla_norm) * jax.nn.silu(heads(r_b, H_B).astype(jnp.float32))
    mixed = jnp.concatenate([o_a.reshape(B, T, -1), o_b.reshape(B, T, -1)], axis=-1).astype(x.dtype) @ w_o
    x = x + rmsnorm(mixed, g_post_mix)
    f, conv_new = conv_ffn(rmsnorm(x, g_pre_ffn), conv_buf, w_up, conv_w, conv_b, w_down)
    x = x + rmsnorm(f, g_post_ffn)
    return x, k_a, v_a, s_new, conv_new


def setup_inputs(seed: int = 0) -> dict:
    key = jax.random.key(seed)
    ks = jax.random.split(key, 20)
    nrm = lambda k, shape, scale: jax.random.normal(k, shape, jnp.float32) * scale
    return {
        "x_prompt": nrm(ks[0], (BATCH, SEQ, D_MODEL), 1.0),
        "x_sample": nrm(ks[1], (DEC_BATCH, DEC_SEQ, D_MODEL), 1.0),
        "cache_k_win": nrm(ks[2], (DEPTH, DEC_BATCH, WIN_BUF, H_A, HD_A), 1.0),
        "cache_v_win": nrm(ks[3], (DEPTH, DEC_BATCH, WIN_BUF, H_A, HD_A), 1.0),
        "state_gla": nrm(ks[4], (DEPTH, DEC_BATCH, H_B, DK_B, DV_B), 1.0),
        "state_ffn_conv": nrm(ks[5], (DEPTH, DEC_BATCH, CONV_W - 1, D_FF), 1.0),
        "w_in": nrm(ks[6], (DEPTH, D_MODEL, D_IN), D_MODEL ** -0.5),
        "w_a2": nrm(ks[7], (DEPTH, GATE_RANK, H_B * DK_B), GATE_RANK ** -0.5),
        "b_a": 2.0 + nrm(ks[8], (DEPTH, H_B * DK_B), 0.5),
        "g_gla_norm": 1.0 + nrm(ks[9], (DEPTH, H_B, DV_B), 0.05),
        "w_o": nrm(ks[10], (DEPTH, D_MIX, D_MODEL), D_MIX ** -0.5),
        "g_pre_mix": 1.0 + nrm(ks[11], (DEPTH, D_MODEL), 0.05),
        "g_post_mix": 1.0 + nrm(ks[12], (DEPTH, D_MODEL), 0.05),
        "g_pre_ffn": 1.0 + nrm(ks[13], (DEPTH, D_MODEL), 0.05),
        "g_post_ffn": 1.0 + nrm(ks[14], (DEPTH, D_MODEL), 0.05),
        "w_up": nrm(ks[15], (DEPTH, D_MODEL, 2 * D_FF), D_MODEL ** -0.5),
        "conv_w": nrm(ks[16], (DEPTH, CONV_W, D_FF), CONV_W ** -0.5),
        "conv_b": nrm(ks[17], (DEPTH, D_FF), 0.02),
        "w_down": nrm(ks[18], (DEPTH, D_FF, D_MODEL), D_FF ** -0.5),
    }


def reference(x_prompt, x_sample, cache_k_win, cache_v_win, state_gla, state_ffn_conv,
              w_in, w_a2, b_a, g_gla_norm, w_o, g_pre_mix, g_post_mix, g_pre_ffn, g_post_ffn,
              w_up, conv_w, conv_b, w_down):
    xp, xs = x_prompt, x_sample
    Bp, Sp_len, _ = x_prompt.shape
    prompt_buf = min(MAX_WINDOW, Sp_len)
    kp_l, vp_l, sp_l, cp_l = [], [], [], []
    ks_l, vs_l, ss_l, cs_l = [], [], [], []
    for l in range(DEPTH):
        params = (w_in[l], w_a2[l], b_a[l], g_gla_norm[l], w_o[l], g_pre_mix[l], g_post_mix[l],
                  g_pre_ffn[l], g_post_ffn[l], w_up[l], conv_w[l], conv_b[l], w_down[l])
        s0 = jnp.zeros((Bp, H_B, DK_B, DV_B), jnp.float32)
        c0 = jnp.zeros((Bp, CONV_W - 1, D_FF), xp.dtype)
        xp, k_a, v_a, s_new, c_new = layer(xp, prompt_attn, s0, c0, *params)
        kp_l.append(k_a[:, Sp_len - prompt_buf:])
        vp_l.append(v_a[:, Sp_len - prompt_buf:])
        sp_l.append(s_new)
        cp_l.append(c_new)
        attn_s = functools.partial(sample_attn, ck=cache_k_win[l], cv=cache_v_win[l])
        xs, k_a, v_a, s_new, c_new = layer(xs, attn_s, state_gla[l], state_ffn_conv[l], *params)
        ks_l.append(k_a)
        vs_l.append(v_a)
        ss_l.append(s_new)
        cs_l.append(c_new)
    return (xp, xs,
            jnp.stack(kp_l), jnp.stack(vp_l), jnp.stack(sp_l), jnp.stack(cp_l),
            jnp.stack(ks_l), jnp.stack(vs_l), jnp.stack(ss_l), jnp.stack(cs_l))
```

```python
import numpy as np
import ml_dtypes
from contextlib import ExitStack
import concourse.bass as bass
import concourse.mybir as mybir
from concourse.bass_utils import run_bass_kernel_spmd

F32 = mybir.dt.float32
BF16 = mybir.dt.bfloat16
AF = mybir.ActivationFunctionType
ALU = mybir.AluOpType
AX = mybir.AxisListType

NCORES = 8
D = 1024
SEQ = 2048
NSEQ = 2
NSS = 4
TS = 8
NST = NSS * TS
HA = 8
DFF = 2816
NFC = DFF // 128
DIN = 3088
EPS = 1e-6
G = 512
DBG = {"nseq": NSEQ, "ngrp": 4, "phaseB": True, "stage": 9, "gla": True, "gs": 9, "gs2": 9, "sample": True}
QA, KA, VA, QB, KB, VB, RB, AL = 0, 512, 1024, 1536, 1792, 2048, 2560, 3072


class Buf:
    __slots__ = ("name", "w", "r", "dsem", "dcnt", "excl")

    def __init__(self, name):
        self.name = name
        self.excl = False
        self.w = None
        self.r = []
        self.dsem = None
        self.dcnt = 0


class Eng:
    def __init__(self, name, h, sem):
        self.name, self.h, self.sem = name, h, sem
        self.count = 0
        self.waited = {}


class Sched:
    def __init__(self, nc, es):
        self.nc, self.es = nc, es
        mk = lambda n, h: Eng(n, h, es.enter_context(nc.semaphore("s_" + n)))
        self.pe = mk("pe", nc.tensor)
        self.act = mk("act", nc.scalar)
        self.dve = mk("dve", nc.vector)
        self.pool = mk("pool", nc.gpsimd)
        self.sp = mk("sp", nc.sync)
        self.semcur = {}
        self.out_tickets = {}
        self.nsem = 5

    def _wait(self, eng, tk, same_ok):
        if tk is None:
            return
        sem, val, en = tk
        if en == eng.name and eng.name == "pe":
            return
        if en == "dma":
            val = max(val, self.semcur[id(sem)][1])
        if eng.waited.get(id(sem), 0) >= val:
            return
        eng.waited[id(sem)] = val
        eng.h.wait_ge(sem, val)

    def _deps(self, eng, reads, writes):
        for b in reads:
            b = b.buf if hasattr(b, "buf") else b
            self._wait(eng, b.w, True)
            if b.excl:
                for tk in b.r:
                    if tk[2] != eng.name:
                        self._wait(eng, tk, True)
        for b in writes:
            b = b.buf if hasattr(b, "buf") else b
            self._wait(eng, b.w, False)
            for tk in b.r:
                self._wait(eng, tk, False)

    def _commit(self, tk, reads, writes):
        for b in reads:
            b = b.buf if hasattr(b, "buf") else b
            b.r.append(tk)
        for b in writes:
            b = b.buf if hasattr(b, "buf") else b
            b.w = tk
            b.r = []

    def op(self, eng, fn, reads=(), writes=(), inc=True):
        self._deps(eng, reads, writes)
        ins = fn()
        if inc:
            eng.count += 1
            ins.then_inc(eng.sem, 1)
            tk = (eng.sem, eng.count, eng.name)
        else:
            tk = (eng.sem, eng.count + 1, eng.name)
        self._commit(tk, reads, writes)
        return tk

    def barrier(self):
        engs = [self.pe, self.act, self.dve, self.pool, self.sp]
        for e in engs:
            for o in engs:
                if o is e or o.count == 0:
                    continue
                if e.waited.get(id(o.sem), 0) < o.count:
                    e.waited[id(o.sem)] = o.count
                    e.h.wait_ge(o.sem, o.count)
            for sem, val in self.semcur.values():
                if e.waited.get(id(sem), 0) < val:
                    e.waited[id(sem)] = val
                    e.h.wait_ge(sem, val)

    def dma(self, eng, out, in_, reads, writes, sembuf=None, **kw):
        sb = sembuf if sembuf is not None else writes[0]
        sb = sb.buf if hasattr(sb, "buf") else sb
        if sb.dsem is None:
            sb.dsem = self.es.enter_context(self.nc.semaphore("d_" + sb.name))
            self.nsem += 1
        saved = []
        for b in writes:
            b = b.buf if hasattr(b, "buf") else b
            if b.w is not None and b.w[2] == "dma" and b.w[0] is sb.dsem:
                saved.append((b, b.w))
                b.w = None
        self._deps(eng, reads, writes)
        for b, w in saved:
            b.w = w
        sb.dcnt += 16
        eng.h.dma_start(out=out, in_=in_, **kw).then_inc(sb.dsem, 16)
        self.semcur[id(sb.dsem)] = (sb.dsem, sb.dcnt)
        tk = (sb.dsem, sb.dcnt, "dma")
        self._commit(tk, reads, writes)
        return tk


class T:
    def __init__(self, S, es, name, shape, dtype=F32, psum=False):
        if psum:
            self.t = es.enter_context(S.nc.psum_tensor(name, shape, dtype))
        else:
            self.t = es.enter_context(S.nc.sbuf_tensor(name, shape, dtype))
        self.buf = Buf(name)
        self.buf.excl = psum
        self.shape = shape

    def __getitem__(self, idx):
        return self.t[idx]


def build_program():
    nc = bass.Bass("TRN2", target_bir_lowering=False)
    dt_in = lambda n, s, d=F32: nc.dram_tensor(n, s, d, kind="ExternalInput").ap()
    dt_out = lambda n, s: nc.dram_tensor(n, s, F32, kind="ExternalOutput").ap()
    xp = dt_in("xp", [NSEQ, SEQ, D])
    xs = dt_in("xs", [NST, D])
    ck = dt_in("ck", [NSS, SEQ, 512])
    cv = dt_in("cv", [NSS, SEQ, 512])
    sg = dt_in("sg", [NSS, 4, 64, 128])
    sc = dt_in("sc", [NSS * 2, DFF])
    w_in = dt_in("w_in", [D, DIN])
    w_a2 = dt_in("w_a2", [16, 256])
    b_a = dt_in("b_a", [1, 256])
    g_gla = dt_in("g_gla", [1, 512])
    w_o = dt_in("w_o", [D, D])
    g_pre_mix = dt_in("g_pre_mix", [1, D])
    g_post_mix = dt_in("g_post_mix", [1, D])
    g_pre_ffn = dt_in("g_pre_ffn", [1, D])
    g_post_ffn = dt_in("g_post_ffn", [1, D])
    w_up = dt_in("w_up", [D, 2 * DFF])
    conv_w = dt_in("conv_w", [3, DFF])
    conv_b = dt_in("conv_b", [1, DFF])
    w_down = dt_in("w_down", [DFF, D])
    c_ident = dt_in("c_ident", [128, 128], BF16)
    c_amask = dt_in("c_amask", [128, 19, 128], BF16)
    c_uinc = dt_in("c_uinc", [128, 128])
    c_uaft = dt_in("c_uaft", [128, 128])
    c_caus = dt_in("c_caus", [128, 128], BF16)
    c_ones = dt_in("c_ones", [1, 128])
    c_smask = dt_in("c_smask", [128, NSS * 16, NST], BF16)
    c_nmask = dt_in("c_nmask", [NST, NST], BF16)
    c_suinc = dt_in("c_suinc", [NST, NST])
    c_suaft = dt_in("c_suaft", [NST, NST])
    c_scaus = dt_in("c_scaus", [NST, NST], BF16)
    c_ssel = dt_in("c_ssel", [128, NSS, NST])
    c_srow = dt_in("c_srow", [NST, NSS])
    c_id8 = dt_in("c_id8", [8, 8])
    c_id32 = dt_in("c_id32", [32, 32])
    y_p = dt_out("y_p", [NSEQ, SEQ, D])
    y_s = dt_out("y_s", [NST, D])
    wk_p = dt_out("wk_p", [NSEQ, SEQ, 512])
    wv_p = dt_out("wv_p", [NSEQ, SEQ, 512])
    gs_p = dt_out("gs_p", [NSEQ, 4, 64, 128])
    fc_p = dt_out("fc_p", [NSEQ, 2, DFF])
    wk_s = dt_out("wk_s", [NST, 512])
    wv_s = dt_out("wv_s", [NST, 512])
    gs_s = dt_out("gs_s", [NSS, 4, 64, 128])
    fc_s = dt_out("fc_s", [NSS, 2, DFF])
    x1_p = nc.dram_tensor("x1_p", [NSEQ, SEQ, D], F32, kind="Internal").ap()
    x1_s = nc.dram_tensor("x1_s", [NST, D], F32, kind="Internal").ap()

    with ExitStack() as es0:
        S = Sched(nc, es0)
        PE, ACT, DVE, POOL, SP = S.pe, S.act, S.dve, S.pool, S.sp
        V, A, P, GP = nc.vector, nc.scalar, nc.tensor, nc.gpsimd
        out_bufs = []

        def store(eng, dst, src_ap, src_t):
            ob = getattr(src_t, "obuf", None)
            if ob is None:
                ob = Buf(src_t.buf.name + "_o")
                src_t.obuf = ob
            S.dma(eng, dst, src_ap, reads=[src_t], writes=[], sembuf=ob)
            if ob not in out_bufs:
                out_bufs.append(ob)

        ident = T(S, es0, "ident", [128, 128], BF16)
        S.dma(SP, ident[:], c_ident[:, :], [], [ident])
        ones_r = T(S, es0, "ones_r", [1, 128])
        S.dma(SP, ones_r[:], c_ones[:, :], [], [ones_r])

        pairs = [es0.enter_context(nc.psum_tensor("pp%d" % i, [128, 1024], F32)) for i in range(4)]

        class _Bank:
            def __init__(self, i):
                self.buf = Buf("ps%d" % i)
                self.buf.excl = True
                self.v = pairs[i // 2][:, (i % 2) * 512:(i % 2 + 1) * 512]

            def __getitem__(self, idx):
                return self.v[idx]
        ps = [_Bank(i) for i in range(8)]

        class _View:
            def __init__(self, t):
                self.buf = t.buf
                self.v = t[:].bitcast(BF16).rearrange("p (c i) -> p c i", c=8)

            def __getitem__(self, idx):
                return self.v[idx]
        ptp = [_View(ps[6]), _View(ps[7]), _View(ps[0]), _View(ps[1])]

        def rms_rstd(ss_ap, out_ap, n, tmp_ap, ss_t, out_t, tmp_t, nfeat):
            S.op(ACT, lambda: A.activation(out=tmp_ap, in_=ss_ap, func=AF.Ln, scale=1.0 / nfeat, bias=epsb[0:n, 0:1]),
                 [ss_t, epsb], [tmp_t])
            S.op(ACT, lambda: A.activation(out=out_ap, in_=tmp_ap, func=AF.Exp, scale=-0.5), [tmp_t], [out_t])

        epsb = T(S, es0, "epsb", [128, 1])
        S.op(DVE, lambda: V.memset(epsb[:], EPS), [], [epsb])
        oneb = T(S, es0, "oneb", [128, 1])
        S.op(DVE, lambda: V.memset(oneb[:], 1.0), [], [oneb])

        def transpose_tile(src_t, src_ap_fn, np_, dst_t, dst_ap, nchunk, slot):
            pt = ptp[slot]
            for c in range(nchunk):
                S.op(PE, lambda c=c: P.transpose(pt[:, c, 0:np_], src_ap_fn(c), ident[0:np_, 0:np_]),
                     [src_t, ident], [pt], inc=(c == nchunk - 1))
            S.op(ACT, lambda: A.activation(out=dst_ap, in_=pt[:, 0:nchunk, 0:np_], func=AF.Copy), [pt], [dst_t])

        with ExitStack() as es:
            win = T(S, es, "win", [128, 8, DIN], BF16)
            wo = T(S, es, "wo", [128, 8, D], BF16)
            class _Blk:
                def __init__(self, name):
                    self.buf = Buf(name)
            w_in3 = w_in.rearrange("(kc p) n -> p kc n", p=128)
            win_blocks = [(AL, DIN), (QB, VB), (VB, AL), (QA, KA), (KA, VA), (VA, QB)]
            win_bufs = []
            for bi, (c0, c1) in enumerate(win_blocks):
                wb = _Blk("winb%d" % bi)
                win_bufs.append(wb)
                for k0 in range(0, 8, 4):
                    S.dma(POOL, win[:, k0:k0 + 4, c0:c1], w_in3[:, k0:k0 + 4, c0:c1], [], [wb])

            def winb(col0):
                for (c0, c1), wb in zip(win_blocks, win_bufs):
                    if c0 <= col0 < c1:
                        return wb
            for kc in range(8):
                S.dma(POOL, wo[:, kc, :], w_o[kc * 128:(kc + 1) * 128, :], [], [wo])
            wa2 = T(S, es, "wa2", [16, 256])
            S.dma(SP, wa2[:], w_a2[:, :], [], [wa2])
            bar = T(S, es, "bar", [1, 256])
            S.dma(SP, bar[:], b_a[:, :], [], [bar])
            gpre = T(S, es, "gpre", [128, D])
            S.dma(SP, gpre[:], g_pre_mix[0:1, :].partition_broadcast(128), [], [gpre])
            gpost = T(S, es, "gpost", [128, D])
            S.dma(SP, gpost[:], g_post_mix[0:1, :].partition_broadcast(128), [], [gpost])
            ggla = T(S, es, "ggla", [128, 512])
            S.dma(SP, ggla[:], g_gla[0:1, :].partition_broadcast(128), [], [ggla])

            xs1 = [T(S, es, "xs1a_%d" % i, [128, D]) for i in range(2)]
            xr = [T(S, es, "xra_%d" % i, [128, D]) for i in range(2)]
            f_ss = [T(S, es, "fss%d" % i, [128, 1]) for i in range(4)]
            f_ln = [T(S, es, "fln%d" % i, [128, 1]) for i in range(4)]
            f_rs = [T(S, es, "frs%d" % i, [128, 1]) for i in range(4)]
            ssq = T(S, es, "ssq", [128, 8])
            lnt = T(S, es, "lnt", [128, 8])
            rstd = T(S, es, "rstd", [128, 8])
            xnb4 = [T(S, es, "xnb%d" % i, [128, D], BF16) for i in range(4)]
            xnb = xnb4[0]
            xnT = T(S, es, "xnT", [128, 8, G], BF16)
            qTa = T(S, es, "qTa", [128, 4, G], BF16)
            qTb = T(S, es, "qTb", [128, 2, G], BF16)
            kTb = T(S, es, "kTb", [128, 2, G], BF16)
            alT = T(S, es, "alT", [16, G])
            ktok = [T(S, es, "ktok%d" % i, [128, 512]) for i in range(2)]
            vtok = [T(S, es, "vtok%d" % i, [128, 512]) for i in range(1)] * 2
            kbt = T(S, es, "kbt", [128, 4, 256], BF16)
            vbt = T(S, es, "vbt", [128, 4, 512], BF16)
            rbt = T(S, es, "rbt", [128, 4, 512], BF16)
            mixed = T(S, es, "mixed", [128, 4, D], BF16)
            mixT = xnT
            pT2 = [T(S, es, "pT2_%d" % i, [128, 2, 512], BF16) for i in range(2)]
            rden = [T(S, es, "rden%d" % i, [128, 4]) for i in range(2)]
            gtmp = T(S, es, "gtmp", [128, 512])
            lzs = [T(S, es, "lz%d" % i, [128, 256]) for i in range(2)]
            ebTs = [T(S, es, "ebT%d" % i, [128, 2, 128]) for i in range(2)]
            enbTs = [T(S, es, "enbT%d" % i, [128, 2, 128]) for i in range(2)]
            qeTds = [T(S, es, "qeTd%d" % i, [128, 2, 2, 128], BF16) for i in range(2)]
            keTs = [T(S, es, "keT%d" % i, [128, 2, 128], BF16) for i in range(2)]
            klbs = [T(S, es, "kl%d" % i, [128, 256], BF16) for i in range(2)]
            attms = [T(S, es, "attm%d" % i, [128, 4, 128], BF16) for i in range(2)]
            osss = [T(S, es, "oss%d" % i, [128, 4]) for i in range(2)]
            olts = [T(S, es, "olt%d" % i, [128, 4]) for i in range(2)]
            orstds = [T(S, es, "orstd%d" % i, [128, 4]) for i in range(2)]
            osqs = [gtmp, T(S, es, "osq1", [128, 512])]
            for i in range(2):
                S.op(POOL, lambda i=i: GP.memset(qeTds[i][:], 0.0), [], [qeTds[i]])
            ytmp = T(S, es, "ytmp", [128, D])

            class _JunkA:
                buf = ytmp.buf

                def __getitem__(self, idx):
                    return ytmp[:].bitcast(BF16)[:, 0:D][idx]
            junk = _JunkA()

            def front_a(x_src, np_, t):
                xs_, xb_ = xs1[t % 2], xnb4[t]
                S.dma(SP, xs_[0:np_, :], x_src, [], [xs_])
                S.op(ACT, lambda: A.activation(out=xb_[0:np_, :], in_=xs_[0:np_, :], func=AF.Square,
                                               accum_out=f_ss[t][0:np_, 0:1]), [xs_], [xb_, f_ss[t]])
                rms_rstd(f_ss[t][0:np_, 0:1], f_rs[t][0:np_, 0:1], np_, f_ln[t][0:np_, 0:1], f_ss[t], f_rs[t], f_ln[t], D)
                S.op(DVE, lambda: V.scalar_tensor_tensor(out=xb_[0:np_, :], in0=xs_[0:np_, :], scalar=f_rs[t][0:np_, 0:1],
                                                         in1=gpre[0:np_, :], op0=ALU.mult, op1=ALU.mult),
                     [xs_, f_rs[t], gpre], [xb_])

            def front_b(np_, t):
                xb_ = xnb4[t]
                transpose_tile(xb_, lambda c: xb_[0:np_, c * 128:(c + 1) * 128], np_, xnT,
                               xnT[:, :, t * np_:(t + 1) * np_], 8, t % 2)

            def mixer_group(nt, np_, x_src_fn, s_idx, g_idx, first_in_seq, k_dst_fn, v_dst_fn, x1_dst_fn,
                            sample=False, front_done=False, nxt=None):
                ntok = nt * np_
                if not front_done:
                    for t in range(nt):
                        front_a(x_src_fn(t), np_, t)
                for t in range(nt):
                    front_b(np_, t)

                if DBG["stage"] < 2:
                    return
                def fproj(col0, ncols, dst_t, dst_ap, bank):
                    for kc in range(8):
                        S.op(PE, lambda kc=kc: P.matmul(ps[bank][0:ncols, 0:ntok], lhsT=win[:, kc, col0:col0 + ncols],
                                                        rhs=xnT[:, kc, 0:ntok], start=(kc == 0), stop=(kc == 7)),
                             [winb(col0), xnT], [ps[bank]], inc=(kc == 7))
                    S.op(ACT, lambda: A.activation(out=dst_ap, in_=ps[bank][0:ncols, 0:ntok], func=AF.Copy),
                         [ps[bank]], [dst_t])

                if DBG["stage"] < 3:
                    return
                def tproj(t, col0, ncols, bank, evac):
                    for kc in range(8):
                        S.op(PE, lambda kc=kc: P.matmul(ps[bank][0:np_, 0:ncols], lhsT=xnT[:, kc, t * np_:(t + 1) * np_],
                                                        rhs=win[:, kc, col0:col0 + ncols], start=(kc == 0), stop=(kc == 7)),
                             [winb(col0), xnT], [ps[bank]], inc=(kc == 7))
                    evac(ps[bank])

                evs = []
                for t in range(nt):
                    jb = (g_idx * 4 + t) if not sample else 0
                    kt, vt = ktok[t % 2], vtok[t % 2]

                    def ev_k(pb, kt=kt, t=t):
                        S.op(ACT, lambda: A.activation(out=kt[0:np_, :], in_=pb[0:np_, 0:512], func=AF.Copy), [pb], [kt])
                        store(POOL, k_dst_fn(t), kt[0:np_, :], kt)

                    def ev_v(pb, vt=vt, t=t, jb=jb):
                        S.op(ACT, lambda: A.activation(out=vt[0:np_, :], in_=pb[0:np_, 0:512], func=AF.Copy), [pb], [vt])
                        S.op(DVE, lambda: V.tensor_copy(out=vaug_cur[0:np_, jb, :, 0:64],
                                                        in_=pb[0:np_, 0:512].rearrange("p (h e) -> p h e", h=HA)),
                             [pb], [vaug_cur])
                        store(POOL, v_dst_fn(t), vt[0:np_, :], vt)

                    def ev_kb(pb, t=t):
                        S.op(ACT, lambda: A.activation(out=kbt[0:np_, t, :], in_=pb[0:np_, 0:256], func=AF.Copy), [pb], [kbt])

                    def ev_vb(pb, t=t):
                        S.op(DVE, lambda: V.tensor_copy(out=vbt[0:np_, t, :], in_=pb[0:np_, 0:512]), [pb], [vbt])

                    def ev_rb(pb, t=t):
                        S.op(ACT, lambda: A.activation(out=gtmp[0:np_, :], in_=pb[0:np_, 0:512], func=AF.Exp, scale=-1.0), [pb], [gtmp])
                        S.op(DVE, lambda: V.tensor_tensor(out=rbt[0:np_, t, :], in0=pb[0:np_, 0:512], in1=ggla[0:np_, :], op=ALU.mult),
                             [pb, ggla], [rbt])
                        S.op(DVE, lambda: V.tensor_scalar_add(out=gtmp[0:np_, :], in0=gtmp[0:np_, :], scalar1=1.0), [gtmp], [gtmp])
                        S.op(DVE, lambda: V.reciprocal(out=gtmp[0:np_, :], in_=gtmp[0:np_, :]), [gtmp], [gtmp])
                        S.op(DVE, lambda: V.tensor_tensor(out=rbt[0:np_, t, :], in0=rbt[0:np_, t, :], in1=gtmp[0:np_, :], op=ALU.mult),
                             [rbt, gtmp], [rbt])

                    evs.append((ev_k, ev_v, ev_kb, ev_vb, ev_rb))

                bk = 0
                fproj(AL, 16, alT, alT[:, 0:ntok], bk); bk ^= 1
                for c in range(2):
                    fproj(QB + c * 128, 128, qTb, qTb[:, c, 0:ntok], bk); bk ^= 1
                for c in range(2):
                    fproj(KB + c * 128, 128, kTb, kTb[:, c, 0:ntok], bk); bk ^= 1
                for t in range(nt):
                    ev_k, ev_v, ev_kb, ev_vb, ev_rb = evs[t]
                    tproj(t, KB, 256, bk, ev_kb); bk ^= 1
                    tproj(t, VB, 512, bk, ev_vb); bk ^= 1
                    tproj(t, RB, 512, bk, ev_rb); bk ^= 1

                def gla_all():
                    gens = [gla_chunk(t, np_, sample) for t in range(nt)]
                    state_done = [False] * nt
                    waiting = [False] * nt
                    started = min(2, nt)
                    active = list(range(started))
                    while active:
                        for t in list(active):
                            if waiting[t]:
                                if t > 0 and not state_done[t - 1]:
                                    continue
                                waiting[t] = False
                            try:
                                r = next(gens[t])
                            except StopIteration:
                                active.remove(t)
                                if started < nt:
                                    active.append(started)
                                    started += 1
                                continue
                            if r == "need_state":
                                waiting[t] = True
                            elif r == "state_done":
                                state_done[t] = True
                            yield
                gla = gla_all()

                def pump(n):
                    for _ in range(n):
                        if next(gla, "done") == "done":
                            return

                npump = 0 if sample else 3
                for c in range(4):
                    fproj(QA + c * 128, 128, qTa, qTa[:, c, 0:ntok], bk); bk ^= 1
                    pump(npump)
                kcol0 = (g_idx * G) if not sample else 0
                for c in range(4):
                    fproj(KA + c * 128, 128, kTa_cur, kTa_cur[:, c, kcol0:kcol0 + ntok], bk); bk ^= 1
                    pump(npump)
                for t in range(nt):
                    ev_k, ev_v, ev_kb, ev_vb, ev_rb = evs[t]
                    tproj(t, KA, 512, bk, ev_k); bk ^= 1
                    pump(npump)
                    tproj(t, VA, 512, bk, ev_v); bk ^= 1
                    pump(npump)

                if not sample:
                    attention_prompt(g_idx, gla, nxt)
                else:
                    attention_sample(gla)
                    for _ in gla:
                        pass
                if DBG["stage"] < 5:
                    return
                for t in range(nt):
                    transpose_tile(mixed, lambda c, t=t: mixed[0:np_, t, c * 128:(c + 1) * 128], np_, mixT,
                                   mixT[:, :, t * np_:(t + 1) * np_], 8, t % 2)

                def wo_mm(t):
                    S.dma(SP, xr[t % 2][0:np_, :], x_src_fn(t), [], [xr[t % 2]])
                    for hf in range(2):
                        pw = ps[(2 * t + hf) % 6]
                        for kc in range(8):
                            S.op(PE, lambda kc=kc, hf=hf, pw=pw: P.matmul(pw[0:np_, :], lhsT=mixT[:, kc, t * np_:(t + 1) * np_],
                                                                          rhs=wo[:, kc, hf * 512:(hf + 1) * 512],
                                                                          start=(kc == 0), stop=(kc == 7)),
                                 [wo, mixT], [pw], inc=(kc == 7))

                def wo_epi(t):
                    for hf in range(2):
                        pw = ps[(2 * t + hf) % 6]
                        S.op(ACT, lambda hf=hf, pw=pw: A.activation(out=junk[0:np_, hf * 512:(hf + 1) * 512], in_=pw[0:np_, :],
                                                                    func=AF.Square, accum_out=ssq[0:np_, 4 + hf:5 + hf]),
                             [pw], [junk, ssq])
                    S.op(DVE, lambda: V.tensor_tensor(out=ssq[0:np_, 6:7], in0=ssq[0:np_, 4:5], in1=ssq[0:np_, 5:6], op=ALU.add),
                         [ssq], [ssq])
                    rms_rstd(ssq[0:np_, 6:7], rstd[0:np_, 6:7], np_, lnt[0:np_, 6:7], ssq, rstd, lnt, D)
                    for hf in range(2):
                        pw = ps[(2 * t + hf) % 6]
                        S.op(DVE, lambda hf=hf, pw=pw: V.scalar_tensor_tensor(out=ytmp[0:np_, hf * 512:(hf + 1) * 512], in0=pw[0:np_, :],
                                                                              scalar=rstd[0:np_, 6:7],
                                                                              in1=gpost[0:np_, hf * 512:(hf + 1) * 512],
                                                                              op0=ALU.mult, op1=ALU.mult),
                             [pw, rstd, gpost], [ytmp])
                    xr_ = xr[t % 2]
                    S.op(DVE, lambda: V.tensor_tensor(out=xr_[0:np_, :], in0=ytmp[0:np_, :], in1=xr_[0:np_, :], op=ALU.add),
                         [ytmp, xr_], [xr_])
                    S.dma(SP, x1_dst_fn(t), xr_[0:np_, :], [xr_], [x1buf], sembuf=xgo[t % 2])

                wo_mm(0)
                for t in range(nt):
                    if t + 1 < nt:
                        wo_mm(t + 1)
                    wo_epi(t)

            def attention_prompt(g_idx, gla, nxt=None):
                nblk = 4 * g_idx + 4
                blocks = [(pr, jb) for pr in range(4) for jb in range(nblk)]
                stb = [ps[4], ps[5], ps[0], ps[1]]
                n_gla = 4 * 52 - 48
                per_blk = -(-n_gla // len(blocks))

                def geo(jb):
                    d = 4 * g_idx - jb
                    return d, 128 * max(0, -d)

                def A_(b):
                    pr, jb = blocks[b]
                    d, q0 = geo(jb)
                    for e in range(2):
                        st = stb[2 * (b % 2) + e]
                        r0 = 64 * e
                        S.op(PE, lambda st=st, r0=r0: P.matmul(st[:, q0:G], lhsT=kTa[r0:r0 + 64, pr, jb * 128:(jb + 1) * 128],
                                                               rhs=qTa[r0:r0 + 64, pr, q0:G], start=True, stop=True),
                             [kTa, qTa], [st])

                def B_(b):
                    pr, jb = blocks[b]
                    d, q0 = geo(jb)
                    bk0 = 4 if b % 2 == 0 else 0
                    pair = pairs[bk0 // 2]
                    pt = pT2[b % 2]
                    S.op(ACT, lambda: A.activation(out=pt[:, :, q0:G], in_=pair[:, :].rearrange("p (e i) -> p e i", e=2)[:, :, q0:G],
                                                   func=AF.Exp, scale=0.125), [ps[bk0], ps[bk0 + 1]], [pt])
                    S.op(DVE, lambda: V.tensor_tensor(
                        out=pt[:, :, q0:G], in0=pt[:, :, q0:G],
                        in1=amask[:, d + 3:d + 7, :].rearrange("p a b -> p (a b)")[:, q0:G].unsqueeze(1).broadcast_to([128, 2, G - q0]),
                        op=ALU.mult), [pt, amask], [pt])

                def C_(b):
                    pr, jb = blocks[b]
                    d, q0 = geo(jb)
                    ibs = list(range(max(0, -d), 4))
                    pt = pT2[b % 2]
                    for e in range(2):
                        h = 2 * pr + e
                        oacc = ps[2 + e]
                        for ib in ibs:
                            S.op(PE, lambda ib=ib, oacc=oacc, e=e, h=h: P.matmul(
                                oacc[:, ib * 65:(ib + 1) * 65], lhsT=pt[:, e, ib * 128:(ib + 1) * 128], rhs=vaug[:, jb, h, :],
                                start=(jb == 0 and ib == ibs[0]), stop=(jb == 4 * g_idx + ib), skip_group_check=True),
                                 [pt, vaug], [oacc], inc=(ib == ibs[-1]))
                    if jb == nblk - 1:
                        for e in range(2):
                            h = 2 * pr + e
                            oacc = ps[2 + e]
                            o3 = oacc[:, 0:260].rearrange("p (i e) -> p i e", i=4)
                            rd = rden[e]
                            S.op(DVE, lambda o3=o3, rd=rd: V.reciprocal(out=rd[:, :], in_=o3[:, :, 64]), [oacc], [rd])
                            S.op(DVE, lambda o3=o3, rd=rd, h=h: V.tensor_tensor(
                                out=mixed[:, :, h * 64:(h + 1) * 64], in0=o3[:, :, 0:64],
                                in1=rd[:, :].unsqueeze(2).broadcast_to([128, 4, 64]), op=ALU.mult), [oacc, rd], [mixed])

                nbk = len(blocks)
                fpos = {(nbk * (2 * t + 1)) // 9: t for t in range(4)} if nxt is not None else {}
                A_(0)
                for b in range(nbk):
                    if b + 1 < nbk:
                        A_(b + 1)
                    B_(b)
                    C_(b)
                    if b in fpos:
                        front_a(nxt(fpos[b]), 128, fpos[b])
                    for _ in range(per_blk):
                        if next(gla, "done") == "done":
                            break
                for _ in gla:
                    pass

            def attention_sample(gla):
                oaccs = [ps[2], ps[3]]
                sts = [ps[4], ps[5]]
                first = [True, True]

                def scores(nk, kT_t, kT_fn, mask_t, mask_ap, slot):
                    for h in range(HA):
                        c, r0 = h // 2, 64 * (h % 2)
                        st = sts[h % 2]
                        S.op(PE, lambda c=c, r0=r0, st=st: P.matmul(st[0:nk, c * NST:(c + 1) * NST], lhsT=kT_fn(c, r0),
                                                                    rhs=qTa[r0:r0 + 64, c, 0:NST], start=True, stop=True),
                             [kT_t, qTa], [st], inc=(h >= HA - 2))
                    pt = pTs[slot]
                    for hh in range(2):
                        S.op(ACT, lambda hh=hh: A.activation(
                            out=pt[0:nk, :, :].rearrange("p (c hh) t -> p c hh t", hh=2)[:, :, hh, :],
                            in_=sts[hh][0:nk, 0:4 * NST].rearrange("p (c t) -> p c t", c=4), func=AF.Exp, scale=0.125),
                             [sts[hh]], [pt])
                    S.op(DVE, lambda: V.tensor_tensor(out=pt[0:nk, :, :], in0=pt[0:nk, :, :],
                                                      in1=mask_ap.unsqueeze(1).broadcast_to([nk, HA, NST]), op=ALU.mult),
                         [pt, mask_t], [pt])

                def pv(nk, v_t, v_fn, slot, is_last):
                    pt = pTs[slot]
                    for h in range(HA):
                        oa = oaccs[h // 4]
                        col = (h % 4) * 65
                        S.op(PE, lambda h=h, oa=oa, col=col, fl=first[h // 4]: P.matmul(
                            oa[0:NST, col:col + 65], lhsT=pt[0:nk, h, :], rhs=v_fn(h), start=fl, stop=is_last,
                            skip_group_check=True), [pt, v_t], [oa], inc=(h % 4 == 3))
                        first[h // 4] = False

                S.op(POOL, lambda: GP.memset(vaugc[0][:, :, 64:65], 1.0), [], [vaugc[0]])
                S.op(POOL, lambda: GP.memset(vaugc[1][:, :, 64:65], 1.0), [], [vaugc[1]])
                cblocks = [(sq, jb) for sq in range(NSS) for jb in range(16)]

                def load_d(m):
                    sq, jb = cblocks[2 * m]
                    sl2 = m % 2
                    S.dma(POOL, kcb[sl2][:], ck[sq, jb * 128:(jb + 2) * 128, :].rearrange("(j p) f -> p j f", p=128), [], [kcb[sl2]])
                    S.dma(POOL, vcb[sl2][:], cv[sq, jb * 128:(jb + 2) * 128, :].rearrange("(j p) f -> p j f", p=128), [], [vcb[sl2]])

                def prep(n):
                    sl2, j = (n // 2) % 2, n % 2
                    sl = n % 2
                    kb_, vb_, kt_, va_ = kcb[sl2], vcb[sl2], kTc[sl], vaugc[sl]
                    transpose_tile(kb_, lambda c, kb_=kb_, j=j: kb_[:, j, c * 128:(c + 1) * 128], 128, kt_, kt_[:, :, :], 4, 2 + sl)
                    S.op(DVE, lambda vb_=vb_, va_=va_, j=j: V.tensor_copy(out=va_[:, :, 0:64],
                                                                          in_=vb_[:, j, :].rearrange("p (h e) -> p h e", h=HA)),
                         [vb_], [va_])

                def sc_(n):
                    sq, jb = cblocks[n]
                    kt_ = kTc[n % 2]
                    scores(128, kt_, lambda c, r0, kt_=kt_: kt_[r0:r0 + 64, c, :], smask, smask[:, sq * 16 + jb, :], n % 2)

                def pv_(n):
                    va_ = vaugc[n % 2]
                    pv(128, va_, lambda h, va_=va_: va_[:, h, :], n % 2, False)

                nb_ = len(cblocks)
                nd_ = nb_ // 2
                load_d(0)
                load_d(1)
                prep(0)
                prep(1)
                sc_(0)
                for n in range(nb_):
                    if n + 1 < nb_:
                        sc_(n + 1)
                    pv_(n)
                    if n + 2 < nb_:
                        prep(n + 2)
                    if n % 2 == 1 and (n // 2) + 2 < nd_:
                        load_d(n // 2 + 2)
                    next(gla, None)
                    next(gla, None)
                scores(NST, kTa_cur, lambda c, r0: kTa_cur[r0:r0 + 64, c, 0:NST], nmask, nmask[:, :], 0)
                pv(NST, vaug_cur, lambda h: vaug_cur[0:NST, 0, h, :], 0, True)
                for half in range(2):
                    oa = oaccs[half]
                    o3 = oa[0:NST, 0:260].rearrange("p (i e) -> p i e", i=4)
                    S.op(DVE, lambda o3=o3: V.reciprocal(out=rden[0][0:NST, :], in_=o3[:, :, 64]), [oa], [rden[0]])
                    S.op(DVE, lambda o3=o3, half=half: V.tensor_tensor(
                        out=mixed[0:NST, 0, half * 256:(half + 1) * 256].rearrange("p (i e) -> p i e", i=4),
                        in0=o3[:, :, 0:64], in1=rden[0][0:NST, :].unsqueeze(2).broadcast_to([NST, 4, 64]), op=ALU.mult),
                         [oa, rden[0]], [mixed])

            def gla_chunk(t, np_, sample):
                sl = t % 2
                bank = ps[6 + sl]
                lz, ebT, enbT, qeTd, keT, kl, attm = lzs[sl], ebTs[sl], enbTs[sl], qeTds[sl], keTs[sl], klbs[sl], attms[sl]
                oss, olt, orstd, osq = osss[sl], olts[sl], orstds[sl], osqs[sl]
                ez = ekl = lz
                tok = slice(t * np_, (t + 1) * np_)
                U_inc, U_aft, CA = (suinc, suaft, scaus) if sample else (uinc, uaft, caus)
                pz = bank
                S.op(PE, lambda: P.matmul(pz[0:np_, 0:256], lhsT=alT[0:16, tok], rhs=wa2[0:16, :], start=True, stop=False),
                     [alT, wa2], [pz], inc=False)
                S.op(PE, lambda: P.matmul(pz[0:np_, 0:256], lhsT=ones_r[0:1, 0:np_], rhs=bar[0:1, :], start=False, stop=True),
                     [ones_r, bar], [pz])
                yield
                S.op(ACT, lambda: A.activation(out=ez[0:np_, :], in_=pz[0:np_, 0:256], func=AF.Exp, scale=-1.0), [pz], [ez])
                yield
                S.op(ACT, lambda: A.activation(out=lz[0:np_, :], in_=ez[0:np_, :], func=AF.Ln, bias=oneb[0:np_, 0:1]), [ez, oneb], [lz])
                yield
                pb = bank
                for kc in range(2):
                    S.op(PE, lambda kc=kc: P.matmul(pb[:, kc * 128:kc * 128 + np_], lhsT=lz[0:np_, kc * 128:(kc + 1) * 128],
                                                    rhs=U_inc[0:np_, 0:np_], start=True, stop=True),
                         [lz, U_inc], [pb], inc=False)
                pa = bank
                S.op(PE, lambda: P.matmul(pa[0:np_, 256:512], lhsT=U_aft[0:np_, 0:np_], rhs=lz[0:np_, :], start=True, stop=True),
                     [lz, U_aft], [pa])
                yield
                pb3 = pb[:, 0:256].rearrange("p (c i) -> p c i", c=2)[:, :, 0:np_]
                S.op(ACT, lambda: A.activation(out=ebT[:, :, 0:np_], in_=pb3, func=AF.Exp), [pb], [ebT])
                yield
                S.op(ACT, lambda: A.activation(out=enbT[:, :, 0:np_], in_=pb3, func=AF.Exp, scale=-1.0), [pb], [enbT])
                yield
                S.op(ACT, lambda: A.activation(out=ekl[0:np_, :], in_=pa[0:np_, 256:512], func=AF.Exp), [pa], [ekl])
                yield
                for hh in range(2):
                    r0 = 64 * hh
                    S.op(DVE, lambda hh=hh, r0=r0: V.scalar_tensor_tensor(
                        out=qeTd[r0:r0 + 64, :, hh, 0:np_], in0=qTb[r0:r0 + 64, :, tok], scalar=0.125,
                        in1=ebT[r0:r0 + 64, :, 0:np_], op0=ALU.mult, op1=ALU.mult), [qTb, ebT], [qeTd])
                    yield
                S.op(DVE, lambda: V.tensor_tensor(out=keT[:, :, 0:np_], in0=kTb[:, :, tok], in1=enbT[:, :, 0:np_], op=ALU.mult),
                     [kTb, enbT], [keT])
                yield
                S.op(DVE, lambda: V.tensor_tensor(out=kl[0:np_, :], in0=kbt[0:np_, t, :], in1=ekl[0:np_, :], op=ALU.mult),
                     [kbt, ekl], [kl])
                yield
                pat = bank
                if np_ == 128:
                    for c in range(2):
                        S.op(PE, lambda c=c: P.matmul(pat[:, c * 256:(c + 1) * 256], lhsT=keT[:, c, :],
                                                      rhs=qeTd[:, c, :, :].rearrange("p hh i -> p (hh i)"), start=True, stop=True),
                             [keT, qeTd], [pat], inc=(c == 1))
                else:
                    for h in range(4):
                        S.op(PE, lambda h=h: P.matmul(pat[0:np_, h * 128:h * 128 + np_], lhsT=keT[:, h // 2, 0:np_],
                                                      rhs=qeTd[:, h // 2, h % 2, 0:np_], start=True, stop=True),
                             [keT, qeTd], [pat], inc=(h == 3))
                yield
                S.op(DVE, lambda: V.tensor_tensor(
                    out=attm[0:np_, :, 0:np_], in0=pat[0:np_, :].rearrange("p (h i) -> p h i", h=4)[:, :, 0:np_],
                    in1=CA[0:np_, 0:np_].unsqueeze(1).broadcast_to([np_, 4, np_]), op=ALU.mult), [pat, CA], [attm])
                yield
                yield "need_state"
                po = bank
                if not sample:
                    for h in range(4):
                        c, r0 = h // 2, 64 * (h % 2)
                        S.op(PE, lambda h=h: P.matmul(po[0:np_, h * 128:(h + 1) * 128], lhsT=attm[0:np_, h, 0:np_],
                                                      rhs=vbt[0:np_, t, h * 128:(h + 1) * 128], start=True, stop=False),
                             [attm, vbt], [po], inc=False)
                        S.op(PE, lambda h=h, c=c, r0=r0: P.matmul(po[0:np_, h * 128:(h + 1) * 128], lhsT=qeTd[r0:r0 + 64, c, h % 2, 0:np_],
                                                                  rhs=Sbf[r0:r0 + 64, c, :], start=False, stop=True),
                             [qeTd, Sbf], [po], inc=(h == 3))
                        yield
                else:
                    for sq in range(NSS):
                        S.op(DVE, lambda sq=sq: V.tensor_tensor(out=qeTs[:, sq, :, :],
                                                                in0=qeTd[:, :, :, 0:NST].rearrange("p c hh i -> p (c hh) i"),
                                                                in1=ssel[:, sq, :].unsqueeze(1).broadcast_to([128, 4, NST]),
                                                                op=ALU.mult), [qeTd, ssel], [qeTs])
                        yield
                    for h in range(4):
                        c, r0 = h // 2, 64 * (h % 2)
                        S.op(PE, lambda h=h: P.matmul(po[0:NST, h * 128:(h + 1) * 128], lhsT=attm[:, h, 0:NST],
                                                      rhs=vbt[:, t, h * 128:(h + 1) * 128], start=True, stop=False),
                             [attm, vbt], [po], inc=False)
                        for sq in range(NSS):
                            S.op(PE, lambda h=h, c=c, r0=r0, sq=sq: P.matmul(
                                po[0:NST, h * 128:(h + 1) * 128], lhsT=qeTs[r0:r0 + 64, sq, h, :], rhs=Sbf_s[r0:r0 + 64, sq, c, :],
                                start=False, stop=(sq == NSS - 1)), [qeTs, Sbf_s], [po], inc=(h == 3 and sq == NSS - 1))
                        yield
                S.op(ACT, lambda: A.activation(out=osq[0:np_, :], in_=po[0:np_, :], func=AF.Copy), [po], [osq])
                yield
                pd = bank
                if not sample:
                    for c in range(2):
                        S.op(PE, lambda c=c: P.matmul(pd[:, c * 256:(c + 1) * 256], lhsT=kl[0:np_, c * 128:(c + 1) * 128],
                                                      rhs=vbt[0:np_, t, c * 256:(c + 1) * 256], start=True, stop=True),
                             [kl, vbt], [pd], inc=(c == 1))
                    yield
                    for c in range(2):
                        for hh in range(2):
                            r0 = 64 * hh
                            S.op(DVE, lambda c=c, hh=hh, r0=r0: V.scalar_tensor_tensor(
                                out=Sst[r0:r0 + 64, c, :], in0=Sst[r0:r0 + 64, c, :], scalar=ebT[r0:r0 + 64, c, np_ - 1:np_],
                                in1=pd[r0:r0 + 64, c * 256 + hh * 128:c * 256 + (hh + 1) * 128], op0=ALU.mult, op1=ALU.add),
                                 [Sst, ebT, pd], [Sst])
                            yield
                    S.op(ACT, lambda: A.activation(out=Sbf[:], in_=Sst[:], func=AF.Copy), [Sst], [Sbf])
                    yield "state_done"
                else:
                    for sq in range(NSS):
                        S.op(DVE, lambda sq=sq: V.tensor_scalar(out=kls[0:NST, :], in0=kl[0:NST, :], scalar1=srow[0:NST, sq:sq + 1],
                                                                scalar2=None, op0=ALU.mult), [kl, srow], [kls])
                        yield
                        for c in range(2):
                            S.op(PE, lambda c=c: P.matmul(pd[:, c * 256:(c + 1) * 256], lhsT=kls[0:NST, c * 128:(c + 1) * 128],
                                                          rhs=vbt[0:NST, t, c * 256:(c + 1) * 256], start=True, stop=True),
                                 [kls, vbt], [pd], inc=(c == 1))
                        yield
                        for c in range(2):
                            for hh in range(2):
                                r0 = 64 * hh
                                S.op(DVE, lambda c=c, hh=hh, r0=r0, sq=sq: V.scalar_tensor_tensor(
                                    out=Sst_s[r0:r0 + 64, sq, c, :], in0=Sst_s[r0:r0 + 64, sq, c, :],
                                    scalar=ebT[r0:r0 + 64, c, TS * sq + TS - 1:TS * sq + TS],
                                    in1=pd[r0:r0 + 64, c * 256 + hh * 128:c * 256 + (hh + 1) * 128], op0=ALU.mult, op1=ALU.add),
                                     [Sst_s, ebT, pd], [Sst_s])
                                yield
                for h in range(4):
                    S.op(ACT, lambda h=h: A.activation(out=lz[0:np_, 0:128], in_=osq[0:np_, h * 128:(h + 1) * 128], func=AF.Square,
                                                       accum_out=oss[0:np_, h:h + 1]), [osq], [lz, oss])
                    yield
                rms_rstd(oss[0:np_, :], orstd[0:np_, :], np_, olt[0:np_, :], oss, orstd, olt, 128)
                yield
                for h in range(4):
                    S.op(DVE, lambda h=h: V.scalar_tensor_tensor(out=mixed[0:np_, t, 512 + h * 128:512 + (h + 1) * 128],
                                                                 in0=osq[0:np_, h * 128:(h + 1) * 128], scalar=orstd[0:np_, h:h + 1],
                                                                 in1=rbt[0:np_, t, h * 128:(h + 1) * 128], op0=ALU.mult, op1=ALU.mult),
                         [osq, orstd, rbt], [mixed])
                    yield

            x1buf = Buf("x1dram")
            xgo = [Buf("xgo0"), Buf("xgo1")]
            with ExitStack() as esp:
                amask = T(S, esp, "amask", [128, 19, 128], BF16)
                S.dma(SP, amask[:], c_amask[:, :, :], [], [amask])
                uinc = T(S, esp, "uinc", [128, 128])
                S.dma(SP, uinc[:], c_uinc[:, :], [], [uinc])
                uaft = T(S, esp, "uaft", [128, 128])
                S.dma(SP, uaft[:], c_uaft[:, :], [], [uaft])
                caus = T(S, esp, "caus", [128, 128], BF16)
                S.dma(SP, caus[:], c_caus[:, :], [], [caus])
                kTa = T(S, esp, "kTa", [128, 4, SEQ], BF16)
                vaug = T(S, esp, "vaug", [128, 16, HA, 65], BF16)
                S.op(POOL, lambda: GP.memset(vaug[:, :, :, 64:65], 1.0), [], [vaug])
                Sst = T(S, esp, "Sst", [128, 2, 128])
                Sbf = T(S, esp, "Sbf", [128, 2, 128], BF16)
                kTa_cur, vaug_cur = kTa, vaug
                glist = [(sq_, g_) for sq_ in range(DBG["nseq"]) for g_ in range(DBG["ngrp"])]

                def xsrc(sq_, g_):
                    return lambda t: xp[sq_, g_ * G + t * 128:g_ * G + (t + 1) * 128, :]
                for gi, (sq_, g_) in enumerate(glist):
                    if g_ == 0:
                        S.op(DVE, lambda: V.memset(Sst[:], 0.0), [], [Sst])
                        S.op(DVE, lambda: V.memset(Sbf[:], 0.0), [], [Sbf])
                    tok0 = g_ * G
                    nxt = xsrc(*glist[gi + 1]) if gi + 1 < len(glist) else None
                    mixer_group(4, 128, xsrc(sq_, g_), sq_, g_, g_ == 0,
                                lambda t, s=sq_, tok0=tok0: wk_p[s, tok0 + t * 128:tok0 + (t + 1) * 128, :],
                                lambda t, s=sq_, tok0=tok0: wv_p[s, tok0 + t * 128:tok0 + (t + 1) * 128, :],
                                lambda t, s=sq_, tok0=tok0: x1_p[s, tok0 + t * 128:tok0 + (t + 1) * 128, :],
                                front_done=(gi > 0), nxt=nxt)
                    if g_ == DBG["ngrp"] - 1:
                        for c in range(2):
                            for hh in range(2):
                                store(POOL, gs_p[sq_, 2 * c + hh, :, :], Sst[64 * hh:64 * hh + 64, c, :], Sst)

            S.barrier()
            if DBG["sample"]:
                ess = es.enter_context(ExitStack())
                smask = T(S, ess, "smask", [128, NSS * 16, NST], BF16)
                S.dma(SP, smask[:], c_smask[:, :, :], [], [smask])
                nmask = T(S, ess, "nmask", [NST, NST], BF16)
                S.dma(SP, nmask[:], c_nmask[:, :], [], [nmask])
                suinc = T(S, ess, "suinc", [NST, NST])
                S.dma(SP, suinc[:], c_suinc[:, :], [], [suinc])
                suaft = T(S, ess, "suaft", [NST, NST])
                S.dma(SP, suaft[:], c_suaft[:, :], [], [suaft])
                scaus = T(S, ess, "scaus", [NST, NST], BF16)
                S.dma(SP, scaus[:], c_scaus[:, :], [], [scaus])
                ssel = T(S, ess, "ssel", [128, NSS, NST])
                S.dma(SP, ssel[:], c_ssel[:, :, :], [], [ssel])
                srow = T(S, ess, "srow", [NST, NSS])
                S.dma(SP, srow[:], c_srow[:, :], [], [srow])
                Sst_s = T(S, ess, "Sst_s", [128, NSS, 2, 128])
                Sbf_s = T(S, ess, "Sbf_s", [128, NSS, 2, 128], BF16)
                qeTs = T(S, ess, "qeTs", [128, NSS, 4, NST], BF16)
                kls = T(S, ess, "kls", [NST, 256], BF16)
                kcb = [T(S, ess, "kcb%d" % i, [128, 2, 512], BF16) for i in range(2)]
                vcb = [T(S, ess, "vcb%d" % i, [128, 2, 512], BF16) for i in range(2)]
                kTc = [T(S, ess, "kTc%d" % i, [128, 4, 128], BF16) for i in range(2)]
                vaugc = [T(S, ess, "vaugc%d" % i, [128, HA, 65], BF16) for i in range(2)]
                pTs = [T(S, ess, "pTs%d" % i, [128, HA, NST], BF16) for i in range(2)]
                kTs = T(S, ess, "kTs", [128, 4, NST], BF16)
                vaug_s = T(S, ess, "vaug_s", [128, 1, HA, 65], BF16)
                S.op(POOL, lambda: GP.memset(vaug_s[:, :, :, 64:65], 1.0), [], [vaug_s])
                kTa_cur, vaug_cur = kTs, vaug_s
                S.op(DVE, lambda: V.memset(attms[0][:], 0.0), [], [attms[0]])
                S.op(DVE, lambda: V.memset(vbt[:], 0.0), [], [vbt])
                for sq in range(NSS):
                    for c in range(2):
                        for hh in range(2):
                            S.dma(SP, Sst_s[64 * hh:64 * hh + 64, sq, c, :], sg[sq, 2 * c + hh, :, :], [], [Sst_s])
                S.op(ACT, lambda: A.activation(out=Sbf_s[:], in_=Sst_s[:], func=AF.Copy), [Sst_s], [Sbf_s])
                mixer_group(1, NST, lambda t: xs[0:NST, :], 0, 0, True,
                            lambda t: wk_s[0:NST, :], lambda t: wv_s[0:NST, :], lambda t: x1_s[0:NST, :], sample=True)
                for sq in range(NSS):
                    for c in range(2):
                        for hh in range(2):
                            store(POOL, gs_s[sq, 2 * c + hh, :, :], Sst_s[64 * hh:64 * hh + 64, sq, c, :], Sst_s)

        S.barrier()
        with ExitStack() as es:
            wup = T(S, es, "wup", [128, 8, 2 * DFF], BF16)
            wdn = T(S, es, "wdn", [128, NFC, D], BF16)
            class _Blk2:
                def __init__(self, name):
                    self.buf = Buf(name)
            w_up3 = w_up.rearrange("(kc p) n -> p kc n", p=128)
            wupg = [_Blk2("wupg%d" % i) for i in range(6)]
            wupu = [_Blk2("wupu%d" % i) for i in range(6)]
            for i in range(6):
                ncol = min(512, DFF - i * 512)
                S.dma(POOL, wup[:, :, i * 512:i * 512 + ncol], w_up3[:, :, i * 512:i * 512 + ncol], [], [wupg[i]])
                S.dma(POOL, wup[:, :, DFF + i * 512:DFF + i * 512 + ncol], w_up3[:, :, DFF + i * 512:DFF + i * 512 + ncol], [], [wupu[i]])
            for fc in range(NFC):
                S.dma(POOL, wdn[:, fc, :], w_down[fc * 128:(fc + 1) * 128, :], [], [wdn])
            gpre2 = T(S, es, "gpre2", [128, D])
            S.dma(SP, gpre2[:], g_pre_ffn[0:1, :].partition_broadcast(128), [], [gpre2])
            gpost2 = T(S, es, "gpost2", [128, D])
            S.dma(SP, gpost2[:], g_post_ffn[0:1, :].partition_broadcast(128), [], [gpost2])
            cw = T(S, es, "cw", [128, 4, NFC])
            cwt = T(S, es, "cwt", [NFC, 4, 128])
            id32 = T(S, es, "id32", [32, 32])
            S.dma(SP, id32[:], c_id32[:, :], [], [id32])
            S.dma(SP, cwt[:, 0:3, :], conv_w.rearrange("j (c p) -> c j p", p=128), [], [cwt])
            S.dma(SP, cwt[:, 3, :], conv_b[0, :].rearrange("(c p) -> c p", p=128), [], [cwt])
            for j in range(4):
                S.op(PE, lambda j=j: P.matmul(ps[0][:, j * 32:j * 32 + NFC], lhsT=cwt[0:NFC, j, :], rhs=id32[0:NFC, 0:NFC],
                                              start=True, stop=True), [cwt, id32], [ps[0]], inc=(j == 3))
            S.op(ACT, lambda: A.activation(out=cw[:], in_=ps[0][:, 0:128].rearrange("p (j c) -> p j c", j=4)[:, :, 0:NFC],
                                           func=AF.Copy), [ps[0]], [cw])

            xs1 = [T(S, es, "xs1_%d" % i, [128, D]) for i in range(2)]
            xr = [T(S, es, "xr_%d" % i, [128, D]) for i in range(2)]
            xnbs = [T(S, es, "xnb2_%d" % i, [128, D], BF16) for i in range(2)]
            st_ss = [T(S, es, "pss%d" % i, [128, 1]) for i in range(2)]
            st_ln = [T(S, es, "pln%d" % i, [128, 1]) for i in range(2)]
            st_rs = [T(S, es, "prs%d" % i, [128, 1]) for i in range(2)]
            ssq = T(S, es, "ssq2", [128, 8])
            lnt = T(S, es, "lnt2", [128, 8])
            rstd = T(S, es, "rstd2", [128, 8])
            xnT = T(S, es, "xnT2", [128, 8, G], BF16)
            hT = T(S, es, "hT", [128, NFC, G], BF16)

            class _Sub:
                def __init__(self, fc):
                    self.buf = Buf("hT%d" % fc)
            hTs = [_Sub(fc) for fc in range(NFC)]
            junkb = T(S, es, "junkb", [128, 512], BF16)
            gbuf = [T(S, es, "gbuf%d" % i, [128, G + 2]) for i in range(1)] * 2
            cbuf = [T(S, es, "cbuf%d" % i, [128, G]) for i in range(1)] * 2
            gel = [T(S, es, "gel%d" % i, [128, G]) for i in range(1)] * 2
            carry = T(S, es, "carry", [128, NFC, 2])
            ytmp = [T(S, es, "ytmp2_%d" % i, [128, 512]) for i in range(2)]
            gl = [T(S, es, "gl%d" % i, [NST, 512]) for i in range(1)] * 2
            schist = T(S, es, "schist", [128, NFC, 8])
            gbs = T(S, es, "gbs", [128, NSS, TS + 2])
            id8 = T(S, es, "id8", [8, 8])
            S.dma(SP, id8[:], c_id8[:, :], [], [id8])
            xgo2 = [Buf("xgo2_%d" % i) for i in range(2)]

            def prenorm_a(x_src, np_, slot):
                xs_, xb_ = xs1[slot], xnbs[slot]
                S.dma(SP, xs_[0:np_, :], x_src, [x1buf], [xs_])
                S.op(ACT, lambda: A.activation(out=xb_[0:np_, :], in_=xs_[0:np_, :], func=AF.Square,
                                               accum_out=st_ss[slot][0:np_, 0:1]), [xs_], [xb_, st_ss[slot]])
                rms_rstd(st_ss[slot][0:np_, 0:1], st_rs[slot][0:np_, 0:1], np_, st_ln[slot][0:np_, 0:1],
                         st_ss[slot], st_rs[slot], st_ln[slot], D)
                S.op(DVE, lambda: V.scalar_tensor_tensor(out=xb_[0:np_, :], in0=xs_[0:np_, :], scalar=st_rs[slot][0:np_, 0:1],
                                                         in1=gpre2[0:np_, :], op0=ALU.mult, op1=ALU.mult),
                     [xs_, st_rs[slot], gpre2], [xb_])

            def prenorm_b(t, np_, slot):
                xb_ = xnbs[slot]
                transpose_tile(xb_, lambda c: xb_[0:np_, c * 128:(c + 1) * 128], np_, xnT,
                               xnT[:, :, t * np_:(t + 1) * np_], 8, slot)

            def ffn_group(nt, np_, x_src_fn, y_dst_fn, first_in_seq, last_in_seq, fc_dst, sample=False, nxt=None):
                ntok = nt * np_
                if sample:
                    for n in range(6):
                        ncol = min(512, DFF - n * 512)
                        glt = gl[0]
                        S.dma(SP, glt[0:8, 0:ncol], sc[:, n * 512:n * 512 + ncol], [], [glt])
                        nf = ncol // 128
                        for j in range(nf):
                            S.op(PE, lambda j=j: P.matmul(ps[0][:, j * 8:(j + 1) * 8], lhsT=glt[0:8, j * 128:(j + 1) * 128],
                                                          rhs=id8[0:8, 0:8], start=True, stop=True), [glt, id8], [ps[0]],
                                 inc=(j == nf - 1))
                        S.op(ACT, lambda n=n, nf=nf: A.activation(out=schist[:, 4 * n:4 * n + nf, :],
                                                                  in_=ps[0][:, 0:nf * 8].rearrange("p (f r) -> p f r", r=8),
                                                                  func=AF.Copy), [ps[0]], [schist])
                if first_in_seq:
                    S.op(POOL, lambda: GP.memset(carry[:], 0.0), [], [carry])
                for fc in range(NFC):
                    pg, pu = ps[(2 * fc) % 6], ps[(2 * fc + 1) % 6]
                    gb, cb_, ge = gbuf[fc % 2], cbuf[fc % 2], gel[fc % 2]
                    for kc in range(8):
                        S.op(PE, lambda kc=kc: P.matmul(pg[:, 0:ntok], lhsT=wup[:, kc, fc * 128:(fc + 1) * 128],
                                                        rhs=xnT[:, kc, 0:ntok], start=(kc == 0), stop=(kc == 7)),
                             [wupg[fc // 4], xnT], [pg], inc=(kc == 7))
                    for kc in range(8):
                        S.op(PE, lambda kc=kc: P.matmul(pu[:, 0:ntok], lhsT=wup[:, kc, DFF + fc * 128:DFF + (fc + 1) * 128],
                                                        rhs=xnT[:, kc, 0:ntok], start=(kc == 0), stop=(kc == 7)),
                             [wupu[fc // 4], xnT], [pu], inc=(kc == 7))
                    if not sample:
                        S.op(POOL, lambda: GP.tensor_copy(out=gb[:, 0:2], in_=carry[:, fc, :]), [carry], [gb])
                        S.op(ACT, lambda: A.activation(out=gb[:, 2:2 + ntok], in_=pg[:, 0:ntok], func=AF.Copy), [pg], [gb])
                        S.op(POOL, lambda: GP.tensor_copy(out=carry[:, fc, :], in_=gb[:, ntok:ntok + 2]), [gb], [carry])
                        g2, g1, g0, co = gb[:, 2:2 + ntok], gb[:, 1:1 + ntok], gb[:, 0:ntok], cb_[:, 0:ntok]
                        gsrc = gb
                    else:
                        S.op(POOL, lambda: GP.tensor_copy(out=gbs[:, :, 0:2], in_=schist[:, fc, :].rearrange("p (s j) -> p s j", j=2)),
                             [schist], [gbs])
                        S.op(ACT, lambda: A.activation(out=gbs[:, :, 2:2 + TS], in_=pg[:, 0:NST].rearrange("p (s t) -> p s t", t=TS),
                                                       func=AF.Copy), [pg], [gbs])
                        g2, g1, g0 = gbs[:, :, 2:2 + TS], gbs[:, :, 1:1 + TS], gbs[:, :, 0:TS]
                        co = cb_[:, 0:NST].rearrange("p (s t) -> p s t", t=TS)
                        gsrc = gbs
                    S.op(DVE, lambda: V.tensor_scalar(out=co, in0=g2, scalar1=cw[:, 2, fc:fc + 1],
                                                      scalar2=cw[:, 3, fc:fc + 1], op0=ALU.mult, op1=ALU.add),
                         [gsrc, cw], [cb_])
                    S.op(DVE, lambda: V.scalar_tensor_tensor(out=co, in0=g1, scalar=cw[:, 1, fc:fc + 1],
                                                             in1=co, op0=ALU.mult, op1=ALU.add),
                         [gsrc, cw, cb_], [cb_])
                    S.op(DVE, lambda: V.scalar_tensor_tensor(out=co, in0=g0, scalar=cw[:, 0, fc:fc + 1],
                                                             in1=co, op0=ALU.mult, op1=ALU.add),
                         [gsrc, cw, cb_], [cb_])
                    S.op(ACT, lambda: A.activation(out=ge[:, 0:ntok], in_=cb_[:, 0:ntok], func=AF.Gelu), [cb_], [ge])
                    S.op(DVE, lambda: V.tensor_tensor(out=hT[:, fc, 0:ntok], in0=ge[:, 0:ntok], in1=pu[:, 0:ntok], op=ALU.mult),
                         [ge, pu], [hTs[fc]])
                if last_in_seq:
                    nrow = NST if sample else 2
                    for n in range(6):
                        ncol = min(512, DFF - n * 512)
                        pq = ps[n % 2]
                        for kc in range(8):
                            S.op(PE, lambda kc=kc: P.matmul(pq[0:nrow, 0:ncol], lhsT=xnT[:, kc, ntok - nrow:ntok],
                                                            rhs=wup[:, kc, n * 512:n * 512 + ncol], start=(kc == 0), stop=(kc == 7)),
                                 [wupg[n], xnT], [pq], inc=(kc == 7))
                        glt = gl[n % 2]
                        S.op(ACT, lambda: A.activation(out=glt[0:nrow, 0:ncol], in_=pq[0:nrow, 0:ncol], func=AF.Copy),
                             [pq], [glt])
                        if not sample:
                            store(POOL, fc_dst[:, n * 512:n * 512 + ncol], glt[0:2, 0:ncol], glt)
                        else:
                            for sq in range(NSS):
                                store(POOL, fc_s[sq, :, n * 512:n * 512 + ncol], glt[TS * sq + TS - 2:TS * sq + TS, 0:ncol], glt)

                def dn_mm(t):
                    for hf in range(2):
                        pdn = ps[(2 * t + hf) % 6]
                        for fc in range(NFC):
                            S.op(PE, lambda fc=fc, hf=hf, pdn=pdn: P.matmul(pdn[0:np_, :], lhsT=hT[:, fc, t * np_:(t + 1) * np_],
                                                                            rhs=wdn[:, fc, hf * 512:(hf + 1) * 512],
                                                                            start=(fc == 0), stop=(fc == NFC - 1)),
                                 [wdn, hTs[fc]], [pdn], inc=(fc == NFC - 1))

                def dn_epi(t):
                    xr_ = xr[t % 2]
                    for hf in range(2):
                        pdn = ps[(2 * t + hf) % 6]
                        S.op(ACT, lambda hf=hf, pdn=pdn: A.activation(out=junkb[0:np_, :], in_=pdn[0:np_, :],
                                                                      func=AF.Square, accum_out=ssq[0:np_, 4 + hf:5 + hf]),
                             [pdn], [junkb, ssq])
                    S.op(DVE, lambda: V.tensor_tensor(out=ssq[0:np_, 6:7], in0=ssq[0:np_, 4:5], in1=ssq[0:np_, 5:6], op=ALU.add),
                         [ssq], [ssq])
                    rms_rstd(ssq[0:np_, 6:7], rstd[0:np_, 6:7], np_, lnt[0:np_, 6:7], ssq, rstd, lnt, D)
                    for hf in range(2):
                        yt_ = ytmp[hf]
                        pdn = ps[(2 * t + hf) % 6]
                        S.op(DVE, lambda hf=hf, yt_=yt_, pdn=pdn: V.scalar_tensor_tensor(
                            out=yt_[0:np_, :], in0=pdn[0:np_, :], scalar=rstd[0:np_, 6:7],
                            in1=gpost2[0:np_, hf * 512:(hf + 1) * 512], op0=ALU.mult, op1=ALU.mult),
                             [pdn, rstd, gpost2], [yt_])
                        S.op(DVE, lambda hf=hf, yt_=yt_: V.tensor_tensor(out=xr_[0:np_, hf * 512:(hf + 1) * 512], in0=yt_[0:np_, :],
                                                                         in1=xr_[0:np_, hf * 512:(hf + 1) * 512], op=ALU.add),
                             [yt_, xr_], [xr_])
                    S.dma(SP, y_dst_fn(t), xr_[0:np_, :], [xr_], [], sembuf=xgo2[t % 2])
                    if xgo2[t % 2] not in out_bufs:
                        out_bufs.append(xgo2[t % 2])

                def dn_pre(t):
                    if nxt is not None:
                        prenorm_a(nxt(t), 128, t % 2)
                    S.dma(SP, xr[t % 2][0:np_, :], x_src_fn(t), [x1buf], [xr[t % 2]])

                if sample and nxt is not None:
                    for t4 in range(4):
                        prenorm_a(nxt(t4), 128, t4 % 2)
                        prenorm_b(t4, 128, t4 % 2)
                    nxt = None
                dn_pre(0)
                dn_mm(0)
                for t in range(nt):
                    if t + 1 < nt:
                        dn_pre(t + 1)
                        dn_mm(t + 1)
                    if nxt is not None:
                        prenorm_b(t, 128, t % 2)
                    dn_epi(t)

            groups = []
            for s in range(DBG["nseq"] if DBG["phaseB"] else 0):
                for g in range(DBG["ngrp"]):
                    tok0 = g * G
                    groups.append((lambda t, s=s, tok0=tok0: x1_p[s, tok0 + t * 128:tok0 + (t + 1) * 128, :],
                                   lambda t, s=s, tok0=tok0: y_p[s, tok0 + t * 128:tok0 + (t + 1) * 128, :],
                                   g == 0, g == DBG["ngrp"] - 1, fc_p[s, :, :]))
            if DBG["sample"] and DBG["phaseB"]:
                prenorm_a(x1_s[0:NST, :], NST, 0)
                prenorm_b(0, NST, 0)
                ffn_group(1, NST, lambda t: x1_s[0:NST, :], lambda t: y_s[0:NST, :], True, True, None, sample=True,
                          nxt=(groups[0][0] if groups else None))
            elif groups:
                for t in range(4):
                    prenorm_a(groups[0][0](t), 128, t % 2)
                    prenorm_b(t, 128, t % 2)
            for i, (xf, yf, fi, la, fcd) in enumerate(groups):
                ffn_group(4, 128, xf, yf, fi, la, fcd, nxt=(groups[i + 1][0] if i + 1 < len(groups) else None))

        for b in out_bufs:
            SP.h.wait_ge(b.dsem, b.dcnt)
    return nc


def _mult(delta):
    delta = np.asarray(delta)
    nn = delta >= 0
    m = (nn & (delta <= 128)).astype(np.float32)
    m += (nn & (delta <= 512) & (delta % 4 == 0))
    m += (nn & (delta <= 2048) & (delta % 16 == 0))
    return m


def _constants():
    bf = ml_dtypes.bfloat16
    c = {}
    c["c_ident"] = np.eye(128, dtype=np.float32).astype(bf)
    jj = np.arange(128)[:, None]
    ii = np.arange(128)[None, :]
    am = np.zeros((128, 19, 128), np.float32)
    for e in range(-3, 16):
        am[:, e + 3, :] = _mult(128 * e + ii - jj)
    c["c_amask"] = am.astype(bf)
    j = np.arange(128)[:, None]
    i = np.arange(128)[None, :]
    c["c_uinc"] = np.where(j <= i, -1.0 / 16.0, 0.0).astype(np.float32)
    c["c_uaft"] = np.where(j > i, -1.0 / 16.0, 0.0).astype(np.float32)
    c["c_caus"] = (j <= i).astype(np.float32).astype(bf)
    c["c_ones"] = np.ones((1, 128), np.float32)
    sm = np.zeros((128, NSS * 16, NST), np.float32)
    tt = np.arange(TS)[None, :]
    for s in range(NSS):
        for jb in range(16):
            sm[:, s * 16 + jb, s * TS:(s + 1) * TS] = _mult(SEQ + tt - (128 * jb + jj))
    c["c_smask"] = sm.astype(bf)
    tok = np.arange(NST)
    same = (tok[:, None] // TS) == (tok[None, :] // TS)
    dl = tok[None, :] - tok[:, None]
    c["c_nmask"] = (np.where(same, _mult(dl), 0.0)).astype(np.float32).astype(bf)
    c["c_suinc"] = np.where(same & (tok[:, None] <= tok[None, :]), -1.0 / 16.0, 0.0).astype(np.float32)
    c["c_suaft"] = np.where(same & (tok[:, None] > tok[None, :]), -1.0 / 16.0, 0.0).astype(np.float32)
    c["c_scaus"] = (same & (tok[:, None] <= tok[None, :])).astype(np.float32).astype(bf)
    ssel = np.zeros((128, NSS, NST), np.float32)
    srow = np.zeros((NST, NSS), np.float32)
    for s in range(NSS):
        ssel[:, s, s * TS:(s + 1) * TS] = 1.0
        srow[s * TS:(s + 1) * TS, s] = 1.0
    c["c_ssel"] = ssel
    c["c_srow"] = srow
    c["c_id8"] = np.eye(8, dtype=np.float32)
    c["c_id32"] = np.eye(32, dtype=np.float32)
    return c


_NC_CACHE = {}


def kernel(x_prompt, x_sample, cache_k_win, cache_v_win, state_gla, state_ffn_conv,
           w_in, w_a2, b_a, g_gla_norm, w_o, g_pre_mix, g_post_mix, g_pre_ffn, g_post_ffn,
           w_up, conv_w, conv_b, w_down):
    f = lambda a: np.ascontiguousarray(np.asarray(a, dtype=np.float32))
    if "nc" not in _NC_CACHE:
        _NC_CACHE["nc"] = build_program()
    nc = _NC_CACHE["nc"]
    consts = _constants()
    shared = {
        "w_in": f(w_in[0]), "w_a2": f(w_a2[0]), "b_a": f(b_a[0]).reshape(1, 256),
        "g_gla": f(g_gla_norm[0]).reshape(1, 512), "w_o": f(w_o[0]),
        "g_pre_mix": f(g_pre_mix[0]).reshape(1, D), "g_post_mix": f(g_post_mix[0]).reshape(1, D),
        "g_pre_ffn": f(g_pre_ffn[0]).reshape(1, D), "g_post_ffn": f(g_post_ffn[0]).reshape(1, D),
        "w_up": f(w_up[0]), "conv_w": f(conv_w[0]), "conv_b": f(conv_b[0]).reshape(1, DFF), "w_down": f(w_down[0]),
    }
    shared.update(consts)
    x_prompt = np.asarray(x_prompt); x_sample = np.asarray(x_sample)
    cache_k_win = np.asarray(cache_k_win); cache_v_win = np.asarray(cache_v_win)
    state_gla = np.asarray(state_gla); state_ffn_conv = np.asarray(state_ffn_conv)
    in_maps = []
    for c in range(NCORES):
        m = dict(shared)
        m["xp"] = f(x_prompt[NSEQ * c:NSEQ * (c + 1)])
        m["xs"] = f(x_sample[NSS * c:NSS * (c + 1)]).reshape(NST, D)
        m["ck"] = f(cache_k_win[0, NSS * c:NSS * (c + 1)]).reshape(NSS, SEQ, 512)
        m["cv"] = f(cache_v_win[0, NSS * c:NSS * (c + 1)]).reshape(NSS, SEQ, 512)
        m["sg"] = f(state_gla[0, NSS * c:NSS * (c + 1)])
        m["sc"] = f(state_ffn_conv[0, NSS * c:NSS * (c + 1)]).reshape(NSS * 2, DFF)
        in_maps.append(m)
    res = run_bass_kernel_spmd(nc, in_maps, core_ids=list(range(NCORES)))
    R = res.results
    cat = lambda k: np.concatenate([np.asarray(r[k], dtype=np.float32) for r in R], axis=0)
    y_prompt = cat("y_p")
    y_sample = cat("y_s").reshape(32, TS, D)
    wk_p = cat("wk_p").reshape(1, 16, SEQ, HA, 64)
    wv_p = cat("wv_p").reshape(1, 16, SEQ, HA, 64)
    gs_p = cat("gs_p").reshape(1, 16, 4, 64, 128)
    fc_p = cat("fc_p").reshape(1, 16, 2, DFF)
    wk_s = cat("wk_s").reshape(1, 32, TS, HA, 64)
    wv_s = cat("wv_s").reshape(1, 32, TS, HA, 64)
    gs_s = cat("gs_s").reshape(1, 32, 4, 64, 128)
    fc_s = cat("fc_s").reshape(1, 32, 2, DFF)
    return (y_prompt, y_sample, wk_p, wv_p, gs_p, fc_p, wk_s, wv_s, gs_s, fc_s)
```

```python
import numpy as np
import ml_dtypes
from contextlib import ExitStack
import concourse.bass as bass
import concourse.mybir as mybir
from concourse.bass_utils import run_bass_kernel_spmd

F32 = mybir.dt.float32
BF16 = mybir.dt.bfloat16
AF = mybir.ActivationFunctionType
ALU = mybir.AluOpType
AX = mybir.AxisListType

NCORES = 8
D = 1024
SEQ = 2048
NSEQ = 2
NSS = 4
TS = 8
NST = NSS * TS
HA = 8
DFF = 2816
NFC = DFF // 128
DIN = 3088
EPS = 1e-6
G = 512
DBG = {"nseq": NSEQ, "ngrp": 4, "phaseB": True, "stage": 9, "gla": True, "gs": 9, "gs2": 9, "sample": True, "npump": 6, "perblk": 0, "look": 1}
QA, KA, VA, QB, KB, VB, RB, AL = 0, 512, 1024, 1536, 1792, 2048, 2560, 3072


class Buf:
    __slots__ = ("name", "w", "r", "dsem", "dcnt", "excl")

    def __init__(self, name):
        self.name = name
        self.excl = False
        self.w = None
        self.r = []
        self.dsem = None
        self.dcnt = 0


class Eng:
    def __init__(self, name, h, sem):
        self.name, self.h, self.sem = name, h, sem
        self.count = 0
        self.waited = {}


class Sched:
    def __init__(self, nc, es):
        self.nc, self.es = nc, es
        mk = lambda n, h: Eng(n, h, es.enter_context(nc.semaphore("s_" + n)))
        self.pe = mk("pe", nc.tensor)
        self.act = mk("act", nc.scalar)
        self.dve = mk("dve", nc.vector)
        self.pool = mk("pool", nc.gpsimd)
        self.sp = mk("sp", nc.sync)
        self.semcur = {}
        self.out_tickets = {}
        self.nsem = 5

    def _wait(self, eng, tk, same_ok):
        if tk is None:
            return
        sem, val, en = tk
        if en == eng.name and eng.name == "pe":
            return
        if en == "dma":
            val = max(val, self.semcur[id(sem)][1])
        if eng.waited.get(id(sem), 0) >= val:
            return
        eng.waited[id(sem)] = val
        eng.h.wait_ge(sem, val)

    def _deps(self, eng, reads, writes):
        for b in reads:
            b = b.buf if hasattr(b, "buf") else b
            self._wait(eng, b.w, True)
            if b.excl:
                for tk in b.r:
                    if tk[2] != eng.name:
                        self._wait(eng, tk, True)
        for b in writes:
            b = b.buf if hasattr(b, "buf") else b
            self._wait(eng, b.w, False)
            for tk in b.r:
                self._wait(eng, tk, False)

    def _commit(self, tk, reads, writes):
        for b in reads:
            b = b.buf if hasattr(b, "buf") else b
            b.r.append(tk)
        for b in writes:
            b = b.buf if hasattr(b, "buf") else b
            b.w = tk
            b.r = []

    def op(self, eng, fn, reads=(), writes=(), inc=True):
        self._deps(eng, reads, writes)
        ins = fn()
        if inc:
            eng.count += 1
            ins.then_inc(eng.sem, 1)
            tk = (eng.sem, eng.count, eng.name)
        else:
            tk = (eng.sem, eng.count + 1, eng.name)
        self._commit(tk, reads, writes)
        return tk

    def barrier(self):
        engs = [self.pe, self.act, self.dve, self.pool, self.sp]
        for e in engs:
            for o in engs:
                if o is e or o.count == 0:
                    continue
                if e.waited.get(id(o.sem), 0) < o.count:
                    e.waited[id(o.sem)] = o.count
                    e.h.wait_ge(o.sem, o.count)
            for sem, val in self.semcur.values():
                if e.waited.get(id(sem), 0) < val:
                    e.waited[id(sem)] = val
                    e.h.wait_ge(sem, val)

    def dma(self, eng, out, in_, reads, writes, sembuf=None, **kw):
        sb = sembuf if sembuf is not None else writes[0]
        sb = sb.buf if hasattr(sb, "buf") else sb
        if sb.dsem is None:
            sb.dsem = self.es.enter_context(self.nc.semaphore("d_" + sb.name))
            self.nsem += 1
        saved = []
        for b in writes:
            b = b.buf if hasattr(b, "buf") else b
            if b.w is not None and b.w[2] == "dma" and b.w[0] is sb.dsem:
                saved.append((b, b.w))
                b.w = None
        self._deps(eng, reads, writes)
        for b, w in saved:
            b.w = w
        sb.dcnt += 16
        eng.h.dma_start(out=out, in_=in_, **kw).then_inc(sb.dsem, 16)
        self.semcur[id(sb.dsem)] = (sb.dsem, sb.dcnt)
        tk = (sb.dsem, sb.dcnt, "dma")
        self._commit(tk, reads, writes)
        return tk


class T:
    def __init__(self, S, es, name, shape, dtype=F32, psum=False):
        if psum:
            self.t = es.enter_context(S.nc.psum_tensor(name, shape, dtype))
        else:
            self.t = es.enter_context(S.nc.sbuf_tensor(name, shape, dtype))
        self.buf = Buf(name)
        self.buf.excl = psum
        self.shape = shape

    def __getitem__(self, idx):
        return self.t[idx]


def build_program():
    nc = bass.Bass("TRN2", target_bir_lowering=False)
    dt_in = lambda n, s, d=F32: nc.dram_tensor(n, s, d, kind="ExternalInput").ap()
    dt_out = lambda n, s: nc.dram_tensor(n, s, F32, kind="ExternalOutput").ap()
    xp = dt_in("xp", [NSEQ, SEQ, D])
    xs = dt_in("xs", [NST, D])
    ck = dt_in("ck", [NSS, SEQ, 512])
    cv = dt_in("cv", [NSS, SEQ, 512])
    sg = dt_in("sg", [NSS, 4, 64, 128])
    sc = dt_in("sc", [NSS * 2, DFF])
    w_in = dt_in("w_in", [D, DIN])
    w_a2 = dt_in("w_a2", [16, 256])
    b_a = dt_in("b_a", [1, 256])
    g_gla = dt_in("g_gla", [1, 512])
    w_o = dt_in("w_o", [D, D])
    g_pre_mix = dt_in("g_pre_mix", [1, D])
    g_post_mix = dt_in("g_post_mix", [1, D])
    g_pre_ffn = dt_in("g_pre_ffn", [1, D])
    g_post_ffn = dt_in("g_post_ffn", [1, D])
    w_up = dt_in("w_up", [D, 2 * DFF])
    conv_w = dt_in("conv_w", [3, DFF])
    conv_b = dt_in("conv_b", [1, DFF])
    w_down = dt_in("w_down", [DFF, D])
    c_ident = dt_in("c_ident", [128, 128], BF16)
    c_amask = dt_in("c_amask", [128, 19, 128], BF16)
    c_uinc = dt_in("c_uinc", [128, 128])
    c_uaft = dt_in("c_uaft", [128, 128])
    c_caus = dt_in("c_caus", [128, 128], BF16)
    c_ones = dt_in("c_ones", [1, 128])
    c_smask = dt_in("c_smask", [128, NSS * 16, NST], BF16)
    c_nmask = dt_in("c_nmask", [NST, NST], BF16)
    c_suinc = dt_in("c_suinc", [NST, NST])
    c_suaft = dt_in("c_suaft", [NST, NST])
    c_scaus = dt_in("c_scaus", [NST, NST], BF16)
    c_ssel = dt_in("c_ssel", [128, NSS, NST])
    c_srow = dt_in("c_srow", [NST, NSS])
    c_id8 = dt_in("c_id8", [8, 8])
    c_id32 = dt_in("c_id32", [32, 32])
    y_p = dt_out("y_p", [NSEQ, SEQ, D])
    y_s = dt_out("y_s", [NST, D])
    wk_p = dt_out("wk_p", [NSEQ, SEQ, 512])
    wv_p = dt_out("wv_p", [NSEQ, SEQ, 512])
    gs_p = dt_out("gs_p", [NSEQ, 4, 64, 128])
    fc_p = dt_out("fc_p", [NSEQ, 2, DFF])
    wk_s = dt_out("wk_s", [NST, 512])
    wv_s = dt_out("wv_s", [NST, 512])
    gs_s = dt_out("gs_s", [NSS, 4, 64, 128])
    fc_s = dt_out("fc_s", [NSS, 2, DFF])
    x1_p = nc.dram_tensor("x1_p", [NSEQ, SEQ, D], F32, kind="Internal").ap()
    x1_s = nc.dram_tensor("x1_s", [NST, D], F32, kind="Internal").ap()

    with ExitStack() as es0:
        S = Sched(nc, es0)
        PE, ACT, DVE, POOL, SP = S.pe, S.act, S.dve, S.pool, S.sp
        V, A, P, GP = nc.vector, nc.scalar, nc.tensor, nc.gpsimd
        out_bufs = []

        def store(eng, dst, src_ap, src_t):
            ob = getattr(src_t, "obuf", None)
            if ob is None:
                ob = Buf(src_t.buf.name + "_o")
                src_t.obuf = ob
            S.dma(eng, dst, src_ap, reads=[src_t], writes=[], sembuf=ob)
            if ob not in out_bufs:
                out_bufs.append(ob)

        ident = T(S, es0, "ident", [128, 128], BF16)
        S.dma(SP, ident[:], c_ident[:, :], [], [ident])
        ones_r = T(S, es0, "ones_r", [1, 128])
        S.dma(SP, ones_r[:], c_ones[:, :], [], [ones_r])

        ps = [T(S, es0, "ps%d" % i, [128, 512], F32, psum=True) for i in range(8)]

        class _View:
            def __init__(self, t):
                self.buf = t.buf
                self.v = t[:].bitcast(BF16).rearrange("p (c i) -> p c i", c=8)

            def __getitem__(self, idx):
                return self.v[idx]
        ptp = [_View(ps[6]), _View(ps[7]), _View(ps[0]), _View(ps[1])]

        def rms_rstd(ss_ap, out_ap, n, tmp_ap, ss_t, out_t, tmp_t, nfeat):
            S.op(ACT, lambda: A.activation(out=tmp_ap, in_=ss_ap, func=AF.Ln, scale=1.0 / nfeat, bias=epsb[0:n, 0:1]),
                 [ss_t, epsb], [tmp_t])
            S.op(ACT, lambda: A.activation(out=out_ap, in_=tmp_ap, func=AF.Exp, scale=-0.5), [tmp_t], [out_t])

        epsb = T(S, es0, "epsb", [128, 1])
        S.op(DVE, lambda: V.memset(epsb[:], EPS), [], [epsb])
        oneb = T(S, es0, "oneb", [128, 1])
        S.op(DVE, lambda: V.memset(oneb[:], 1.0), [], [oneb])

        def transpose_tile(src_t, src_ap_fn, np_, dst_t, dst_ap, nchunk, slot):
            pt = ptp[slot]
            for c in range(nchunk):
                S.op(PE, lambda c=c: P.transpose(pt[:, c, 0:np_], src_ap_fn(c), ident[0:np_, 0:np_]),
                     [src_t, ident], [pt], inc=(c == nchunk - 1))
            S.op(ACT, lambda: A.activation(out=dst_ap, in_=pt[:, 0:nchunk, 0:np_], func=AF.Copy), [pt], [dst_t])

        with ExitStack() as es:
            win = T(S, es, "win", [128, 8, DIN], BF16)
            wo = T(S, es, "wo", [128, 8, D], BF16)
            class _Blk:
                def __init__(self, name):
                    self.buf = Buf(name)
            w_in3 = w_in.rearrange("(kc p) n -> p kc n", p=128)
            win_blocks = [(AL, DIN), (QB, VB), (VB, AL), (QA, KA), (KA, VA), (VA, QB)]
            win_bufs = []
            for bi, (c0, c1) in enumerate(win_blocks):
                wb = _Blk("winb%d" % bi)
                win_bufs.append(wb)
                for k0 in range(0, 8, 4):
                    S.dma(POOL, win[:, k0:k0 + 4, c0:c1], w_in3[:, k0:k0 + 4, c0:c1], [], [wb])

            def winb(col0):
                for (c0, c1), wb in zip(win_blocks, win_bufs):
                    if c0 <= col0 < c1:
                        return wb
            for kc in range(8):
                S.dma(POOL, wo[:, kc, :], w_o[kc * 128:(kc + 1) * 128, :], [], [wo])
            wa2 = T(S, es, "wa2", [16, 256])
            S.dma(SP, wa2[:], w_a2[:, :], [], [wa2])
            bar = T(S, es, "bar", [1, 256])
            S.dma(SP, bar[:], b_a[:, :], [], [bar])
            gpre = T(S, es, "gpre", [128, D])
            S.dma(SP, gpre[:], g_pre_mix[0:1, :].partition_broadcast(128), [], [gpre])
            gpost = T(S, es, "gpost", [128, D])
            S.dma(SP, gpost[:], g_post_mix[0:1, :].partition_broadcast(128), [], [gpost])
            ggla = T(S, es, "ggla", [128, 512])
            S.dma(SP, ggla[:], g_gla[0:1, :].partition_broadcast(128), [], [ggla])

            xs1 = [T(S, es, "xs1a_%d" % i, [128, D]) for i in range(2)]
            xr = [T(S, es, "xra_%d" % i, [128, D]) for i in range(2)]
            f_ss = [T(S, es, "fss%d" % i, [128, 1]) for i in range(4)]
            f_ln = [T(S, es, "fln%d" % i, [128, 1]) for i in range(4)]
            f_rs = [T(S, es, "frs%d" % i, [128, 1]) for i in range(4)]
            ssq = T(S, es, "ssq", [128, 8])
            lnt = T(S, es, "lnt", [128, 8])
            rstd = T(S, es, "rstd", [128, 8])
            xnb4 = [T(S, es, "xnb%d" % i, [128, D], BF16) for i in range(4)]
            xnb = xnb4[0]
            xnT = T(S, es, "xnT", [128, 8, G], BF16)
            qTa = T(S, es, "qTa", [128, 4, G], BF16)
            qTb = T(S, es, "qTb", [128, 2, G], BF16)
            kTb = T(S, es, "kTb", [128, 2, G], BF16)
            alT = T(S, es, "alT", [16, G])
            ktok = [T(S, es, "ktok%d" % i, [128, 512]) for i in range(2)]
            vtok = [T(S, es, "vtok%d" % i, [128, 512]) for i in range(1)] * 2
            kbt = T(S, es, "kbt", [128, 4, 256], BF16)
            vbt = T(S, es, "vbt", [128, 4, 512], BF16)
            rbt = T(S, es, "rbt", [128, 4, 512], BF16)
            mixed = T(S, es, "mixed", [128, 4, D], BF16)
            mixT = xnT
            pT = [T(S, es, "pT%d" % i, [128, 512], BF16) for i in range(4)]
            rden = [T(S, es, "rden%d" % i, [128, 4]) for i in range(2)]
            gtmp = T(S, es, "gtmp", [128, 512])
            lzs = [T(S, es, "lz%d" % i, [128, 256]) for i in range(2)]
            ebTs = [T(S, es, "ebT%d" % i, [128, 2, 128]) for i in range(2)]
            enbTs = [T(S, es, "enbT%d" % i, [128, 2, 128]) for i in range(2)]
            qeTds = [T(S, es, "qeTd%d" % i, [128, 2, 2, 128], BF16) for i in range(2)]
            keTs = [T(S, es, "keT%d" % i, [128, 2, 128], BF16) for i in range(2)]
            klbs = [T(S, es, "kl%d" % i, [128, 256], BF16) for i in range(2)]
            attms = [T(S, es, "attm%d" % i, [128, 4, 128], BF16) for i in range(2)]
            osss = [T(S, es, "oss%d" % i, [128, 4]) for i in range(2)]
            olts = [T(S, es, "olt%d" % i, [128, 4]) for i in range(2)]
            orstds = [T(S, es, "orstd%d" % i, [128, 4]) for i in range(2)]
            osqs = [gtmp, T(S, es, "osq1", [128, 512])]
            for i in range(2):
                S.op(POOL, lambda i=i: GP.memset(qeTds[i][:], 0.0), [], [qeTds[i]])
            ytmp = T(S, es, "ytmp", [128, D])

            class _JunkA:
                buf = ytmp.buf

                def __getitem__(self, idx):
                    return ytmp[:].bitcast(BF16)[:, 0:D][idx]
            junk = _JunkA()

            def front_a(x_src, np_, t):
                xs_, xb_ = xs1[t % 2], xnb4[t]
                S.dma(SP, xs_[0:np_, :], x_src, [], [xs_])
                S.op(ACT, lambda: A.activation(out=xb_[0:np_, :], in_=xs_[0:np_, :], func=AF.Square,
                                               accum_out=f_ss[t][0:np_, 0:1]), [xs_], [xb_, f_ss[t]])
                rms_rstd(f_ss[t][0:np_, 0:1], f_rs[t][0:np_, 0:1], np_, f_ln[t][0:np_, 0:1], f_ss[t], f_rs[t], f_ln[t], D)
                S.op(DVE, lambda: V.scalar_tensor_tensor(out=xb_[0:np_, :], in0=xs_[0:np_, :], scalar=f_rs[t][0:np_, 0:1],
                                                         in1=gpre[0:np_, :], op0=ALU.mult, op1=ALU.mult),
                     [xs_, f_rs[t], gpre], [xb_])

            def front_b(np_, t):
                xb_ = xnb4[t]
                transpose_tile(xb_, lambda c: xb_[0:np_, c * 128:(c + 1) * 128], np_, xnT,
                               xnT[:, :, t * np_:(t + 1) * np_], 8, t % 2)

            def mixer_group(nt, np_, x_src_fn, s_idx, g_idx, first_in_seq, k_dst_fn, v_dst_fn, x1_dst_fn,
                            sample=False, front_done=False, nxt=None):
                ntok = nt * np_
                if not front_done:
                    for t in range(nt):
                        front_a(x_src_fn(t), np_, t)
                for t in range(nt):
                    front_b(np_, t)

                if DBG["stage"] < 2:
                    return
                def fproj(col0, ncols, dst_t, dst_ap, bank):
                    for kc in range(8):
                        S.op(PE, lambda kc=kc: P.matmul(ps[bank][0:ncols, 0:ntok], lhsT=win[:, kc, col0:col0 + ncols],
                                                        rhs=xnT[:, kc, 0:ntok], start=(kc == 0), stop=(kc == 7)),
                             [winb(col0), xnT], [ps[bank]], inc=(kc == 7))
                    S.op(ACT, lambda: A.activation(out=dst_ap, in_=ps[bank][0:ncols, 0:ntok], func=AF.Copy),
                         [ps[bank]], [dst_t])

                if DBG["stage"] < 3:
                    return
                def tproj(t, col0, ncols, bank, evac):
                    for kc in range(8):
                        S.op(PE, lambda kc=kc: P.matmul(ps[bank][0:np_, 0:ncols], lhsT=xnT[:, kc, t * np_:(t + 1) * np_],
                                                        rhs=win[:, kc, col0:col0 + ncols], start=(kc == 0), stop=(kc == 7)),
                             [winb(col0), xnT], [ps[bank]], inc=(kc == 7))
                    evac(ps[bank])

                evs = []
                for t in range(nt):
                    jb = (g_idx * 4 + t) if not sample else 0
                    kt, vt = ktok[t % 2], vtok[t % 2]

                    def ev_k(pb, kt=kt, t=t):
                        S.op(ACT, lambda: A.activation(out=kt[0:np_, :], in_=pb[0:np_, 0:512], func=AF.Copy), [pb], [kt])
                        store(POOL, k_dst_fn(t), kt[0:np_, :], kt)

                    def ev_v(pb, vt=vt, t=t, jb=jb):
                        S.op(ACT, lambda: A.activation(out=vt[0:np_, :], in_=pb[0:np_, 0:512], func=AF.Copy), [pb], [vt])
                        S.op(DVE, lambda: V.tensor_copy(out=vaug_cur[0:np_, jb, :, 0:64],
                                                        in_=pb[0:np_, 0:512].rearrange("p (h e) -> p h e", h=HA)),
                             [pb], [vaug_cur])
                        store(POOL, v_dst_fn(t), vt[0:np_, :], vt)

                    def ev_kb(pb, t=t):
                        S.op(ACT, lambda: A.activation(out=kbt[0:np_, t, :], in_=pb[0:np_, 0:256], func=AF.Copy), [pb], [kbt])

                    def ev_vb(pb, t=t):
                        S.op(DVE, lambda: V.tensor_copy(out=vbt[0:np_, t, :], in_=pb[0:np_, 0:512]), [pb], [vbt])

                    def ev_rb(pb, t=t):
                        S.op(ACT, lambda: A.activation(out=gtmp[0:np_, :], in_=pb[0:np_, 0:512], func=AF.Exp, scale=-1.0), [pb], [gtmp])
                        S.op(DVE, lambda: V.tensor_tensor(out=rbt[0:np_, t, :], in0=pb[0:np_, 0:512], in1=ggla[0:np_, :], op=ALU.mult),
                             [pb, ggla], [rbt])
                        S.op(DVE, lambda: V.tensor_scalar_add(out=gtmp[0:np_, :], in0=gtmp[0:np_, :], scalar1=1.0), [gtmp], [gtmp])
                        S.op(DVE, lambda: V.reciprocal(out=gtmp[0:np_, :], in_=gtmp[0:np_, :]), [gtmp], [gtmp])
                        S.op(DVE, lambda: V.tensor_tensor(out=rbt[0:np_, t, :], in0=rbt[0:np_, t, :], in1=gtmp[0:np_, :], op=ALU.mult),
                             [rbt, gtmp], [rbt])

                    evs.append((ev_k, ev_v, ev_kb, ev_vb, ev_rb))

                bk = 0
                fproj(AL, 16, alT, alT[:, 0:ntok], bk); bk ^= 1
                for c in range(2):
                    fproj(QB + c * 128, 128, qTb, qTb[:, c, 0:ntok], bk); bk ^= 1
                for c in range(2):
                    fproj(KB + c * 128, 128, kTb, kTb[:, c, 0:ntok], bk); bk ^= 1
                for t in range(nt):
                    ev_k, ev_v, ev_kb, ev_vb, ev_rb = evs[t]
                    tproj(t, KB, 256, bk, ev_kb); bk ^= 1
                    tproj(t, VB, 512, bk, ev_vb); bk ^= 1
                    tproj(t, RB, 512, bk, ev_rb); bk ^= 1

                def gla_all():
                    gens = [gla_chunk(t, np_, sample) for t in range(nt)]
                    state_done = [False] * nt
                    waiting = [False] * nt
                    started = min(2, nt)
                    active = list(range(started))
                    while active:
                        for t in list(active):
                            if waiting[t]:
                                if t > 0 and not state_done[t - 1]:
                                    continue
                                waiting[t] = False
                            try:
                                r = next(gens[t])
                            except StopIteration:
                                active.remove(t)
                                if started < nt:
                                    active.append(started)
                                    started += 1
                                continue
                            if r == "need_state":
                                waiting[t] = True
                            elif r == "state_done":
                                state_done[t] = True
                            yield
                gla = gla_all()

                def pump(n):
                    for _ in range(n):
                        if next(gla, "done") == "done":
                            return

                npump = 0 if sample else DBG["npump"]
                for c in range(4):
                    fproj(QA + c * 128, 128, qTa, qTa[:, c, 0:ntok], bk); bk ^= 1
                    pump(npump)
                kcol0 = (g_idx * G) if not sample else 0
                for c in range(4):
                    fproj(KA + c * 128, 128, kTa_cur, kTa_cur[:, c, kcol0:kcol0 + ntok], bk); bk ^= 1
                    pump(npump)
                for t in range(nt):
                    ev_k, ev_v, ev_kb, ev_vb, ev_rb = evs[t]
                    tproj(t, KA, 512, bk, ev_k); bk ^= 1
                    pump(npump)
                    tproj(t, VA, 512, bk, ev_v); bk ^= 1
                    pump(npump)

                if not sample:
                    attention_prompt(g_idx, gla, nxt)
                else:
                    attention_sample(gla)
                    for _ in gla:
                        pass
                if DBG["stage"] < 5:
                    return
                for t in range(nt):
                    transpose_tile(mixed, lambda c, t=t: mixed[0:np_, t, c * 128:(c + 1) * 128], np_, mixT,
                                   mixT[:, :, t * np_:(t + 1) * np_], 8, t % 2)

                def wo_mm(t):
                    S.dma(SP, xr[t % 2][0:np_, :], x_src_fn(t), [], [xr[t % 2]])
                    for hf in range(2):
                        pw = ps[(2 * t + hf) % 6]
                        for kc in range(8):
                            S.op(PE, lambda kc=kc, hf=hf, pw=pw: P.matmul(pw[0:np_, :], lhsT=mixT[:, kc, t * np_:(t + 1) * np_],
                                                                          rhs=wo[:, kc, hf * 512:(hf + 1) * 512],
                                                                          start=(kc == 0), stop=(kc == 7)),
                                 [wo, mixT], [pw], inc=(kc == 7))

                def wo_epi(t):
                    for hf in range(2):
                        pw = ps[(2 * t + hf) % 6]
                        S.op(ACT, lambda hf=hf, pw=pw: A.activation(out=junk[0:np_, hf * 512:(hf + 1) * 512], in_=pw[0:np_, :],
                                                                    func=AF.Square, accum_out=ssq[0:np_, 4 + hf:5 + hf]),
                             [pw], [junk, ssq])
                    S.op(DVE, lambda: V.tensor_tensor(out=ssq[0:np_, 6:7], in0=ssq[0:np_, 4:5], in1=ssq[0:np_, 5:6], op=ALU.add),
                         [ssq], [ssq])
                    rms_rstd(ssq[0:np_, 6:7], rstd[0:np_, 6:7], np_, lnt[0:np_, 6:7], ssq, rstd, lnt, D)
                    for hf in range(2):
                        pw = ps[(2 * t + hf) % 6]
                        S.op(DVE, lambda hf=hf, pw=pw: V.scalar_tensor_tensor(out=ytmp[0:np_, hf * 512:(hf + 1) * 512], in0=pw[0:np_, :],
                                                                              scalar=rstd[0:np_, 6:7],
                                                                              in1=gpost[0:np_, hf * 512:(hf + 1) * 512],
                                                                              op0=ALU.mult, op1=ALU.mult),
                             [pw, rstd, gpost], [ytmp])
                    xr_ = xr[t % 2]
                    S.op(DVE, lambda: V.tensor_tensor(out=xr_[0:np_, :], in0=ytmp[0:np_, :], in1=xr_[0:np_, :], op=ALU.add),
                         [ytmp, xr_], [xr_])
                    S.dma(SP, x1_dst_fn(t), xr_[0:np_, :], [xr_], [x1buf], sembuf=xgo[t % 2])

                wo_mm(0)
                for t in range(nt):
                    if t + 1 < nt:
                        wo_mm(t + 1)
                    wo_epi(t)

            def attention_prompt(g_idx, gla, nxt=None):
                nblk = 4 * g_idx + 4
                blocks = [(pr, jb) for pr in range(4) for jb in range(nblk)]
                stb = [ps[4], ps[5], ps[0], ps[1]]
                n_gla = 4 * 52 - 48
                per_blk = DBG["perblk"] or -(-n_gla // len(blocks))

                def geo(jb):
                    d = 4 * g_idx - jb
                    return d, 128 * max(0, -d)

                def A_(b):
                    pr, jb = blocks[b]
                    d, q0 = geo(jb)
                    for e in range(2):
                        st = stb[2 * (b % 2) + e]
                        r0 = 64 * e
                        S.op(PE, lambda st=st, r0=r0: P.matmul(st[:, q0:G], lhsT=kTa[r0:r0 + 64, pr, jb * 128:(jb + 1) * 128],
                                                               rhs=qTa[r0:r0 + 64, pr, q0:G], start=True, stop=True),
                             [kTa, qTa], [st])

                def B_(b):
                    pr, jb = blocks[b]
                    d, q0 = geo(jb)
                    for e in range(2):
                        st = stb[2 * (b % 2) + e]
                        pt = pT[2 * (b % 2) + e]
                        S.op(ACT, lambda st=st, pt=pt: A.activation(out=pt[:, q0:G], in_=st[:, q0:G], func=AF.Exp, scale=0.125),
                             [st], [pt])
                    for e in range(2):
                        pt = pT[2 * (b % 2) + e]
                        S.op(DVE, lambda pt=pt: V.tensor_tensor(
                            out=pt[:, q0:G], in0=pt[:, q0:G],
                            in1=amask[:, d + 3:d + 7, :].rearrange("p a b -> p (a b)")[:, q0:G], op=ALU.mult), [pt, amask], [pt])

                def C_(b):
                    pr, jb = blocks[b]
                    d, q0 = geo(jb)
                    ibs = list(range(max(0, -d), 4))
                    for e in range(2):
                        h = 2 * pr + e
                        oacc = ps[2 + e]
                        pt = pT[2 * (b % 2) + e]
                        for ib in ibs:
                            S.op(PE, lambda ib=ib, oacc=oacc, pt=pt, h=h: P.matmul(
                                oacc[:, ib * 65:(ib + 1) * 65], lhsT=pt[:, ib * 128:(ib + 1) * 128], rhs=vaug[:, jb, h, :],
                                start=(jb == 0 and ib == ibs[0]), stop=(jb == 4 * g_idx + ib), skip_group_check=True),
                                 [pt, vaug], [oacc], inc=(ib == ibs[-1]))
                    if jb == nblk - 1:
                        for e in range(2):
                            h = 2 * pr + e
                            oacc = ps[2 + e]
                            o3 = oacc[:, 0:260].rearrange("p (i e) -> p i e", i=4)
                            rd = rden[e]
                            S.op(DVE, lambda o3=o3, rd=rd: V.reciprocal(out=rd[:, :], in_=o3[:, :, 64]), [oacc], [rd])
                            S.op(DVE, lambda o3=o3, rd=rd, h=h: V.tensor_tensor(
                                out=mixed[:, :, h * 64:(h + 1) * 64], in0=o3[:, :, 0:64],
                                in1=rd[:, :].unsqueeze(2).broadcast_to([128, 4, 64]), op=ALU.mult), [oacc, rd], [mixed])

                nbk = len(blocks)
                fpos = {(nbk * (2 * t + 1)) // 9: t for t in range(4)} if nxt is not None else {}
                A_(0)
                for b in range(nbk):
                    if b + 1 < nbk:
                        A_(b + 1)
                    B_(b)
                    C_(b)
                    if b in fpos:
                        front_a(nxt(fpos[b]), 128, fpos[b])
                    for _ in range(per_blk):
                        if next(gla, "done") == "done":
                            break
                for _ in gla:
                    pass

            def attention_sample(gla):
                oaccs = [ps[2], ps[3]]
                sts = [ps[4], ps[5]]
                first = [True, True]

                def scores(nk, kT_t, kT_fn, mask_t, mask_ap, slot):
                    for h in range(HA):
                        c, r0 = h // 2, 64 * (h % 2)
                        st = sts[h % 2]
                        S.op(PE, lambda c=c, r0=r0, st=st: P.matmul(st[0:nk, c * NST:(c + 1) * NST], lhsT=kT_fn(c, r0),
                                                                    rhs=qTa[r0:r0 + 64, c, 0:NST], start=True, stop=True),
                             [kT_t, qTa], [st], inc=(h >= HA - 2))
                    pt = pTs[slot]
                    for hh in range(2):
                        S.op(ACT, lambda hh=hh: A.activation(
                            out=pt[0:nk, :, :].rearrange("p (c hh) t -> p c hh t", hh=2)[:, :, hh, :],
                            in_=sts[hh][0:nk, 0:4 * NST].rearrange("p (c t) -> p c t", c=4), func=AF.Exp, scale=0.125),
                             [sts[hh]], [pt])
                    S.op(DVE, lambda: V.tensor_tensor(out=pt[0:nk, :, :], in0=pt[0:nk, :, :],
                                                      in1=mask_ap.unsqueeze(1).broadcast_to([nk, HA, NST]), op=ALU.mult),
                         [pt, mask_t], [pt])

                def pv(nk, v_t, v_fn, slot, is_last):
                    pt = pTs[slot]
                    for h in range(HA):
                        oa = oaccs[h // 4]
                        col = (h % 4) * 65
                        S.op(PE, lambda h=h, oa=oa, col=col, fl=first[h // 4]: P.matmul(
                            oa[0:NST, col:col + 65], lhsT=pt[0:nk, h, :], rhs=v_fn(h), start=fl, stop=is_last,
                            skip_group_check=True), [pt, v_t], [oa], inc=(h % 4 == 3))
                        first[h // 4] = False

                S.op(POOL, lambda: GP.memset(vaugc[0][:, :, 64:65], 1.0), [], [vaugc[0]])
                S.op(POOL, lambda: GP.memset(vaugc[1][:, :, 64:65], 1.0), [], [vaugc[1]])
                cblocks = [(sq, jb) for sq in range(NSS) for jb in range(16)]

                def load_d(m):
                    sq, jb = cblocks[2 * m]
                    sl2 = m % 2
                    S.dma(POOL, kcb[sl2][:], ck[sq, jb * 128:(jb + 2) * 128, :].rearrange("(j p) f -> p j f", p=128), [], [kcb[sl2]])
                    S.dma(POOL, vcb[sl2][:], cv[sq, jb * 128:(jb + 2) * 128, :].rearrange("(j p) f -> p j f", p=128), [], [vcb[sl2]])

                def prep(n):
                    sl2, j = (n // 2) % 2, n % 2
                    sl = n % 2
                    kb_, vb_, kt_, va_ = kcb[sl2], vcb[sl2], kTc[sl], vaugc[sl]
                    transpose_tile(kb_, lambda c, kb_=kb_, j=j: kb_[:, j, c * 128:(c + 1) * 128], 128, kt_, kt_[:, :, :], 4, 2 + sl)
                    S.op(DVE, lambda vb_=vb_, va_=va_, j=j: V.tensor_copy(out=va_[:, :, 0:64],
                                                                          in_=vb_[:, j, :].rearrange("p (h e) -> p h e", h=HA)),
                         [vb_], [va_])

                def sc_(n):
                    sq, jb = cblocks[n]
                    kt_ = kTc[n % 2]
                    scores(128, kt_, lambda c, r0, kt_=kt_: kt_[r0:r0 + 64, c, :], smask, smask[:, sq * 16 + jb, :], n % 2)

                def pv_(n):
                    va_ = vaugc[n % 2]
                    pv(128, va_, lambda h, va_=va_: va_[:, h, :], n % 2, False)

                nb_ = len(cblocks)
                nd_ = nb_ // 2
                load_d(0)
                load_d(1)
                prep(0)
                prep(1)
                sc_(0)
                for n in range(nb_):
                    if n + 1 < nb_:
                        sc_(n + 1)
                    pv_(n)
                    if n + 2 < nb_:
                        prep(n + 2)
                    if n % 2 == 1 and (n // 2) + 2 < nd_:
                        load_d(n // 2 + 2)
                    next(gla, None)
                    next(gla, None)
                scores(NST, kTa_cur, lambda c, r0: kTa_cur[r0:r0 + 64, c, 0:NST], nmask, nmask[:, :], 0)
                pv(NST, vaug_cur, lambda h: vaug_cur[0:NST, 0, h, :], 0, True)
                for half in range(2):
                    oa = oaccs[half]
                    o3 = oa[0:NST, 0:260].rearrange("p (i e) -> p i e", i=4)
                    S.op(DVE, lambda o3=o3: V.reciprocal(out=rden[0][0:NST, :], in_=o3[:, :, 64]), [oa], [rden[0]])
                    S.op(DVE, lambda o3=o3, half=half: V.tensor_tensor(
                        out=mixed[0:NST, 0, half * 256:(half + 1) * 256].rearrange("p (i e) -> p i e", i=4),
                        in0=o3[:, :, 0:64], in1=rden[0][0:NST, :].unsqueeze(2).broadcast_to([NST, 4, 64]), op=ALU.mult),
                         [oa, rden[0]], [mixed])

            def gla_chunk(t, np_, sample):
                sl = t % 2
                bank = ps[6 + sl]
                lz, ebT, enbT, qeTd, keT, kl, attm = lzs[sl], ebTs[sl], enbTs[sl], qeTds[sl], keTs[sl], klbs[sl], attms[sl]
                oss, olt, orstd, osq = osss[sl], olts[sl], orstds[sl], osqs[sl]
                ez = ekl = lz
                tok = slice(t * np_, (t + 1) * np_)
                U_inc, U_aft, CA = (suinc, suaft, scaus) if sample else (uinc, uaft, caus)
                pz = bank
                S.op(PE, lambda: P.matmul(pz[0:np_, 0:256], lhsT=alT[0:16, tok], rhs=wa2[0:16, :], start=True, stop=False),
                     [alT, wa2], [pz], inc=False)
                S.op(PE, lambda: P.matmul(pz[0:np_, 0:256], lhsT=ones_r[0:1, 0:np_], rhs=bar[0:1, :], start=False, stop=True),
                     [ones_r, bar], [pz])
                yield
                S.op(ACT, lambda: A.activation(out=ez[0:np_, :], in_=pz[0:np_, 0:256], func=AF.Exp, scale=-1.0), [pz], [ez])
                yield
                S.op(ACT, lambda: A.activation(out=lz[0:np_, :], in_=ez[0:np_, :], func=AF.Ln, bias=oneb[0:np_, 0:1]), [ez, oneb], [lz])
                yield
                pb = bank
                for kc in range(2):
                    S.op(PE, lambda kc=kc: P.matmul(pb[:, kc * 128:kc * 128 + np_], lhsT=lz[0:np_, kc * 128:(kc + 1) * 128],
                                                    rhs=U_inc[0:np_, 0:np_], start=True, stop=True),
                         [lz, U_inc], [pb], inc=False)
                pa = bank
                S.op(PE, lambda: P.matmul(pa[0:np_, 256:512], lhsT=U_aft[0:np_, 0:np_], rhs=lz[0:np_, :], start=True, stop=True),
                     [lz, U_aft], [pa])
                yield
                pb3 = pb[:, 0:256].rearrange("p (c i) -> p c i", c=2)[:, :, 0:np_]
                S.op(ACT, lambda: A.activation(out=ebT[:, :, 0:np_], in_=pb3, func=AF.Exp), [pb], [ebT])
                yield
                S.op(ACT, lambda: A.activation(out=enbT[:, :, 0:np_], in_=pb3, func=AF.Exp, scale=-1.0), [pb], [enbT])
                yield
                S.op(ACT, lambda: A.activation(out=ekl[0:np_, :], in_=pa[0:np_, 256:512], func=AF.Exp), [pa], [ekl])
                yield
                for hh in range(2):
                    r0 = 64 * hh
                    S.op(DVE, lambda hh=hh, r0=r0: V.scalar_tensor_tensor(
                        out=qeTd[r0:r0 + 64, :, hh, 0:np_], in0=qTb[r0:r0 + 64, :, tok], scalar=0.125,
                        in1=ebT[r0:r0 + 64, :, 0:np_], op0=ALU.mult, op1=ALU.mult), [qTb, ebT], [qeTd])
                    yield
                S.op(DVE, lambda: V.tensor_tensor(out=keT[:, :, 0:np_], in0=kTb[:, :, tok], in1=enbT[:, :, 0:np_], op=ALU.mult),
                     [kTb, enbT], [keT])
                yield
                S.op(DVE, lambda: V.tensor_tensor(out=kl[0:np_, :], in0=kbt[0:np_, t, :], in1=ekl[0:np_, :], op=ALU.mult),
                     [kbt, ekl], [kl])
                yield
                pat = bank
                if np_ == 128:
                    for c in range(2):
                        S.op(PE, lambda c=c: P.matmul(pat[:, c * 256:(c + 1) * 256], lhsT=keT[:, c, :],
                                                      rhs=qeTd[:, c, :, :].rearrange("p hh i -> p (hh i)"), start=True, stop=True),
                             [keT, qeTd], [pat], inc=(c == 1))
                else:
                    for h in range(4):
                        S.op(PE, lambda h=h: P.matmul(pat[0:np_, h * 128:h * 128 + np_], lhsT=keT[:, h // 2, 0:np_],
                                                      rhs=qeTd[:, h // 2, h % 2, 0:np_], start=True, stop=True),
                             [keT, qeTd], [pat], inc=(h == 3))
                yield
                S.op(DVE, lambda: V.tensor_tensor(
                    out=attm[0:np_, :, 0:np_], in0=pat[0:np_, :].rearrange("p (h i) -> p h i", h=4)[:, :, 0:np_],
                    in1=CA[0:np_, 0:np_].unsqueeze(1).broadcast_to([np_, 4, np_]), op=ALU.mult), [pat, CA], [attm])
                yield
                yield "need_state"
                po = bank
                if not sample:
                    for h in range(4):
                        c, r0 = h // 2, 64 * (h % 2)
                        S.op(PE, lambda h=h: P.matmul(po[0:np_, h * 128:(h + 1) * 128], lhsT=attm[0:np_, h, 0:np_],
                                                      rhs=vbt[0:np_, t, h * 128:(h + 1) * 128], start=True, stop=False),
                             [attm, vbt], [po], inc=False)
                        S.op(PE, lambda h=h, c=c, r0=r0: P.matmul(po[0:np_, h * 128:(h + 1) * 128], lhsT=qeTd[r0:r0 + 64, c, h % 2, 0:np_],
                                                                  rhs=Sbf[r0:r0 + 64, c, :], start=False, stop=True),
                             [qeTd, Sbf], [po], inc=(h == 3))
                        yield
                else:
                    for sq in range(NSS):
                        S.op(DVE, lambda sq=sq: V.tensor_tensor(out=qeTs[:, sq, :, :],
                                                                in0=qeTd[:, :, :, 0:NST].rearrange("p c hh i -> p (c hh) i"),
                                                                in1=ssel[:, sq, :].unsqueeze(1).broadcast_to([128, 4, NST]),
                                                                op=ALU.mult), [qeTd, ssel], [qeTs])
                        yield
                    for h in range(4):
                        c, r0 = h // 2, 64 * (h % 2)
                        S.op(PE, lambda h=h: P.matmul(po[0:NST, h * 128:(h + 1) * 128], lhsT=attm[:, h, 0:NST],
                                                      rhs=vbt[:, t, h * 128:(h + 1) * 128], start=True, stop=False),
                             [attm, vbt], [po], inc=False)
                        for sq in range(NSS):
                            S.op(PE, lambda h=h, c=c, r0=r0, sq=sq: P.matmul(
                                po[0:NST, h * 128:(h + 1) * 128], lhsT=qeTs[r0:r0 + 64, sq, h, :], rhs=Sbf_s[r0:r0 + 64, sq, c, :],
                                start=False, stop=(sq == NSS - 1)), [qeTs, Sbf_s], [po], inc=(h == 3 and sq == NSS - 1))
                        yield
                S.op(ACT, lambda: A.activation(out=osq[0:np_, :], in_=po[0:np_, :], func=AF.Copy), [po], [osq])
                yield
                pd = bank
                if not sample:
                    for c in range(2):
                        S.op(PE, lambda c=c: P.matmul(pd[:, c * 256:(c + 1) * 256], lhsT=kl[0:np_, c * 128:(c + 1) * 128],
                                                      rhs=vbt[0:np_, t, c * 256:(c + 1) * 256], start=True, stop=True),
                             [kl, vbt], [pd], inc=(c == 1))
                    yield
                    for c in range(2):
                        for hh in range(2):
                            r0 = 64 * hh
                            S.op(DVE, lambda c=c, hh=hh, r0=r0: V.scalar_tensor_tensor(
                                out=Sst[r0:r0 + 64, c, :], in0=Sst[r0:r0 + 64, c, :], scalar=ebT[r0:r0 + 64, c, np_ - 1:np_],
                                in1=pd[r0:r0 + 64, c * 256 + hh * 128:c * 256 + (hh + 1) * 128], op0=ALU.mult, op1=ALU.add),
                                 [Sst, ebT, pd], [Sst])
                            yield
                    S.op(ACT, lambda: A.activation(out=Sbf[:], in_=Sst[:], func=AF.Copy), [Sst], [Sbf])
                    yield "state_done"
                else:
                    for sq in range(NSS):
                        S.op(DVE, lambda sq=sq: V.tensor_scalar(out=kls[0:NST, :], in0=kl[0:NST, :], scalar1=srow[0:NST, sq:sq + 1],
                                                                scalar2=None, op0=ALU.mult), [kl, srow], [kls])
                        yield
                        for c in range(2):
                            S.op(PE, lambda c=c: P.matmul(pd[:, c * 256:(c + 1) * 256], lhsT=kls[0:NST, c * 128:(c + 1) * 128],
                                                          rhs=vbt[0:NST, t, c * 256:(c + 1) * 256], start=True, stop=True),
                                 [kls, vbt], [pd], inc=(c == 1))
                        yield
                        for c in range(2):
                            for hh in range(2):
                                r0 = 64 * hh
                                S.op(DVE, lambda c=c, hh=hh, r0=r0, sq=sq: V.scalar_tensor_tensor(
                                    out=Sst_s[r0:r0 + 64, sq, c, :], in0=Sst_s[r0:r0 + 64, sq, c, :],
                                    scalar=ebT[r0:r0 + 64, c, TS * sq + TS - 1:TS * sq + TS],
                                    in1=pd[r0:r0 + 64, c * 256 + hh * 128:c * 256 + (hh + 1) * 128], op0=ALU.mult, op1=ALU.add),
                                     [Sst_s, ebT, pd], [Sst_s])
                                yield
                for h in range(4):
                    S.op(ACT, lambda h=h: A.activation(out=lz[0:np_, 0:128], in_=osq[0:np_, h * 128:(h + 1) * 128], func=AF.Square,
                                                       accum_out=oss[0:np_, h:h + 1]), [osq], [lz, oss])
                    yield
                rms_rstd(oss[0:np_, :], orstd[0:np_, :], np_, olt[0:np_, :], oss, orstd, olt, 128)
                yield
                for h in range(4):
                    S.op(DVE, lambda h=h: V.scalar_tensor_tensor(out=mixed[0:np_, t, 512 + h * 128:512 + (h + 1) * 128],
                                                                 in0=osq[0:np_, h * 128:(h + 1) * 128], scalar=orstd[0:np_, h:h + 1],
                                                                 in1=rbt[0:np_, t, h * 128:(h + 1) * 128], op0=ALU.mult, op1=ALU.mult),
                         [osq, orstd, rbt], [mixed])
                    yield

            x1buf = Buf("x1dram")
            xgo = [Buf("xgo0"), Buf("xgo1")]
            with ExitStack() as esp:
                amask = T(S, esp, "amask", [128, 19, 128], BF16)
                S.dma(SP, amask[:], c_amask[:, :, :], [], [amask])
                uinc = T(S, esp, "uinc", [128, 128])
                S.dma(SP, uinc[:], c_uinc[:, :], [], [uinc])
                uaft = T(S, esp, "uaft", [128, 128])
                S.dma(SP, uaft[:], c_uaft[:, :], [], [uaft])
                caus = T(S, esp, "caus", [128, 128], BF16)
                S.dma(SP, caus[:], c_caus[:, :], [], [caus])
                kTa = T(S, esp, "kTa", [128, 4, SEQ], BF16)
                vaug = T(S, esp, "vaug", [128, 16, HA, 65], BF16)
                S.op(POOL, lambda: GP.memset(vaug[:, :, :, 64:65], 1.0), [], [vaug])
                Sst = T(S, esp, "Sst", [128, 2, 128])
                Sbf = T(S, esp, "Sbf", [128, 2, 128], BF16)
                kTa_cur, vaug_cur = kTa, vaug
                glist = [(sq_, g_) for sq_ in range(DBG["nseq"]) for g_ in range(DBG["ngrp"])]

                def xsrc(sq_, g_):
                    return lambda t: xp[sq_, g_ * G + t * 128:g_ * G + (t + 1) * 128, :]
                for gi, (sq_, g_) in enumerate(glist):
                    if g_ == 0:
                        S.op(DVE, lambda: V.memset(Sst[:], 0.0), [], [Sst])
                        S.op(DVE, lambda: V.memset(Sbf[:], 0.0), [], [Sbf])
                    tok0 = g_ * G
                    nxt = xsrc(*glist[gi + 1]) if gi + 1 < len(glist) else None
                    mixer_group(4, 128, xsrc(sq_, g_), sq_, g_, g_ == 0,
                                lambda t, s=sq_, tok0=tok0: wk_p[s, tok0 + t * 128:tok0 + (t + 1) * 128, :],
                                lambda t, s=sq_, tok0=tok0: wv_p[s, tok0 + t * 128:tok0 + (t + 1) * 128, :],
                                lambda t, s=sq_, tok0=tok0: x1_p[s, tok0 + t * 128:tok0 + (t + 1) * 128, :],
                                front_done=(gi > 0), nxt=nxt)
                    if g_ == DBG["ngrp"] - 1:
                        for c in range(2):
                            for hh in range(2):
                                store(POOL, gs_p[sq_, 2 * c + hh, :, :], Sst[64 * hh:64 * hh + 64, c, :], Sst)

            S.barrier()
            if DBG["sample"]:
                ess = es.enter_context(ExitStack())
                smask = T(S, ess, "smask", [128, NSS * 16, NST], BF16)
                S.dma(SP, smask[:], c_smask[:, :, :], [], [smask])
                nmask = T(S, ess, "nmask", [NST, NST], BF16)
                S.dma(SP, nmask[:], c_nmask[:, :], [], [nmask])
                suinc = T(S, ess, "suinc", [NST, NST])
                S.dma(SP, suinc[:], c_suinc[:, :], [], [suinc])
                suaft = T(S, ess, "suaft", [NST, NST])
                S.dma(SP, suaft[:], c_suaft[:, :], [], [suaft])
                scaus = T(S, ess, "scaus", [NST, NST], BF16)
                S.dma(SP, scaus[:], c_scaus[:, :], [], [scaus])
                ssel = T(S, ess, "ssel", [128, NSS, NST])
                S.dma(SP, ssel[:], c_ssel[:, :, :], [], [ssel])
                srow = T(S, ess, "srow", [NST, NSS])
                S.dma(SP, srow[:], c_srow[:, :], [], [srow])
                Sst_s = T(S, ess, "Sst_s", [128, NSS, 2, 128])
                Sbf_s = T(S, ess, "Sbf_s", [128, NSS, 2, 128], BF16)
                qeTs = T(S, ess, "qeTs", [128, NSS, 4, NST], BF16)
                kls = T(S, ess, "kls", [NST, 256], BF16)
                kcb = [T(S, ess, "kcb%d" % i, [128, 2, 512], BF16) for i in range(2)]
                vcb = [T(S, ess, "vcb%d" % i, [128, 2, 512], BF16) for i in range(2)]
                kTc = [T(S, ess, "kTc%d" % i, [128, 4, 128], BF16) for i in range(2)]
                vaugc = [T(S, ess, "vaugc%d" % i, [128, HA, 65], BF16) for i in range(2)]
                pTs = [T(S, ess, "pTs%d" % i, [128, HA, NST], BF16) for i in range(2)]
                kTs = T(S, ess, "kTs", [128, 4, NST], BF16)
                vaug_s = T(S, ess, "vaug_s", [128, 1, HA, 65], BF16)
                S.op(POOL, lambda: GP.memset(vaug_s[:, :, :, 64:65], 1.0), [], [vaug_s])
                kTa_cur, vaug_cur = kTs, vaug_s
                S.op(DVE, lambda: V.memset(attms[0][:], 0.0), [], [attms[0]])
                S.op(DVE, lambda: V.memset(vbt[:], 0.0), [], [vbt])
                for sq in range(NSS):
                    for c in range(2):
                        for hh in range(2):
                            S.dma(SP, Sst_s[64 * hh:64 * hh + 64, sq, c, :], sg[sq, 2 * c + hh, :, :], [], [Sst_s])
                S.op(ACT, lambda: A.activation(out=Sbf_s[:], in_=Sst_s[:], func=AF.Copy), [Sst_s], [Sbf_s])
                mixer_group(1, NST, lambda t: xs[0:NST, :], 0, 0, True,
                            lambda t: wk_s[0:NST, :], lambda t: wv_s[0:NST, :], lambda t: x1_s[0:NST, :], sample=True)
                for sq in range(NSS):
                    for c in range(2):
                        for hh in range(2):
                            store(POOL, gs_s[sq, 2 * c + hh, :, :], Sst_s[64 * hh:64 * hh + 64, sq, c, :], Sst_s)

        S.barrier()
        with ExitStack() as es:
            wup = T(S, es, "wup", [128, 8, 2 * DFF], BF16)
            wdn = T(S, es, "wdn", [128, NFC, D], BF16)
            class _Blk2:
                def __init__(self, name):
                    self.buf = Buf(name)
            w_up3 = w_up.rearrange("(kc p) n -> p kc n", p=128)
            wupg = [_Blk2("wupg%d" % i) for i in range(6)]
            wupu = [_Blk2("wupu%d" % i) for i in range(6)]
            for i in range(6):
                ncol = min(512, DFF - i * 512)
                S.dma(POOL, wup[:, :, i * 512:i * 512 + ncol], w_up3[:, :, i * 512:i * 512 + ncol], [], [wupg[i]])
                S.dma(POOL, wup[:, :, DFF + i * 512:DFF + i * 512 + ncol], w_up3[:, :, DFF + i * 512:DFF + i * 512 + ncol], [], [wupu[i]])
            for fc in range(NFC):
                S.dma(POOL, wdn[:, fc, :], w_down[fc * 128:(fc + 1) * 128, :], [], [wdn])
            gpre2 = T(S, es, "gpre2", [128, D])
            S.dma(SP, gpre2[:], g_pre_ffn[0:1, :].partition_broadcast(128), [], [gpre2])
            gpost2 = T(S, es, "gpost2", [128, D])
            S.dma(SP, gpost2[:], g_post_ffn[0:1, :].partition_broadcast(128), [], [gpost2])
            cw = T(S, es, "cw", [128, 4, NFC])
            cwt = T(S, es, "cwt", [NFC, 4, 128])
            id32 = T(S, es, "id32", [32, 32])
            S.dma(SP, id32[:], c_id32[:, :], [], [id32])
            S.dma(SP, cwt[:, 0:3, :], conv_w.rearrange("j (c p) -> c j p", p=128), [], [cwt])
            S.dma(SP, cwt[:, 3, :], conv_b[0, :].rearrange("(c p) -> c p", p=128), [], [cwt])
            for j in range(4):
                S.op(PE, lambda j=j: P.matmul(ps[0][:, j * 32:j * 32 + NFC], lhsT=cwt[0:NFC, j, :], rhs=id32[0:NFC, 0:NFC],
                                              start=True, stop=True), [cwt, id32], [ps[0]], inc=(j == 3))
            S.op(ACT, lambda: A.activation(out=cw[:], in_=ps[0][:, 0:128].rearrange("p (j c) -> p j c", j=4)[:, :, 0:NFC],
                                           func=AF.Copy), [ps[0]], [cw])

            xs1 = [T(S, es, "xs1_%d" % i, [128, D]) for i in range(2)]
            xr = [T(S, es, "xr_%d" % i, [128, D]) for i in range(2)]
            xnbs = [T(S, es, "xnb2_%d" % i, [128, D], BF16) for i in range(2)]
            st_ss = [T(S, es, "pss%d" % i, [128, 1]) for i in range(2)]
            st_ln = [T(S, es, "pln%d" % i, [128, 1]) for i in range(2)]
            st_rs = [T(S, es, "prs%d" % i, [128, 1]) for i in range(2)]
            ssq = T(S, es, "ssq2", [128, 8])
            lnt = T(S, es, "lnt2", [128, 8])
            rstd = T(S, es, "rstd2", [128, 8])
            xnT = T(S, es, "xnT2", [128, 8, G], BF16)
            hT = T(S, es, "hT", [128, NFC, G], BF16)

            class _Sub:
                def __init__(self, fc):
                    self.buf = Buf("hT%d" % fc)
            hTs = [_Sub(fc) for fc in range(NFC)]
            junkb = T(S, es, "junkb", [128, 512], BF16)
            gbuf = [T(S, es, "gbuf%d" % i, [128, G + 2]) for i in range(1)] * 2
            cbuf = [T(S, es, "cbuf%d" % i, [128, G]) for i in range(1)] * 2
            gel = [T(S, es, "gel%d" % i, [128, G]) for i in range(1)] * 2
            carry = T(S, es, "carry", [128, NFC, 2])
            ytmp = [T(S, es, "ytmp2_%d" % i, [128, 512]) for i in range(2)]
            gl = [T(S, es, "gl%d" % i, [NST, 512]) for i in range(1)] * 2
            schist = T(S, es, "schist", [128, NFC, 8])
            gbs = T(S, es, "gbs", [128, NSS, TS + 2])
            id8 = T(S, es, "id8", [8, 8])
            S.dma(SP, id8[:], c_id8[:, :], [], [id8])
            xgo2 = [Buf("xgo2_%d" % i) for i in range(2)]

            def prenorm_a(x_src, np_, slot):
                xs_, xb_ = xs1[slot], xnbs[slot]
                S.dma(SP, xs_[0:np_, :], x_src, [x1buf], [xs_])
                S.op(ACT, lambda: A.activation(out=xb_[0:np_, :], in_=xs_[0:np_, :], func=AF.Square,
                                               accum_out=st_ss[slot][0:np_, 0:1]), [xs_], [xb_, st_ss[slot]])
                rms_rstd(st_ss[slot][0:np_, 0:1], st_rs[slot][0:np_, 0:1], np_, st_ln[slot][0:np_, 0:1],
                         st_ss[slot], st_rs[slot], st_ln[slot], D)
                S.op(DVE, lambda: V.scalar_tensor_tensor(out=xb_[0:np_, :], in0=xs_[0:np_, :], scalar=st_rs[slot][0:np_, 0:1],
                                                         in1=gpre2[0:np_, :], op0=ALU.mult, op1=ALU.mult),
                     [xs_, st_rs[slot], gpre2], [xb_])

            def prenorm_b(t, np_, slot):
                xb_ = xnbs[slot]
                transpose_tile(xb_, lambda c: xb_[0:np_, c * 128:(c + 1) * 128], np_, xnT,
                               xnT[:, :, t * np_:(t + 1) * np_], 8, slot)

            def ffn_group(nt, np_, x_src_fn, y_dst_fn, first_in_seq, last_in_seq, fc_dst, sample=False, nxt=None):
                ntok = nt * np_
                if sample:
                    for n in range(6):
                        ncol = min(512, DFF - n * 512)
                        glt = gl[0]
                        S.dma(SP, glt[0:8, 0:ncol], sc[:, n * 512:n * 512 + ncol], [], [glt])
                        nf = ncol // 128
                        for j in range(nf):
                            S.op(PE, lambda j=j: P.matmul(ps[0][:, j * 8:(j + 1) * 8], lhsT=glt[0:8, j * 128:(j + 1) * 128],
                                                          rhs=id8[0:8, 0:8], start=True, stop=True), [glt, id8], [ps[0]],
                                 inc=(j == nf - 1))
                        S.op(ACT, lambda n=n, nf=nf: A.activation(out=schist[:, 4 * n:4 * n + nf, :],
                                                                  in_=ps[0][:, 0:nf * 8].rearrange("p (f r) -> p f r", r=8),
                                                                  func=AF.Copy), [ps[0]], [schist])
                if first_in_seq:
                    S.op(POOL, lambda: GP.memset(carry[:], 0.0), [], [carry])
                for fc in range(NFC):
                    pg, pu = ps[(2 * fc) % 6], ps[(2 * fc + 1) % 6]
                    gb, cb_, ge = gbuf[fc % 2], cbuf[fc % 2], gel[fc % 2]
                    for kc in range(8):
                        S.op(PE, lambda kc=kc: P.matmul(pg[:, 0:ntok], lhsT=wup[:, kc, fc * 128:(fc + 1) * 128],
                                                        rhs=xnT[:, kc, 0:ntok], start=(kc == 0), stop=(kc == 7)),
                             [wupg[fc // 4], xnT], [pg], inc=(kc == 7))
                    for kc in range(8):
                        S.op(PE, lambda kc=kc: P.matmul(pu[:, 0:ntok], lhsT=wup[:, kc, DFF + fc * 128:DFF + (fc + 1) * 128],
                                                        rhs=xnT[:, kc, 0:ntok], start=(kc == 0), stop=(kc == 7)),
                             [wupu[fc // 4], xnT], [pu], inc=(kc == 7))
                    if not sample:
                        S.op(POOL, lambda: GP.tensor_copy(out=gb[:, 0:2], in_=carry[:, fc, :]), [carry], [gb])
                        S.op(ACT, lambda: A.activation(out=gb[:, 2:2 + ntok], in_=pg[:, 0:ntok], func=AF.Copy), [pg], [gb])
                        S.op(POOL, lambda: GP.tensor_copy(out=carry[:, fc, :], in_=gb[:, ntok:ntok + 2]), [gb], [carry])
                        g2, g1, g0, co = gb[:, 2:2 + ntok], gb[:, 1:1 + ntok], gb[:, 0:ntok], cb_[:, 0:ntok]
                        gsrc = gb
                    else:
                        S.op(POOL, lambda: GP.tensor_copy(out=gbs[:, :, 0:2], in_=schist[:, fc, :].rearrange("p (s j) -> p s j", j=2)),
                             [schist], [gbs])
                        S.op(ACT, lambda: A.activation(out=gbs[:, :, 2:2 + TS], in_=pg[:, 0:NST].rearrange("p (s t) -> p s t", t=TS),
                                                       func=AF.Copy), [pg], [gbs])
                        g2, g1, g0 = gbs[:, :, 2:2 + TS], gbs[:, :, 1:1 + TS], gbs[:, :, 0:TS]
                        co = cb_[:, 0:NST].rearrange("p (s t) -> p s t", t=TS)
                        gsrc = gbs
                    S.op(DVE, lambda: V.tensor_scalar(out=co, in0=g2, scalar1=cw[:, 2, fc:fc + 1],
                                                      scalar2=cw[:, 3, fc:fc + 1], op0=ALU.mult, op1=ALU.add),
                         [gsrc, cw], [cb_])
                    S.op(DVE, lambda: V.scalar_tensor_tensor(out=co, in0=g1, scalar=cw[:, 1, fc:fc + 1],
                                                             in1=co, op0=ALU.mult, op1=ALU.add),
                         [gsrc, cw, cb_], [cb_])
                    S.op(DVE, lambda: V.scalar_tensor_tensor(out=co, in0=g0, scalar=cw[:, 0, fc:fc + 1],
                                                             in1=co, op0=ALU.mult, op1=ALU.add),
                         [gsrc, cw, cb_], [cb_])
                    S.op(ACT, lambda: A.activation(out=ge[:, 0:ntok], in_=cb_[:, 0:ntok], func=AF.Gelu), [cb_], [ge])
                    S.op(DVE, lambda: V.tensor_tensor(out=hT[:, fc, 0:ntok], in0=ge[:, 0:ntok], in1=pu[:, 0:ntok], op=ALU.mult),
                         [ge, pu], [hTs[fc]])
                if last_in_seq:
                    nrow = NST if sample else 2
                    for n in range(6):
                        ncol = min(512, DFF - n * 512)
                        pq = ps[n % 2]
                        for kc in range(8):
                            S.op(PE, lambda kc=kc: P.matmul(pq[0:nrow, 0:ncol], lhsT=xnT[:, kc, ntok - nrow:ntok],
                                                            rhs=wup[:, kc, n * 512:n * 512 + ncol], start=(kc == 0), stop=(kc == 7)),
                                 [wupg[n], xnT], [pq], inc=(kc == 7))
                        glt = gl[n % 2]
                        S.op(ACT, lambda: A.activation(out=glt[0:nrow, 0:ncol], in_=pq[0:nrow, 0:ncol], func=AF.Copy),
                             [pq], [glt])
                        if not sample:
                            store(POOL, fc_dst[:, n * 512:n * 512 + ncol], glt[0:2, 0:ncol], glt)
                        else:
                            for sq in range(NSS):
                                store(POOL, fc_s[sq, :, n * 512:n * 512 + ncol], glt[TS * sq + TS - 2:TS * sq + TS, 0:ncol], glt)

                def dn_mm(t):
                    for hf in range(2):
                        pdn = ps[(2 * t + hf) % 6]
                        for fc in range(NFC):
                            S.op(PE, lambda fc=fc, hf=hf, pdn=pdn: P.matmul(pdn[0:np_, :], lhsT=hT[:, fc, t * np_:(t + 1) * np_],
                                                                            rhs=wdn[:, fc, hf * 512:(hf + 1) * 512],
                                                                            start=(fc == 0), stop=(fc == NFC - 1)),
                                 [wdn, hTs[fc]], [pdn], inc=(fc == NFC - 1))

                def dn_epi(t):
                    xr_ = xr[t % 2]
                    for hf in range(2):
                        pdn = ps[(2 * t + hf) % 6]
                        S.op(ACT, lambda hf=hf, pdn=pdn: A.activation(out=junkb[0:np_, :], in_=pdn[0:np_, :],
                                                                      func=AF.Square, accum_out=ssq[0:np_, 4 + hf:5 + hf]),
                             [pdn], [junkb, ssq])
                    S.op(DVE, lambda: V.tensor_tensor(out=ssq[0:np_, 6:7], in0=ssq[0:np_, 4:5], in1=ssq[0:np_, 5:6], op=ALU.add),
                         [ssq], [ssq])
                    rms_rstd(ssq[0:np_, 6:7], rstd[0:np_, 6:7], np_, lnt[0:np_, 6:7], ssq, rstd, lnt, D)
                    for hf in range(2):
                        yt_ = ytmp[hf]
                        pdn = ps[(2 * t + hf) % 6]
                        S.op(DVE, lambda hf=hf, yt_=yt_, pdn=pdn: V.scalar_tensor_tensor(
                            out=yt_[0:np_, :], in0=pdn[0:np_, :], scalar=rstd[0:np_, 6:7],
                            in1=gpost2[0:np_, hf * 512:(hf + 1) * 512], op0=ALU.mult, op1=ALU.mult),
                             [pdn, rstd, gpost2], [yt_])
                        S.op(DVE, lambda hf=hf, yt_=yt_: V.tensor_tensor(out=xr_[0:np_, hf * 512:(hf + 1) * 512], in0=yt_[0:np_, :],
                                                                         in1=xr_[0:np_, hf * 512:(hf + 1) * 512], op=ALU.add),
                             [yt_, xr_], [xr_])
                    S.dma(SP, y_dst_fn(t), xr_[0:np_, :], [xr_], [], sembuf=xgo2[t % 2])
                    if xgo2[t % 2] not in out_bufs:
                        out_bufs.append(xgo2[t % 2])

                def dn_pre(t):
                    if nxt is not None:
                        prenorm_a(nxt(t), 128, t % 2)
                    S.dma(SP, xr[t % 2][0:np_, :], x_src_fn(t), [x1buf], [xr[t % 2]])

                if sample and nxt is not None:
                    for t4 in range(4):
                        prenorm_a(nxt(t4), 128, t4 % 2)
                        prenorm_b(t4, 128, t4 % 2)
                    nxt = None
                dn_pre(0)
                dn_mm(0)
                for t in range(nt):
                    if t + 1 < nt:
                        dn_pre(t + 1)
                        dn_mm(t + 1)
                    if nxt is not None:
                        prenorm_b(t, 128, t % 2)
                    dn_epi(t)

            groups = []
            for s in range(DBG["nseq"] if DBG["phaseB"] else 0):
                for g in range(DBG["ngrp"]):
                    tok0 = g * G
                    groups.append((lambda t, s=s, tok0=tok0: x1_p[s, tok0 + t * 128:tok0 + (t + 1) * 128, :],
                                   lambda t, s=s, tok0=tok0: y_p[s, tok0 + t * 128:tok0 + (t + 1) * 128, :],
                                   g == 0, g == DBG["ngrp"] - 1, fc_p[s, :, :]))
            if DBG["sample"] and DBG["phaseB"]:
                prenorm_a(x1_s[0:NST, :], NST, 0)
                prenorm_b(0, NST, 0)
                ffn_group(1, NST, lambda t: x1_s[0:NST, :], lambda t: y_s[0:NST, :], True, True, None, sample=True,
                          nxt=(groups[0][0] if groups else None))
            elif groups:
                for t in range(4):
                    prenorm_a(groups[0][0](t), 128, t % 2)
                    prenorm_b(t, 128, t % 2)
            for i, (xf, yf, fi, la, fcd) in enumerate(groups):
                ffn_group(4, 128, xf, yf, fi, la, fcd, nxt=(groups[i + 1][0] if i + 1 < len(groups) else None))

        for b in out_bufs:
            SP.h.wait_ge(b.dsem, b.dcnt)
    return nc


def _mult(delta):
    delta = np.asarray(delta)
    nn = delta >= 0
    m = (nn & (delta <= 128)).astype(np.float32)
    m += (nn & (delta <= 512) & (delta % 4 == 0))
    m += (nn & (delta <= 2048) & (delta % 16 == 0))
    return m


def _constants():
    bf = ml_dtypes.bfloat16
    c = {}
    c["c_ident"] = np.eye(128, dtype=np.float32).astype(bf)
    jj = np.arange(128)[:, None]
    ii = np.arange(128)[None, :]
    am = np.zeros((128, 19, 128), np.float32)
    for e in range(-3, 16):
        am[:, e + 3, :] = _mult(128 * e + ii - jj)
    c["c_amask"] = am.astype(bf)
    j = np.arange(128)[:, None]
    i = np.arange(128)[None, :]
    c["c_uinc"] = np.where(j <= i, -1.0 / 16.0, 0.0).astype(np.float32)
    c["c_uaft"] = np.where(j > i, -1.0 / 16.0, 0.0).astype(np.float32)
    c["c_caus"] = (j <= i).astype(np.float32).astype(bf)
    c["c_ones"] = np.ones((1, 128), np.float32)
    sm = np.zeros((128, NSS * 16, NST), np.float32)
    tt = np.arange(TS)[None, :]
    for s in range(NSS):
        for jb in range(16):
            sm[:, s * 16 + jb, s * TS:(s + 1) * TS] = _mult(SEQ + tt - (128 * jb + jj))
    c["c_smask"] = sm.astype(bf)
    tok = np.arange(NST)
    same = (tok[:, None] // TS) == (tok[None, :] // TS)
    dl = tok[None, :] - tok[:, None]
    c["c_nmask"] = (np.where(same, _mult(dl), 0.0)).astype(np.float32).astype(bf)
    c["c_suinc"] = np.where(same & (tok[:, None] <= tok[None, :]), -1.0 / 16.0, 0.0).astype(np.float32)
    c["c_suaft"] = np.where(same & (tok[:, None] > tok[None, :]), -1.0 / 16.0, 0.0).astype(np.float32)
    c["c_scaus"] = (same & (tok[:, None] <= tok[None, :])).astype(np.float32).astype(bf)
    ssel = np.zeros((128, NSS, NST), np.float32)
    srow = np.zeros((NST, NSS), np.float32)
    for s in range(NSS):
        ssel[:, s, s * TS:(s + 1) * TS] = 1.0
        srow[s * TS:(s + 1) * TS, s] = 1.0
    c["c_ssel"] = ssel
    c["c_srow"] = srow
    c["c_id8"] = np.eye(8, dtype=np.float32)
    c["c_id32"] = np.eye(32, dtype=np.float32)
    return c


_NC_CACHE = {}


def kernel(x_prompt, x_sample, cache_k_win, cache_v_win, state_gla, state_ffn_conv,
           w_in, w_a2, b_a, g_gla_norm, w_o, g_pre_mix, g_post_mix, g_pre_ffn, g_post_ffn,
           w_up, conv_w, conv_b, w_down):
    f = lambda a: np.ascontiguousarray(np.asarray(a, dtype=np.float32))
    if "nc" not in _NC_CACHE:
        _NC_CACHE["nc"] = build_program()
    nc = _NC_CACHE["nc"]
    consts = _constants()
    shared = {
        "w_in": f(w_in[0]), "w_a2": f(w_a2[0]), "b_a": f(b_a[0]).reshape(1, 256),
        "g_gla": f(g_gla_norm[0]).reshape(1, 512), "w_o": f(w_o[0]),
        "g_pre_mix": f(g_pre_mix[0]).reshape(1, D), "g_post_mix": f(g_post_mix[0]).reshape(1, D),
        "g_pre_ffn": f(g_pre_ffn[0]).reshape(1, D), "g_post_ffn": f(g_post_ffn[0]).reshape(1, D),
        "w_up": f(w_up[0]), "conv_w": f(conv_w[0]), "conv_b": f(conv_b[0]).reshape(1, DFF), "w_down": f(w_down[0]),
    }
    shared.update(consts)
    x_prompt = np.asarray(x_prompt); x_sample = np.asarray(x_sample)
    cache_k_win = np.asarray(cache_k_win); cache_v_win = np.asarray(cache_v_win)
    state_gla = np.asarray(state_gla); state_ffn_conv = np.asarray(state_ffn_conv)
    in_maps = []
    for c in range(NCORES):
        m = dict(shared)
        m["xp"] = f(x_prompt[NSEQ * c:NSEQ * (c + 1)])
        m["xs"] = f(x_sample[NSS * c:NSS * (c + 1)]).reshape(NST, D)
        m["ck"] = f(cache_k_win[0, NSS * c:NSS * (c + 1)]).reshape(NSS, SEQ, 512)
        m["cv"] = f(cache_v_win[0, NSS * c:NSS * (c + 1)]).reshape(NSS, SEQ, 512)
        m["sg"] = f(state_gla[0, NSS * c:NSS * (c + 1)])
        m["sc"] = f(state_ffn_conv[0, NSS * c:NSS * (c + 1)]).reshape(NSS * 2, DFF)
        in_maps.append(m)
    res = run_bass_kernel_spmd(nc, in_maps, core_ids=list(range(NCORES)))
    R = res.results
    cat = lambda k: np.concatenate([np.asarray(r[k], dtype=np.float32) for r in R], axis=0)
    y_prompt = cat("y_p")
    y_sample = cat("y_s").reshape(32, TS, D)
    wk_p = cat("wk_p").reshape(1, 16, SEQ, HA, 64)
    wv_p = cat("wv_p").reshape(1, 16, SEQ, HA, 64)
    gs_p = cat("gs_p").reshape(1, 16, 4, 64, 128)
    fc_p = cat("fc_p").reshape(1, 16, 2, DFF)
    wk_s = cat("wk_s").reshape(1, 32, TS, HA, 64)
    wv_s = cat("wv_s").reshape(1, 32, TS, HA, 64)
    gs_s = cat("gs_s").reshape(1, 32, 4, 64, 128)
    fc_s = cat("fc_s").reshape(1, 32, 2, DFF)
    return (y_prompt, y_sample, wk_p, wv_p, gs_p, fc_p, wk_s, wv_s, gs_s, fc_s)
```

```python
import numpy as np
import ml_dtypes
from contextlib import ExitStack
import concourse.bass as bass
import concourse.mybir as mybir
from concourse.bass_utils import run_bass_kernel_spmd

F32 = mybir.dt.float32
BF16 = mybir.dt.bfloat16
AF = mybir.ActivationFunctionType
ALU = mybir.AluOpType
AX = mybir.AxisListType

NCORES = 8
D = 1024
SEQ = 2048
NSEQ = 2
NSS = 4
TS = 8
NST = NSS * TS
HA = 8
DFF = 2816
NFC = DFF // 128
DIN = 3088
EPS = 1e-6
G = 512
DBG = {"nseq": NSEQ, "ngrp": 4, "phaseB": True, "stage": 9, "gla": True, "gs": 9, "gs2": 9, "sample": True, "npump": 6, "perblk": 0, "look": 1, "glacopy": "dve"}
QA, KA, VA, QB, KB, VB, RB, AL = 0, 512, 1024, 1536, 1792, 2048, 2560, 3072


class Buf:
    __slots__ = ("name", "w", "r", "dsem", "dcnt", "excl")

    def __init__(self, name):
        self.name = name
        self.excl = False
        self.w = None
        self.r = []
        self.dsem = None
        self.dcnt = 0


class Eng:
    def __init__(self, name, h, sem):
        self.name, self.h, self.sem = name, h, sem
        self.count = 0
        self.waited = {}


class Sched:
    def __init__(self, nc, es):
        self.nc, self.es = nc, es
        mk = lambda n, h: Eng(n, h, es.enter_context(nc.semaphore("s_" + n)))
        self.pe = mk("pe", nc.tensor)
        self.act = mk("act", nc.scalar)
        self.dve = mk("dve", nc.vector)
        self.pool = mk("pool", nc.gpsimd)
        self.sp = mk("sp", nc.sync)
        self.semcur = {}
        self.out_tickets = {}
        self.nsem = 5

    def _wait(self, eng, tk, same_ok):
        if tk is None:
            return
        sem, val, en = tk
        if en == eng.name and eng.name == "pe":
            return
        if en == "dma":
            val = max(val, self.semcur[id(sem)][1])
        if eng.waited.get(id(sem), 0) >= val:
            return
        eng.waited[id(sem)] = val
        eng.h.wait_ge(sem, val)

    def _deps(self, eng, reads, writes):
        for b in reads:
            b = b.buf if hasattr(b, "buf") else b
            self._wait(eng, b.w, True)
            if b.excl:
                for tk in b.r:
                    if tk[2] != eng.name:
                        self._wait(eng, tk, True)
        for b in writes:
            b = b.buf if hasattr(b, "buf") else b
            self._wait(eng, b.w, False)
            for tk in b.r:
                self._wait(eng, tk, False)

    def _commit(self, tk, reads, writes):
        for b in reads:
            b = b.buf if hasattr(b, "buf") else b
            b.r.append(tk)
        for b in writes:
            b = b.buf if hasattr(b, "buf") else b
            b.w = tk
            b.r = []

    def op(self, eng, fn, reads=(), writes=(), inc=True):
        self._deps(eng, reads, writes)
        ins = fn()
        if inc:
            eng.count += 1
            ins.then_inc(eng.sem, 1)
            tk = (eng.sem, eng.count, eng.name)
        else:
            tk = (eng.sem, eng.count + 1, eng.name)
        self._commit(tk, reads, writes)
        return tk

    def barrier(self):
        engs = [self.pe, self.act, self.dve, self.pool, self.sp]
        for e in engs:
            for o in engs:
                if o is e or o.count == 0:
                    continue
                if e.waited.get(id(o.sem), 0) < o.count:
                    e.waited[id(o.sem)] = o.count
                    e.h.wait_ge(o.sem, o.count)
            for sem, val in self.semcur.values():
                if e.waited.get(id(sem), 0) < val:
                    e.waited[id(sem)] = val
                    e.h.wait_ge(sem, val)

    def dma(self, eng, out, in_, reads, writes, sembuf=None, **kw):
        sb = sembuf if sembuf is not None else writes[0]
        sb = sb.buf if hasattr(sb, "buf") else sb
        if sb.dsem is None:
            sb.dsem = self.es.enter_context(self.nc.semaphore("d_" + sb.name))
            self.nsem += 1
        saved = []
        for b in writes:
            b = b.buf if hasattr(b, "buf") else b
            if b.w is not None and b.w[2] == "dma" and b.w[0] is sb.dsem:
                saved.append((b, b.w))
                b.w = None
        self._deps(eng, reads, writes)
        for b, w in saved:
            b.w = w
        sb.dcnt += 16
        eng.h.dma_start(out=out, in_=in_, **kw).then_inc(sb.dsem, 16)
        self.semcur[id(sb.dsem)] = (sb.dsem, sb.dcnt)
        tk = (sb.dsem, sb.dcnt, "dma")
        self._commit(tk, reads, writes)
        return tk


class T:
    def __init__(self, S, es, name, shape, dtype=F32, psum=False):
        if psum:
            self.t = es.enter_context(S.nc.psum_tensor(name, shape, dtype))
        else:
            self.t = es.enter_context(S.nc.sbuf_tensor(name, shape, dtype))
        self.buf = Buf(name)
        self.buf.excl = psum
        self.shape = shape

    def __getitem__(self, idx):
        return self.t[idx]


def build_program():
    nc = bass.Bass("TRN2", target_bir_lowering=False)
    dt_in = lambda n, s, d=F32: nc.dram_tensor(n, s, d, kind="ExternalInput").ap()
    dt_out = lambda n, s: nc.dram_tensor(n, s, F32, kind="ExternalOutput").ap()
    xp = dt_in("xp", [NSEQ, SEQ, D])
    xs = dt_in("xs", [NST, D])
    ck = dt_in("ck", [NSS, SEQ, 512])
    cv = dt_in("cv", [NSS, SEQ, 512])
    sg = dt_in("sg", [NSS, 4, 64, 128])
    sc = dt_in("sc", [NSS * 2, DFF])
    w_in = dt_in("w_in", [D, DIN])
    w_a2 = dt_in("w_a2", [16, 256])
    b_a = dt_in("b_a", [1, 256])
    g_gla = dt_in("g_gla", [1, 512])
    w_o = dt_in("w_o", [D, D])
    g_pre_mix = dt_in("g_pre_mix", [1, D])
    g_post_mix = dt_in("g_post_mix", [1, D])
    g_pre_ffn = dt_in("g_pre_ffn", [1, D])
    g_post_ffn = dt_in("g_post_ffn", [1, D])
    w_up = dt_in("w_up", [D, 2 * DFF])
    conv_w = dt_in("conv_w", [3, DFF])
    conv_b = dt_in("conv_b", [1, DFF])
    w_down = dt_in("w_down", [DFF, D])
    c_ident = dt_in("c_ident", [128, 128], BF16)
    c_amask = dt_in("c_amask", [128, 19, 128], BF16)
    c_uinc = dt_in("c_uinc", [128, 128])
    c_uaft = dt_in("c_uaft", [128, 128])
    c_caus = dt_in("c_caus", [128, 128], BF16)
    c_ones = dt_in("c_ones", [1, 128])
    c_smask = dt_in("c_smask", [128, NSS * 16, NST], BF16)
    c_nmask = dt_in("c_nmask", [NST, NST], BF16)
    c_suinc = dt_in("c_suinc", [NST, NST])
    c_suaft = dt_in("c_suaft", [NST, NST])
    c_scaus = dt_in("c_scaus", [NST, NST], BF16)
    c_ssel = dt_in("c_ssel", [128, NSS, NST])
    c_srow = dt_in("c_srow", [NST, NSS])
    c_id8 = dt_in("c_id8", [8, 8])
    c_id32 = dt_in("c_id32", [32, 32])
    y_p = dt_out("y_p", [NSEQ, SEQ, D])
    y_s = dt_out("y_s", [NST, D])
    wk_p = dt_out("wk_p", [NSEQ, SEQ, 512])
    wv_p = dt_out("wv_p", [NSEQ, SEQ, 512])
    gs_p = dt_out("gs_p", [NSEQ, 4, 64, 128])
    fc_p = dt_out("fc_p", [NSEQ, 2, DFF])
    wk_s = dt_out("wk_s", [NST, 512])
    wv_s = dt_out("wv_s", [NST, 512])
    gs_s = dt_out("gs_s", [NSS, 4, 64, 128])
    fc_s = dt_out("fc_s", [NSS, 2, DFF])
    x1_p = nc.dram_tensor("x1_p", [NSEQ, SEQ, D], F32, kind="Internal").ap()
    x1_s = nc.dram_tensor("x1_s", [NST, D], F32, kind="Internal").ap()

    with ExitStack() as es0:
        S = Sched(nc, es0)
        PE, ACT, DVE, POOL, SP = S.pe, S.act, S.dve, S.pool, S.sp
        V, A, P, GP = nc.vector, nc.scalar, nc.tensor, nc.gpsimd
        out_bufs = []

        def store(eng, dst, src_ap, src_t):
            ob = getattr(src_t, "obuf", None)
            if ob is None:
                ob = Buf(src_t.buf.name + "_o")
                src_t.obuf = ob
            S.dma(eng, dst, src_ap, reads=[src_t], writes=[], sembuf=ob)
            if ob not in out_bufs:
                out_bufs.append(ob)

        ident = T(S, es0, "ident", [128, 128], BF16)
        S.dma(SP, ident[:], c_ident[:, :], [], [ident])
        ones_r = T(S, es0, "ones_r", [1, 128])
        S.dma(SP, ones_r[:], c_ones[:, :], [], [ones_r])

        ps = [T(S, es0, "ps%d" % i, [128, 512], F32, psum=True) for i in range(8)]

        class _View:
            def __init__(self, t):
                self.buf = t.buf
                self.v = t[:].bitcast(BF16).rearrange("p (c i) -> p c i", c=8)

            def __getitem__(self, idx):
                return self.v[idx]
        ptp = [_View(ps[6]), _View(ps[7]), _View(ps[0]), _View(ps[1])]

        def rms_rstd(ss_ap, out_ap, n, tmp_ap, ss_t, out_t, tmp_t, nfeat):
            S.op(ACT, lambda: A.activation(out=tmp_ap, in_=ss_ap, func=AF.Ln, scale=1.0 / nfeat, bias=epsb[0:n, 0:1]),
                 [ss_t, epsb], [tmp_t])
            S.op(ACT, lambda: A.activation(out=out_ap, in_=tmp_ap, func=AF.Exp, scale=-0.5), [tmp_t], [out_t])

        epsb = T(S, es0, "epsb", [128, 1])
        S.op(DVE, lambda: V.memset(epsb[:], EPS), [], [epsb])
        oneb = T(S, es0, "oneb", [128, 1])
        S.op(DVE, lambda: V.memset(oneb[:], 1.0), [], [oneb])

        def transpose_tile(src_t, src_ap_fn, np_, dst_t, dst_ap, nchunk, slot):
            pt = ptp[slot]
            for c in range(nchunk):
                S.op(PE, lambda c=c: P.transpose(pt[:, c, 0:np_], src_ap_fn(c), ident[0:np_, 0:np_]),
                     [src_t, ident], [pt], inc=(c == nchunk - 1))
            S.op(ACT, lambda: A.activation(out=dst_ap, in_=pt[:, 0:nchunk, 0:np_], func=AF.Copy), [pt], [dst_t])

        with ExitStack() as es:
            win = T(S, es, "win", [128, 8, DIN], BF16)
            wo = T(S, es, "wo", [128, 8, D], BF16)
            class _Blk:
                def __init__(self, name):
                    self.buf = Buf(name)
            w_in3 = w_in.rearrange("(kc p) n -> p kc n", p=128)
            win_blocks = [(AL, DIN), (QB, VB), (VB, AL), (QA, KA), (KA, VA), (VA, QB)]
            win_bufs = []
            for bi, (c0, c1) in enumerate(win_blocks):
                wb = _Blk("winb%d" % bi)
                win_bufs.append(wb)
                for k0 in range(0, 8, 4):
                    S.dma(POOL, win[:, k0:k0 + 4, c0:c1], w_in3[:, k0:k0 + 4, c0:c1], [], [wb])

            def winb(col0):
                for (c0, c1), wb in zip(win_blocks, win_bufs):
                    if c0 <= col0 < c1:
                        return wb
            for kc in range(8):
                S.dma(POOL, wo[:, kc, :], w_o[kc * 128:(kc + 1) * 128, :], [], [wo])
            wa2 = T(S, es, "wa2", [16, 256])
            S.dma(SP, wa2[:], w_a2[:, :], [], [wa2])
            bar = T(S, es, "bar", [1, 256])
            S.dma(SP, bar[:], b_a[:, :], [], [bar])
            gpre = T(S, es, "gpre", [128, D])
            S.dma(SP, gpre[:], g_pre_mix[0:1, :].partition_broadcast(128), [], [gpre])
            gpost = T(S, es, "gpost", [128, D])
            S.dma(SP, gpost[:], g_post_mix[0:1, :].partition_broadcast(128), [], [gpost])
            ggla = T(S, es, "ggla", [128, 512])
            S.dma(SP, ggla[:], g_gla[0:1, :].partition_broadcast(128), [], [ggla])

            xs1 = [T(S, es, "xs1a_%d" % i, [128, D]) for i in range(2)]
            xr = [T(S, es, "xra_%d" % i, [128, D]) for i in range(2)]
            f_ss = [T(S, es, "fss%d" % i, [128, 1]) for i in range(4)]
            f_ln = [T(S, es, "fln%d" % i, [128, 1]) for i in range(4)]
            f_rs = [T(S, es, "frs%d" % i, [128, 1]) for i in range(4)]
            ssq = T(S, es, "ssq", [128, 8])
            lnt = T(S, es, "lnt", [128, 8])
            rstd = T(S, es, "rstd", [128, 8])
            xnb4 = [T(S, es, "xnb%d" % i, [128, D], BF16) for i in range(4)]
            xnb = xnb4[0]
            xnT = T(S, es, "xnT", [128, 8, G], BF16)
            qTa = T(S, es, "qTa", [128, 4, G], BF16)
            qTb = T(S, es, "qTb", [128, 2, G], BF16)
            kTb = T(S, es, "kTb", [128, 2, G], BF16)
            alT = T(S, es, "alT", [16, G])
            ktok = [T(S, es, "ktok%d" % i, [128, 512]) for i in range(2)]
            vtok = [T(S, es, "vtok%d" % i, [128, 512]) for i in range(1)] * 2
            kbt = T(S, es, "kbt", [128, 4, 256], BF16)
            vbt = T(S, es, "vbt", [128, 4, 512], BF16)
            rbt = T(S, es, "rbt", [128, 4, 512], BF16)
            mixed = T(S, es, "mixed", [128, 4, D], BF16)
            mixT = xnT
            pT = [T(S, es, "pT%d" % i, [128, 512], BF16) for i in range(4)]
            rden = [T(S, es, "rden%d" % i, [128, 4]) for i in range(2)]
            gtmp = T(S, es, "gtmp", [128, 512])
            lzs = [T(S, es, "lz%d" % i, [128, 256]) for i in range(2)]
            ebTs = [T(S, es, "ebT%d" % i, [128, 2, 128]) for i in range(2)]
            enbTs = [T(S, es, "enbT%d" % i, [128, 2, 128]) for i in range(2)]
            qeTds = [T(S, es, "qeTd%d" % i, [128, 2, 2, 128], BF16) for i in range(2)]
            keTs = [T(S, es, "keT%d" % i, [128, 2, 128], BF16) for i in range(2)]
            klbs = [T(S, es, "kl%d" % i, [128, 256], BF16) for i in range(2)]
            attms = [T(S, es, "attm%d" % i, [128, 4, 128], BF16) for i in range(2)]
            osss = [T(S, es, "oss%d" % i, [128, 4]) for i in range(2)]
            olts = [T(S, es, "olt%d" % i, [128, 4]) for i in range(2)]
            orstds = [T(S, es, "orstd%d" % i, [128, 4]) for i in range(2)]
            osqs = [gtmp, T(S, es, "osq1", [128, 512])]
            for i in range(2):
                S.op(POOL, lambda i=i: GP.memset(qeTds[i][:], 0.0), [], [qeTds[i]])
            ytmp = T(S, es, "ytmp", [128, D])

            class _JunkA:
                buf = ytmp.buf

                def __getitem__(self, idx):
                    return ytmp[:].bitcast(BF16)[:, 0:D][idx]
            junk = _JunkA()

            def front_a(x_src, np_, t):
                xs_, xb_ = xs1[t % 2], xnb4[t]
                S.dma(SP, xs_[0:np_, :], x_src, [], [xs_])
                S.op(ACT, lambda: A.activation(out=xb_[0:np_, :], in_=xs_[0:np_, :], func=AF.Square,
                                               accum_out=f_ss[t][0:np_, 0:1]), [xs_], [xb_, f_ss[t]])
                rms_rstd(f_ss[t][0:np_, 0:1], f_rs[t][0:np_, 0:1], np_, f_ln[t][0:np_, 0:1], f_ss[t], f_rs[t], f_ln[t], D)
                S.op(DVE, lambda: V.scalar_tensor_tensor(out=xb_[0:np_, :], in0=xs_[0:np_, :], scalar=f_rs[t][0:np_, 0:1],
                                                         in1=gpre[0:np_, :], op0=ALU.mult, op1=ALU.mult),
                     [xs_, f_rs[t], gpre], [xb_])

            def front_b(np_, t):
                xb_ = xnb4[t]
                transpose_tile(xb_, lambda c: xb_[0:np_, c * 128:(c + 1) * 128], np_, xnT,
                               xnT[:, :, t * np_:(t + 1) * np_], 8, t % 2)

            def mixer_group(nt, np_, x_src_fn, s_idx, g_idx, first_in_seq, k_dst_fn, v_dst_fn, x1_dst_fn,
                            sample=False, front_done=False, nxt=None):
                ntok = nt * np_
                if not front_done:
                    for t in range(nt):
                        front_a(x_src_fn(t), np_, t)
                for t in range(nt):
                    front_b(np_, t)

                if DBG["stage"] < 2:
                    return
                def fproj(col0, ncols, dst_t, dst_ap, bank):
                    for kc in range(8):
                        S.op(PE, lambda kc=kc: P.matmul(ps[bank][0:ncols, 0:ntok], lhsT=win[:, kc, col0:col0 + ncols],
                                                        rhs=xnT[:, kc, 0:ntok], start=(kc == 0), stop=(kc == 7)),
                             [winb(col0), xnT], [ps[bank]], inc=(kc == 7))
                    S.op(ACT, lambda: A.activation(out=dst_ap, in_=ps[bank][0:ncols, 0:ntok], func=AF.Copy),
                         [ps[bank]], [dst_t])

                if DBG["stage"] < 3:
                    return
                def tproj(t, col0, ncols, bank, evac):
                    for kc in range(8):
                        S.op(PE, lambda kc=kc: P.matmul(ps[bank][0:np_, 0:ncols], lhsT=xnT[:, kc, t * np_:(t + 1) * np_],
                                                        rhs=win[:, kc, col0:col0 + ncols], start=(kc == 0), stop=(kc == 7)),
                             [winb(col0), xnT], [ps[bank]], inc=(kc == 7))
                    evac(ps[bank])

                evs = []
                for t in range(nt):
                    jb = (g_idx * 4 + t) if not sample else 0
                    kt, vt = ktok[t % 2], vtok[t % 2]

                    def ev_k(pb, kt=kt, t=t):
                        S.op(ACT, lambda: A.activation(out=kt[0:np_, :], in_=pb[0:np_, 0:512], func=AF.Copy), [pb], [kt])
                        store(POOL, k_dst_fn(t), kt[0:np_, :], kt)

                    def ev_v(pb, vt=vt, t=t, jb=jb):
                        S.op(ACT, lambda: A.activation(out=vt[0:np_, :], in_=pb[0:np_, 0:512], func=AF.Copy), [pb], [vt])
                        S.op(DVE, lambda: V.tensor_copy(out=vaug_cur[0:np_, jb, :, 0:64],
                                                        in_=pb[0:np_, 0:512].rearrange("p (h e) -> p h e", h=HA)),
                             [pb], [vaug_cur])
                        store(POOL, v_dst_fn(t), vt[0:np_, :], vt)

                    def ev_kb(pb, t=t):
                        S.op(ACT, lambda: A.activation(out=kbt[0:np_, t, :], in_=pb[0:np_, 0:256], func=AF.Copy), [pb], [kbt])

                    def ev_vb(pb, t=t):
                        S.op(DVE, lambda: V.tensor_copy(out=vbt[0:np_, t, :], in_=pb[0:np_, 0:512]), [pb], [vbt])

                    def ev_rb(pb, t=t):
                        S.op(ACT, lambda: A.activation(out=gtmp[0:np_, :], in_=pb[0:np_, 0:512], func=AF.Exp, scale=-1.0), [pb], [gtmp])
                        S.op(DVE, lambda: V.tensor_tensor(out=rbt[0:np_, t, :], in0=pb[0:np_, 0:512], in1=ggla[0:np_, :], op=ALU.mult),
                             [pb, ggla], [rbt])
                        S.op(DVE, lambda: V.tensor_scalar_add(out=gtmp[0:np_, :], in0=gtmp[0:np_, :], scalar1=1.0), [gtmp], [gtmp])
                        S.op(DVE, lambda: V.reciprocal(out=gtmp[0:np_, :], in_=gtmp[0:np_, :]), [gtmp], [gtmp])
                        S.op(DVE, lambda: V.tensor_tensor(out=rbt[0:np_, t, :], in0=rbt[0:np_, t, :], in1=gtmp[0:np_, :], op=ALU.mult),
                             [rbt, gtmp], [rbt])

                    evs.append((ev_k, ev_v, ev_kb, ev_vb, ev_rb))

                bk = 0
                fproj(AL, 16, alT, alT[:, 0:ntok], bk); bk ^= 1
                for c in range(2):
                    fproj(QB + c * 128, 128, qTb, qTb[:, c, 0:ntok], bk); bk ^= 1
                for c in range(2):
                    fproj(KB + c * 128, 128, kTb, kTb[:, c, 0:ntok], bk); bk ^= 1
                for t in range(nt):
                    ev_k, ev_v, ev_kb, ev_vb, ev_rb = evs[t]
                    tproj(t, KB, 256, bk, ev_kb); bk ^= 1
                    tproj(t, VB, 512, bk, ev_vb); bk ^= 1
                    tproj(t, RB, 512, bk, ev_rb); bk ^= 1

                def gla_all():
                    gens = [gla_chunk(t, np_, sample) for t in range(nt)]
                    state_done = [False] * nt
                    waiting = [False] * nt
                    started = min(2, nt)
                    active = list(range(started))
                    while active:
                        for t in list(active):
                            if waiting[t]:
                                if t > 0 and not state_done[t - 1]:
                                    continue
                                waiting[t] = False
                            try:
                                r = next(gens[t])
                            except StopIteration:
                                active.remove(t)
                                if started < nt:
                                    active.append(started)
                                    started += 1
                                continue
                            if r == "need_state":
                                waiting[t] = True
                            elif r == "state_done":
                                state_done[t] = True
                            yield
                gla = gla_all()

                def pump(n):
                    for _ in range(n):
                        if next(gla, "done") == "done":
                            return

                npump = 0 if sample else DBG["npump"]
                for c in range(4):
                    fproj(QA + c * 128, 128, qTa, qTa[:, c, 0:ntok], bk); bk ^= 1
                    pump(npump)
                kcol0 = (g_idx * G) if not sample else 0
                for c in range(4):
                    fproj(KA + c * 128, 128, kTa_cur, kTa_cur[:, c, kcol0:kcol0 + ntok], bk); bk ^= 1
                    pump(npump)
                for t in range(nt):
                    ev_k, ev_v, ev_kb, ev_vb, ev_rb = evs[t]
                    tproj(t, KA, 512, bk, ev_k); bk ^= 1
                    pump(npump)
                    tproj(t, VA, 512, bk, ev_v); bk ^= 1
                    pump(npump)

                if not sample:
                    attention_prompt(g_idx, gla, nxt)
                else:
                    attention_sample(gla)
                    for _ in gla:
                        pass
                if DBG["stage"] < 5:
                    return
                for t in range(nt):
                    transpose_tile(mixed, lambda c, t=t: mixed[0:np_, t, c * 128:(c + 1) * 128], np_, mixT,
                                   mixT[:, :, t * np_:(t + 1) * np_], 8, t % 2)

                def wo_mm(t):
                    S.dma(SP, xr[t % 2][0:np_, :], x_src_fn(t), [], [xr[t % 2]])
                    for hf in range(2):
                        pw = ps[(2 * t + hf) % 6]
                        for kc in range(8):
                            S.op(PE, lambda kc=kc, hf=hf, pw=pw: P.matmul(pw[0:np_, :], lhsT=mixT[:, kc, t * np_:(t + 1) * np_],
                                                                          rhs=wo[:, kc, hf * 512:(hf + 1) * 512],
                                                                          start=(kc == 0), stop=(kc == 7)),
                                 [wo, mixT], [pw], inc=(kc == 7))

                def wo_epi(t):
                    for hf in range(2):
                        pw = ps[(2 * t + hf) % 6]
                        S.op(ACT, lambda hf=hf, pw=pw: A.activation(out=junk[0:np_, hf * 512:(hf + 1) * 512], in_=pw[0:np_, :],
                                                                    func=AF.Square, accum_out=ssq[0:np_, 4 + hf:5 + hf]),
                             [pw], [junk, ssq])
                    S.op(DVE, lambda: V.tensor_tensor(out=ssq[0:np_, 6:7], in0=ssq[0:np_, 4:5], in1=ssq[0:np_, 5:6], op=ALU.add),
                         [ssq], [ssq])
                    rms_rstd(ssq[0:np_, 6:7], rstd[0:np_, 6:7], np_, lnt[0:np_, 6:7], ssq, rstd, lnt, D)
                    for hf in range(2):
                        pw = ps[(2 * t + hf) % 6]
                        S.op(DVE, lambda hf=hf, pw=pw: V.scalar_tensor_tensor(out=ytmp[0:np_, hf * 512:(hf + 1) * 512], in0=pw[0:np_, :],
                                                                              scalar=rstd[0:np_, 6:7],
                                                                              in1=gpost[0:np_, hf * 512:(hf + 1) * 512],
                                                                              op0=ALU.mult, op1=ALU.mult),
                             [pw, rstd, gpost], [ytmp])
                    xr_ = xr[t % 2]
                    S.op(DVE, lambda: V.tensor_tensor(out=xr_[0:np_, :], in0=ytmp[0:np_, :], in1=xr_[0:np_, :], op=ALU.add),
                         [ytmp, xr_], [xr_])
                    S.dma(SP, x1_dst_fn(t), xr_[0:np_, :], [xr_], [x1buf], sembuf=xgo[t % 2])

                wo_mm(0)
                for t in range(nt):
                    if t + 1 < nt:
                        wo_mm(t + 1)
                    wo_epi(t)

            def attention_prompt(g_idx, gla, nxt=None):
                nblk = 4 * g_idx + 4
                blocks = [(pr, jb) for pr in range(4) for jb in range(nblk)]
                stb = [ps[4], ps[5], ps[0], ps[1]]
                n_gla = 4 * 52 - 48
                per_blk = DBG["perblk"] or -(-n_gla // len(blocks))

                def geo(jb):
                    d = 4 * g_idx - jb
                    return d, 128 * max(0, -d)

                def A_(b):
                    pr, jb = blocks[b]
                    d, q0 = geo(jb)
                    for e in range(2):
                        st = stb[2 * (b % 2) + e]
                        r0 = 64 * e
                        S.op(PE, lambda st=st, r0=r0: P.matmul(st[:, q0:G], lhsT=kTa[r0:r0 + 64, pr, jb * 128:(jb + 1) * 128],
                                                               rhs=qTa[r0:r0 + 64, pr, q0:G], start=True, stop=True),
                             [kTa, qTa], [st])

                def B_(b):
                    pr, jb = blocks[b]
                    d, q0 = geo(jb)
                    for e in range(2):
                        st = stb[2 * (b % 2) + e]
                        pt = pT[2 * (b % 2) + e]
                        S.op(ACT, lambda st=st, pt=pt: A.activation(out=pt[:, q0:G], in_=st[:, q0:G], func=AF.Exp, scale=0.125),
                             [st], [pt])
                    for e in range(2):
                        pt = pT[2 * (b % 2) + e]
                        S.op(DVE, lambda pt=pt: V.tensor_tensor(
                            out=pt[:, q0:G], in0=pt[:, q0:G],
                            in1=amask[:, d + 3:d + 7, :].rearrange("p a b -> p (a b)")[:, q0:G], op=ALU.mult), [pt, amask], [pt])

                def C_(b):
                    pr, jb = blocks[b]
                    d, q0 = geo(jb)
                    ibs = list(range(max(0, -d), 4))
                    for e in range(2):
                        h = 2 * pr + e
                        oacc = ps[2 + e]
                        pt = pT[2 * (b % 2) + e]
                        for ib in ibs:
                            S.op(PE, lambda ib=ib, oacc=oacc, pt=pt, h=h: P.matmul(
                                oacc[:, ib * 65:(ib + 1) * 65], lhsT=pt[:, ib * 128:(ib + 1) * 128], rhs=vaug[:, jb, h, :],
                                start=(jb == 0 and ib == ibs[0]), stop=(jb == 4 * g_idx + ib), skip_group_check=True),
                                 [pt, vaug], [oacc], inc=(ib == ibs[-1]))
                    if jb == nblk - 1:
                        for e in range(2):
                            h = 2 * pr + e
                            oacc = ps[2 + e]
                            o3 = oacc[:, 0:260].rearrange("p (i e) -> p i e", i=4)
                            rd = rden[e]
                            S.op(DVE, lambda o3=o3, rd=rd: V.reciprocal(out=rd[:, :], in_=o3[:, :, 64]), [oacc], [rd])
                            S.op(DVE, lambda o3=o3, rd=rd, h=h: V.tensor_tensor(
                                out=mixed[:, :, h * 64:(h + 1) * 64], in0=o3[:, :, 0:64],
                                in1=rd[:, :].unsqueeze(2).broadcast_to([128, 4, 64]), op=ALU.mult), [oacc, rd], [mixed])

                nbk = len(blocks)
                fpos = {(nbk * (2 * t + 1)) // 9: t for t in range(4)} if nxt is not None else {}
                A_(0)
                for b in range(nbk):
                    if b + 1 < nbk:
                        A_(b + 1)
                    B_(b)
                    C_(b)
                    if b in fpos:
                        front_a(nxt(fpos[b]), 128, fpos[b])
                    for _ in range(per_blk):
                        if next(gla, "done") == "done":
                            break
                for _ in gla:
                    pass

            def attention_sample(gla):
                oaccs = [ps[2], ps[3]]
                sts = [ps[4], ps[5]]
                first = [True, True]

                def scores(nk, kT_t, kT_fn, mask_t, mask_ap, slot):
                    for h in range(HA):
                        c, r0 = h // 2, 64 * (h % 2)
                        st = sts[h % 2]
                        S.op(PE, lambda c=c, r0=r0, st=st: P.matmul(st[0:nk, c * NST:(c + 1) * NST], lhsT=kT_fn(c, r0),
                                                                    rhs=qTa[r0:r0 + 64, c, 0:NST], start=True, stop=True),
                             [kT_t, qTa], [st], inc=(h >= HA - 2))
                    pt = pTs[slot]
                    for hh in range(2):
                        S.op(ACT, lambda hh=hh: A.activation(
                            out=pt[0:nk, :, :].rearrange("p (c hh) t -> p c hh t", hh=2)[:, :, hh, :],
                            in_=sts[hh][0:nk, 0:4 * NST].rearrange("p (c t) -> p c t", c=4), func=AF.Exp, scale=0.125),
                             [sts[hh]], [pt])
                    S.op(DVE, lambda: V.tensor_tensor(out=pt[0:nk, :, :], in0=pt[0:nk, :, :],
                                                      in1=mask_ap.unsqueeze(1).broadcast_to([nk, HA, NST]), op=ALU.mult),
                         [pt, mask_t], [pt])

                def pv(nk, v_t, v_fn, slot, is_last):
                    pt = pTs[slot]
                    for h in range(HA):
                        oa = oaccs[h // 4]
                        col = (h % 4) * 65
                        S.op(PE, lambda h=h, oa=oa, col=col, fl=first[h // 4]: P.matmul(
                            oa[0:NST, col:col + 65], lhsT=pt[0:nk, h, :], rhs=v_fn(h), start=fl, stop=is_last,
                            skip_group_check=True), [pt, v_t], [oa], inc=(h % 4 == 3))
                        first[h // 4] = False

                S.op(POOL, lambda: GP.memset(vaugc[0][:, :, 64:65], 1.0), [], [vaugc[0]])
                S.op(POOL, lambda: GP.memset(vaugc[1][:, :, 64:65], 1.0), [], [vaugc[1]])
                cblocks = [(sq, jb) for sq in range(NSS) for jb in range(16)]

                def load_d(m):
                    sq, jb = cblocks[2 * m]
                    sl2 = m % 2
                    S.dma(POOL, kcb[sl2][:], ck[sq, jb * 128:(jb + 2) * 128, :].rearrange("(j p) f -> p j f", p=128), [], [kcb[sl2]])
                    S.dma(POOL, vcb[sl2][:], cv[sq, jb * 128:(jb + 2) * 128, :].rearrange("(j p) f -> p j f", p=128), [], [vcb[sl2]])

                def prep(n):
                    sl2, j = (n // 2) % 2, n % 2
                    sl = n % 2
                    kb_, vb_, kt_, va_ = kcb[sl2], vcb[sl2], kTc[sl], vaugc[sl]
                    transpose_tile(kb_, lambda c, kb_=kb_, j=j: kb_[:, j, c * 128:(c + 1) * 128], 128, kt_, kt_[:, :, :], 4, 2 + sl)
                    S.op(DVE, lambda vb_=vb_, va_=va_, j=j: V.tensor_copy(out=va_[:, :, 0:64],
                                                                          in_=vb_[:, j, :].rearrange("p (h e) -> p h e", h=HA)),
                         [vb_], [va_])

                def sc_(n):
                    sq, jb = cblocks[n]
                    kt_ = kTc[n % 2]
                    scores(128, kt_, lambda c, r0, kt_=kt_: kt_[r0:r0 + 64, c, :], smask, smask[:, sq * 16 + jb, :], n % 2)

                def pv_(n):
                    va_ = vaugc[n % 2]
                    pv(128, va_, lambda h, va_=va_: va_[:, h, :], n % 2, False)

                nb_ = len(cblocks)
                nd_ = nb_ // 2
                load_d(0)
                load_d(1)
                prep(0)
                prep(1)
                sc_(0)
                for n in range(nb_):
                    if n + 1 < nb_:
                        sc_(n + 1)
                    pv_(n)
                    if n + 2 < nb_:
                        prep(n + 2)
                    if n % 2 == 1 and (n // 2) + 2 < nd_:
                        load_d(n // 2 + 2)
                    next(gla, None)
                    next(gla, None)
                scores(NST, kTa_cur, lambda c, r0: kTa_cur[r0:r0 + 64, c, 0:NST], nmask, nmask[:, :], 0)
                pv(NST, vaug_cur, lambda h: vaug_cur[0:NST, 0, h, :], 0, True)
                for half in range(2):
                    oa = oaccs[half]
                    o3 = oa[0:NST, 0:260].rearrange("p (i e) -> p i e", i=4)
                    S.op(DVE, lambda o3=o3: V.reciprocal(out=rden[0][0:NST, :], in_=o3[:, :, 64]), [oa], [rden[0]])
                    S.op(DVE, lambda o3=o3, half=half: V.tensor_tensor(
                        out=mixed[0:NST, 0, half * 256:(half + 1) * 256].rearrange("p (i e) -> p i e", i=4),
                        in0=o3[:, :, 0:64], in1=rden[0][0:NST, :].unsqueeze(2).broadcast_to([NST, 4, 64]), op=ALU.mult),
                         [oa, rden[0]], [mixed])

            def gla_chunk(t, np_, sample):
                sl = t % 2
                bank = ps[6 + sl]
                lz, ebT, enbT, qeTd, keT, kl, attm = lzs[sl], ebTs[sl], enbTs[sl], qeTds[sl], keTs[sl], klbs[sl], attms[sl]
                oss, olt, orstd, osq = osss[sl], olts[sl], orstds[sl], osqs[sl]
                ez = ekl = lz
                tok = slice(t * np_, (t + 1) * np_)
                U_inc, U_aft, CA = (suinc, suaft, scaus) if sample else (uinc, uaft, caus)
                pz = bank
                S.op(PE, lambda: P.matmul(pz[0:np_, 0:256], lhsT=alT[0:16, tok], rhs=wa2[0:16, :], start=True, stop=False),
                     [alT, wa2], [pz], inc=False)
                S.op(PE, lambda: P.matmul(pz[0:np_, 0:256], lhsT=ones_r[0:1, 0:np_], rhs=bar[0:1, :], start=False, stop=True),
                     [ones_r, bar], [pz])
                yield
                S.op(ACT, lambda: A.activation(out=ez[0:np_, :], in_=pz[0:np_, 0:256], func=AF.Exp, scale=-1.0), [pz], [ez])
                yield
                S.op(ACT, lambda: A.activation(out=lz[0:np_, :], in_=ez[0:np_, :], func=AF.Ln, bias=oneb[0:np_, 0:1]), [ez, oneb], [lz])
                yield
                pb = bank
                for kc in range(2):
                    S.op(PE, lambda kc=kc: P.matmul(pb[:, kc * 128:kc * 128 + np_], lhsT=lz[0:np_, kc * 128:(kc + 1) * 128],
                                                    rhs=U_inc[0:np_, 0:np_], start=True, stop=True),
                         [lz, U_inc], [pb], inc=False)
                pa = bank
                S.op(PE, lambda: P.matmul(pa[0:np_, 256:512], lhsT=U_aft[0:np_, 0:np_], rhs=lz[0:np_, :], start=True, stop=True),
                     [lz, U_aft], [pa])
                yield
                pb3 = pb[:, 0:256].rearrange("p (c i) -> p c i", c=2)[:, :, 0:np_]
                S.op(ACT, lambda: A.activation(out=ebT[:, :, 0:np_], in_=pb3, func=AF.Exp), [pb], [ebT])
                yield
                S.op(ACT, lambda: A.activation(out=enbT[:, :, 0:np_], in_=pb3, func=AF.Exp, scale=-1.0), [pb], [enbT])
                yield
                S.op(ACT, lambda: A.activation(out=ekl[0:np_, :], in_=pa[0:np_, 256:512], func=AF.Exp), [pa], [ekl])
                yield
                for hh in range(2):
                    r0 = 64 * hh
                    S.op(DVE, lambda hh=hh, r0=r0: V.scalar_tensor_tensor(
                        out=qeTd[r0:r0 + 64, :, hh, 0:np_], in0=qTb[r0:r0 + 64, :, tok], scalar=0.125,
                        in1=ebT[r0:r0 + 64, :, 0:np_], op0=ALU.mult, op1=ALU.mult), [qTb, ebT], [qeTd])
                    yield
                S.op(DVE, lambda: V.tensor_tensor(out=keT[:, :, 0:np_], in0=kTb[:, :, tok], in1=enbT[:, :, 0:np_], op=ALU.mult),
                     [kTb, enbT], [keT])
                yield
                S.op(DVE, lambda: V.tensor_tensor(out=kl[0:np_, :], in0=kbt[0:np_, t, :], in1=ekl[0:np_, :], op=ALU.mult),
                     [kbt, ekl], [kl])
                yield
                pat = bank
                if np_ == 128:
                    for c in range(2):
                        S.op(PE, lambda c=c: P.matmul(pat[:, c * 256:(c + 1) * 256], lhsT=keT[:, c, :],
                                                      rhs=qeTd[:, c, :, :].rearrange("p hh i -> p (hh i)"), start=True, stop=True),
                             [keT, qeTd], [pat], inc=(c == 1))
                else:
                    for h in range(4):
                        S.op(PE, lambda h=h: P.matmul(pat[0:np_, h * 128:h * 128 + np_], lhsT=keT[:, h // 2, 0:np_],
                                                      rhs=qeTd[:, h // 2, h % 2, 0:np_], start=True, stop=True),
                             [keT, qeTd], [pat], inc=(h == 3))
                yield
                S.op(DVE, lambda: V.tensor_tensor(
                    out=attm[0:np_, :, 0:np_], in0=pat[0:np_, :].rearrange("p (h i) -> p h i", h=4)[:, :, 0:np_],
                    in1=CA[0:np_, 0:np_].unsqueeze(1).broadcast_to([np_, 4, np_]), op=ALU.mult), [pat, CA], [attm])
                yield
                yield "need_state"
                po = bank
                if not sample:
                    for h in range(4):
                        c, r0 = h // 2, 64 * (h % 2)
                        S.op(PE, lambda h=h: P.matmul(po[0:np_, h * 128:(h + 1) * 128], lhsT=attm[0:np_, h, 0:np_],
                                                      rhs=vbt[0:np_, t, h * 128:(h + 1) * 128], start=True, stop=False),
                             [attm, vbt], [po], inc=False)
                        S.op(PE, lambda h=h, c=c, r0=r0: P.matmul(po[0:np_, h * 128:(h + 1) * 128], lhsT=qeTd[r0:r0 + 64, c, h % 2, 0:np_],
                                                                  rhs=Sbf[r0:r0 + 64, c, :], start=False, stop=True),
                             [qeTd, Sbf], [po], inc=(h == 3))
                        yield
                else:
                    for sq in range(NSS):
                        S.op(DVE, lambda sq=sq: V.tensor_tensor(out=qeTs[:, sq, :, :],
                                                                in0=qeTd[:, :, :, 0:NST].rearrange("p c hh i -> p (c hh) i"),
                                                                in1=ssel[:, sq, :].unsqueeze(1).broadcast_to([128, 4, NST]),
                                                                op=ALU.mult), [qeTd, ssel], [qeTs])
                        yield
                    for h in range(4):
                        c, r0 = h // 2, 64 * (h % 2)
                        S.op(PE, lambda h=h: P.matmul(po[0:NST, h * 128:(h + 1) * 128], lhsT=attm[:, h, 0:NST],
                                                      rhs=vbt[:, t, h * 128:(h + 1) * 128], start=True, stop=False),
                             [attm, vbt], [po], inc=False)
                        for sq in range(NSS):
                            S.op(PE, lambda h=h, c=c, r0=r0, sq=sq: P.matmul(
                                po[0:NST, h * 128:(h + 1) * 128], lhsT=qeTs[r0:r0 + 64, sq, h, :], rhs=Sbf_s[r0:r0 + 64, sq, c, :],
                                start=False, stop=(sq == NSS - 1)), [qeTs, Sbf_s], [po], inc=(h == 3 and sq == NSS - 1))
                        yield
                if DBG["glacopy"] == "act":
                    S.op(ACT, lambda: A.activation(out=osq[0:np_, :], in_=po[0:np_, :], func=AF.Copy), [po], [osq])
                else:
                    S.op(DVE, lambda: V.tensor_copy(out=osq[0:np_, :], in_=po[0:np_, :]), [po], [osq])
                yield
                pd = bank
                if not sample:
                    for c in range(2):
                        S.op(PE, lambda c=c: P.matmul(pd[:, c * 256:(c + 1) * 256], lhsT=kl[0:np_, c * 128:(c + 1) * 128],
                                                      rhs=vbt[0:np_, t, c * 256:(c + 1) * 256], start=True, stop=True),
                             [kl, vbt], [pd], inc=(c == 1))
                    yield
                    for c in range(2):
                        for hh in range(2):
                            r0 = 64 * hh
                            S.op(DVE, lambda c=c, hh=hh, r0=r0: V.scalar_tensor_tensor(
                                out=Sst[r0:r0 + 64, c, :], in0=Sst[r0:r0 + 64, c, :], scalar=ebT[r0:r0 + 64, c, np_ - 1:np_],
                                in1=pd[r0:r0 + 64, c * 256 + hh * 128:c * 256 + (hh + 1) * 128], op0=ALU.mult, op1=ALU.add),
                                 [Sst, ebT, pd], [Sst])
                            yield
                    if DBG["glacopy"] == "act":
                        S.op(ACT, lambda: A.activation(out=Sbf[:], in_=Sst[:], func=AF.Copy), [Sst], [Sbf])
                    else:
                        S.op(DVE, lambda: V.tensor_copy(out=Sbf[:], in_=Sst[:]), [Sst], [Sbf])
                    yield "state_done"
                else:
                    for sq in range(NSS):
                        S.op(DVE, lambda sq=sq: V.tensor_scalar(out=kls[0:NST, :], in0=kl[0:NST, :], scalar1=srow[0:NST, sq:sq + 1],
                                                                scalar2=None, op0=ALU.mult), [kl, srow], [kls])
                        yield
                        for c in range(2):
                            S.op(PE, lambda c=c: P.matmul(pd[:, c * 256:(c + 1) * 256], lhsT=kls[0:NST, c * 128:(c + 1) * 128],
                                                          rhs=vbt[0:NST, t, c * 256:(c + 1) * 256], start=True, stop=True),
                                 [kls, vbt], [pd], inc=(c == 1))
                        yield
                        for c in range(2):
                            for hh in range(2):
                                r0 = 64 * hh
                                S.op(DVE, lambda c=c, hh=hh, r0=r0, sq=sq: V.scalar_tensor_tensor(
                                    out=Sst_s[r0:r0 + 64, sq, c, :], in0=Sst_s[r0:r0 + 64, sq, c, :],
                                    scalar=ebT[r0:r0 + 64, c, TS * sq + TS - 1:TS * sq + TS],
                                    in1=pd[r0:r0 + 64, c * 256 + hh * 128:c * 256 + (hh + 1) * 128], op0=ALU.mult, op1=ALU.add),
                                     [Sst_s, ebT, pd], [Sst_s])
                                yield
                for h in range(4):
                    S.op(ACT, lambda h=h: A.activation(out=lz[0:np_, 0:128], in_=osq[0:np_, h * 128:(h + 1) * 128], func=AF.Square,
                                                       accum_out=oss[0:np_, h:h + 1]), [osq], [lz, oss])
                    yield
                rms_rstd(oss[0:np_, :], orstd[0:np_, :], np_, olt[0:np_, :], oss, orstd, olt, 128)
                yield
                for h in range(4):
                    S.op(DVE, lambda h=h: V.scalar_tensor_tensor(out=mixed[0:np_, t, 512 + h * 128:512 + (h + 1) * 128],
                                                                 in0=osq[0:np_, h * 128:(h + 1) * 128], scalar=orstd[0:np_, h:h + 1],
                                                                 in1=rbt[0:np_, t, h * 128:(h + 1) * 128], op0=ALU.mult, op1=ALU.mult),
                         [osq, orstd, rbt], [mixed])
                    yield

            x1buf = Buf("x1dram")
            xgo = [Buf("xgo0"), Buf("xgo1")]
            with ExitStack() as esp:
                amask = T(S, esp, "amask", [128, 19, 128], BF16)
                S.dma(SP, amask[:], c_amask[:, :, :], [], [amask])
                uinc = T(S, esp, "uinc", [128, 128])
                S.dma(SP, uinc[:], c_uinc[:, :], [], [uinc])
                uaft = T(S, esp, "uaft", [128, 128])
                S.dma(SP, uaft[:], c_uaft[:, :], [], [uaft])
                caus = T(S, esp, "caus", [128, 128], BF16)
                S.dma(SP, caus[:], c_caus[:, :], [], [caus])
                kTa = T(S, esp, "kTa", [128, 4, SEQ], BF16)
                vaug = T(S, esp, "vaug", [128, 16, HA, 65], BF16)
                S.op(POOL, lambda: GP.memset(vaug[:, :, :, 64:65], 1.0), [], [vaug])
                Sst = T(S, esp, "Sst", [128, 2, 128])
                Sbf = T(S, esp, "Sbf", [128, 2, 128], BF16)
                kTa_cur, vaug_cur = kTa, vaug
                glist = [(sq_, g_) for sq_ in range(DBG["nseq"]) for g_ in range(DBG["ngrp"])]

                def xsrc(sq_, g_):
                    return lambda t: xp[sq_, g_ * G + t * 128:g_ * G + (t + 1) * 128, :]
                for gi, (sq_, g_) in enumerate(glist):
                    if g_ == 0:
                        S.op(DVE, lambda: V.memset(Sst[:], 0.0), [], [Sst])
                        S.op(DVE, lambda: V.memset(Sbf[:], 0.0), [], [Sbf])
                    tok0 = g_ * G
                    nxt = xsrc(*glist[gi + 1]) if gi + 1 < len(glist) else None
                    mixer_group(4, 128, xsrc(sq_, g_), sq_, g_, g_ == 0,
                                lambda t, s=sq_, tok0=tok0: wk_p[s, tok0 + t * 128:tok0 + (t + 1) * 128, :],
                                lambda t, s=sq_, tok0=tok0: wv_p[s, tok0 + t * 128:tok0 + (t + 1) * 128, :],
                                lambda t, s=sq_, tok0=tok0: x1_p[s, tok0 + t * 128:tok0 + (t + 1) * 128, :],
                                front_done=(gi > 0), nxt=nxt)
                    if g_ == DBG["ngrp"] - 1:
                        for c in range(2):
                            for hh in range(2):
                                store(POOL, gs_p[sq_, 2 * c + hh, :, :], Sst[64 * hh:64 * hh + 64, c, :], Sst)

            S.barrier()
            if DBG["sample"]:
                ess = es.enter_context(ExitStack())
                smask = T(S, ess, "smask", [128, NSS * 16, NST], BF16)
                S.dma(SP, smask[:], c_smask[:, :, :], [], [smask])
                nmask = T(S, ess, "nmask", [NST, NST], BF16)
                S.dma(SP, nmask[:], c_nmask[:, :], [], [nmask])
                suinc = T(S, ess, "suinc", [NST, NST])
                S.dma(SP, suinc[:], c_suinc[:, :], [], [suinc])
                suaft = T(S, ess, "suaft", [NST, NST])
                S.dma(SP, suaft[:], c_suaft[:, :], [], [suaft])
                scaus = T(S, ess, "scaus", [NST, NST], BF16)
                S.dma(SP, scaus[:], c_scaus[:, :], [], [scaus])
                ssel = T(S, ess, "ssel", [128, NSS, NST])
                S.dma(SP, ssel[:], c_ssel[:, :, :], [], [ssel])
                srow = T(S, ess, "srow", [NST, NSS])
                S.dma(SP, srow[:], c_srow[:, :], [], [srow])
                Sst_s = T(S, ess, "Sst_s", [128, NSS, 2, 128])
                Sbf_s = T(S, ess, "Sbf_s", [128, NSS, 2, 128], BF16)
                qeTs = T(S, ess, "qeTs", [128, NSS, 4, NST], BF16)
                kls = T(S, ess, "kls", [NST, 256], BF16)
                kcb = [T(S, ess, "kcb%d" % i, [128, 2, 512], BF16) for i in range(2)]
                vcb = [T(S, ess, "vcb%d" % i, [128, 2, 512], BF16) for i in range(2)]
                kTc = [T(S, ess, "kTc%d" % i, [128, 4, 128], BF16) for i in range(2)]
                vaugc = [T(S, ess, "vaugc%d" % i, [128, HA, 65], BF16) for i in range(2)]
                pTs = [T(S, ess, "pTs%d" % i, [128, HA, NST], BF16) for i in range(2)]
                kTs = T(S, ess, "kTs", [128, 4, NST], BF16)
                vaug_s = T(S, ess, "vaug_s", [128, 1, HA, 65], BF16)
                S.op(POOL, lambda: GP.memset(vaug_s[:, :, :, 64:65], 1.0), [], [vaug_s])
                kTa_cur, vaug_cur = kTs, vaug_s
                S.op(DVE, lambda: V.memset(attms[0][:], 0.0), [], [attms[0]])
                S.op(DVE, lambda: V.memset(vbt[:], 0.0), [], [vbt])
                for sq in range(NSS):
                    for c in range(2):
                        for hh in range(2):
                            S.dma(SP, Sst_s[64 * hh:64 * hh + 64, sq, c, :], sg[sq, 2 * c + hh, :, :], [], [Sst_s])
                S.op(ACT, lambda: A.activation(out=Sbf_s[:], in_=Sst_s[:], func=AF.Copy), [Sst_s], [Sbf_s])
                mixer_group(1, NST, lambda t: xs[0:NST, :], 0, 0, True,
                            lambda t: wk_s[0:NST, :], lambda t: wv_s[0:NST, :], lambda t: x1_s[0:NST, :], sample=True)
                for sq in range(NSS):
                    for c in range(2):
                        for hh in range(2):
                            store(POOL, gs_s[sq, 2 * c + hh, :, :], Sst_s[64 * hh:64 * hh + 64, sq, c, :], Sst_s)

        S.barrier()
        with ExitStack() as es:
            wup = T(S, es, "wup", [128, 8, 2 * DFF], BF16)
            wdn = T(S, es, "wdn", [128, NFC, D], BF16)
            class _Blk2:
                def __init__(self, name):
                    self.buf = Buf(name)
            w_up3 = w_up.rearrange("(kc p) n -> p kc n", p=128)
            wupg = [_Blk2("wupg%d" % i) for i in range(6)]
            wupu = [_Blk2("wupu%d" % i) for i in range(6)]
            for i in range(6):
                ncol = min(512, DFF - i * 512)
                S.dma(POOL, wup[:, :, i * 512:i * 512 + ncol], w_up3[:, :, i * 512:i * 512 + ncol], [], [wupg[i]])
                S.dma(POOL, wup[:, :, DFF + i * 512:DFF + i * 512 + ncol], w_up3[:, :, DFF + i * 512:DFF + i * 512 + ncol], [], [wupu[i]])
            for fc in range(NFC):
                S.dma(POOL, wdn[:, fc, :], w_down[fc * 128:(fc + 1) * 128, :], [], [wdn])
            gpre2 = T(S, es, "gpre2", [128, D])
            S.dma(SP, gpre2[:], g_pre_ffn[0:1, :].partition_broadcast(128), [], [gpre2])
            gpost2 = T(S, es, "gpost2", [128, D])
            S.dma(SP, gpost2[:], g_post_ffn[0:1, :].partition_broadcast(128), [], [gpost2])
            cw = T(S, es, "cw", [128, 4, NFC])
            cwt = T(S, es, "cwt", [NFC, 4, 128])
            id32 = T(S, es, "id32", [32, 32])
            S.dma(SP, id32[:], c_id32[:, :], [], [id32])
            S.dma(SP, cwt[:, 0:3, :], conv_w.rearrange("j (c p) -> c j p", p=128), [], [cwt])
            S.dma(SP, cwt[:, 3, :], conv_b[0, :].rearrange("(c p) -> c p", p=128), [], [cwt])
            for j in range(4):
                S.op(PE, lambda j=j: P.matmul(ps[0][:, j * 32:j * 32 + NFC], lhsT=cwt[0:NFC, j, :], rhs=id32[0:NFC, 0:NFC],
                                              start=True, stop=True), [cwt, id32], [ps[0]], inc=(j == 3))
            S.op(ACT, lambda: A.activation(out=cw[:], in_=ps[0][:, 0:128].rearrange("p (j c) -> p j c", j=4)[:, :, 0:NFC],
                                           func=AF.Copy), [ps[0]], [cw])

            xs1 = [T(S, es, "xs1_%d" % i, [128, D]) for i in range(2)]
            xr = [T(S, es, "xr_%d" % i, [128, D]) for i in range(2)]
            xnbs = [T(S, es, "xnb2_%d" % i, [128, D], BF16) for i in range(2)]
            st_ss = [T(S, es, "pss%d" % i, [128, 1]) for i in range(2)]
            st_ln = [T(S, es, "pln%d" % i, [128, 1]) for i in range(2)]
            st_rs = [T(S, es, "prs%d" % i, [128, 1]) for i in range(2)]
            ssq = T(S, es, "ssq2", [128, 8])
            lnt = T(S, es, "lnt2", [128, 8])
            rstd = T(S, es, "rstd2", [128, 8])
            xnT = T(S, es, "xnT2", [128, 8, G], BF16)
            hT = T(S, es, "hT", [128, NFC, G], BF16)

            class _Sub:
                def __init__(self, fc):
                    self.buf = Buf("hT%d" % fc)
            hTs = [_Sub(fc) for fc in range(NFC)]
            junkb = T(S, es, "junkb", [128, 512], BF16)
            gbuf = [T(S, es, "gbuf%d" % i, [128, G + 2]) for i in range(1)] * 2
            cbuf = [T(S, es, "cbuf%d" % i, [128, G]) for i in range(1)] * 2
            gel = [T(S, es, "gel%d" % i, [128, G]) for i in range(1)] * 2
            carry = T(S, es, "carry", [128, NFC, 2])
            ytmp = [T(S, es, "ytmp2_%d" % i, [128, 512]) for i in range(2)]
            gl = [T(S, es, "gl%d" % i, [NST, 512]) for i in range(1)] * 2
            schist = T(S, es, "schist", [128, NFC, 8])
            gbs = T(S, es, "gbs", [128, NSS, TS + 2])
            id8 = T(S, es, "id8", [8, 8])
            S.dma(SP, id8[:], c_id8[:, :], [], [id8])
            xgo2 = [Buf("xgo2_%d" % i) for i in range(2)]

            def prenorm_a(x_src, np_, slot):
                xs_, xb_ = xs1[slot], xnbs[slot]
                S.dma(SP, xs_[0:np_, :], x_src, [x1buf], [xs_])
                S.op(ACT, lambda: A.activation(out=xb_[0:np_, :], in_=xs_[0:np_, :], func=AF.Square,
                                               accum_out=st_ss[slot][0:np_, 0:1]), [xs_], [xb_, st_ss[slot]])
                rms_rstd(st_ss[slot][0:np_, 0:1], st_rs[slot][0:np_, 0:1], np_, st_ln[slot][0:np_, 0:1],
                         st_ss[slot], st_rs[slot], st_ln[slot], D)
                S.op(DVE, lambda: V.scalar_tensor_tensor(out=xb_[0:np_, :], in0=xs_[0:np_, :], scalar=st_rs[slot][0:np_, 0:1],
                                                         in1=gpre2[0:np_, :], op0=ALU.mult, op1=ALU.mult),
                     [xs_, st_rs[slot], gpre2], [xb_])

            def prenorm_b(t, np_, slot):
                xb_ = xnbs[slot]
                transpose_tile(xb_, lambda c: xb_[0:np_, c * 128:(c + 1) * 128], np_, xnT,
                               xnT[:, :, t * np_:(t + 1) * np_], 8, slot)

            def ffn_group(nt, np_, x_src_fn, y_dst_fn, first_in_seq, last_in_seq, fc_dst, sample=False, nxt=None):
                ntok = nt * np_
                if sample:
                    for n in range(6):
                        ncol = min(512, DFF - n * 512)
                        glt = gl[0]
                        S.dma(SP, glt[0:8, 0:ncol], sc[:, n * 512:n * 512 + ncol], [], [glt])
                        nf = ncol // 128
                        for j in range(nf):
                            S.op(PE, lambda j=j: P.matmul(ps[0][:, j * 8:(j + 1) * 8], lhsT=glt[0:8, j * 128:(j + 1) * 128],
                                                          rhs=id8[0:8, 0:8], start=True, stop=True), [glt, id8], [ps[0]],
                                 inc=(j == nf - 1))
                        S.op(ACT, lambda n=n, nf=nf: A.activation(out=schist[:, 4 * n:4 * n + nf, :],
                                                                  in_=ps[0][:, 0:nf * 8].rearrange("p (f r) -> p f r", r=8),
                                                                  func=AF.Copy), [ps[0]], [schist])
                if first_in_seq:
                    S.op(POOL, lambda: GP.memset(carry[:], 0.0), [], [carry])
                for fc in range(NFC):
                    pg, pu = ps[(2 * fc) % 6], ps[(2 * fc + 1) % 6]
                    gb, cb_, ge = gbuf[fc % 2], cbuf[fc % 2], gel[fc % 2]
                    for kc in range(8):
                        S.op(PE, lambda kc=kc: P.matmul(pg[:, 0:ntok], lhsT=wup[:, kc, fc * 128:(fc + 1) * 128],
                                                        rhs=xnT[:, kc, 0:ntok], start=(kc == 0), stop=(kc == 7)),
                             [wupg[fc // 4], xnT], [pg], inc=(kc == 7))
                    for kc in range(8):
                        S.op(PE, lambda kc=kc: P.matmul(pu[:, 0:ntok], lhsT=wup[:, kc, DFF + fc * 128:DFF + (fc + 1) * 128],
                                                        rhs=xnT[:, kc, 0:ntok], start=(kc == 0), stop=(kc == 7)),
                             [wupu[fc // 4], xnT], [pu], inc=(kc == 7))
                    if not sample:
                        S.op(POOL, lambda: GP.tensor_copy(out=gb[:, 0:2], in_=carry[:, fc, :]), [carry], [gb])
                        S.op(ACT, lambda: A.activation(out=gb[:, 2:2 + ntok], in_=pg[:, 0:ntok], func=AF.Copy), [pg], [gb])
                        S.op(POOL, lambda: GP.tensor_copy(out=carry[:, fc, :], in_=gb[:, ntok:ntok + 2]), [gb], [carry])
                        g2, g1, g0, co = gb[:, 2:2 + ntok], gb[:, 1:1 + ntok], gb[:, 0:ntok], cb_[:, 0:ntok]
                        gsrc = gb
                    else:
                        S.op(POOL, lambda: GP.tensor_copy(out=gbs[:, :, 0:2], in_=schist[:, fc, :].rearrange("p (s j) -> p s j", j=2)),
                             [schist], [gbs])
                        S.op(ACT, lambda: A.activation(out=gbs[:, :, 2:2 + TS], in_=pg[:, 0:NST].rearrange("p (s t) -> p s t", t=TS),
                                                       func=AF.Copy), [pg], [gbs])
                        g2, g1, g0 = gbs[:, :, 2:2 + TS], gbs[:, :, 1:1 + TS], gbs[:, :, 0:TS]
                        co = cb_[:, 0:NST].rearrange("p (s t) -> p s t", t=TS)
                        gsrc = gbs
                    S.op(DVE, lambda: V.tensor_scalar(out=co, in0=g2, scalar1=cw[:, 2, fc:fc + 1],
                                                      scalar2=cw[:, 3, fc:fc + 1], op0=ALU.mult, op1=ALU.add),
                         [gsrc, cw], [cb_])
                    S.op(DVE, lambda: V.scalar_tensor_tensor(out=co, in0=g1, scalar=cw[:, 1, fc:fc + 1],
                                                             in1=co, op0=ALU.mult, op1=ALU.add),
                         [gsrc, cw, cb_], [cb_])
                    S.op(DVE, lambda: V.scalar_tensor_tensor(out=co, in0=g0, scalar=cw[:, 0, fc:fc + 1],
                                                             in1=co, op0=ALU.mult, op1=ALU.add),
                         [gsrc, cw, cb_], [cb_])
                    S.op(ACT, lambda: A.activation(out=ge[:, 0:ntok], in_=cb_[:, 0:ntok], func=AF.Gelu), [cb_], [ge])
                    S.op(DVE, lambda: V.tensor_tensor(out=hT[:, fc, 0:ntok], in0=ge[:, 0:ntok], in1=pu[:, 0:ntok], op=ALU.mult),
                         [ge, pu], [hTs[fc]])
                if last_in_seq:
                    nrow = NST if sample else 2
                    for n in range(6):
                        ncol = min(512, DFF - n * 512)
                        pq = ps[n % 2]
                        for kc in range(8):
                            S.op(PE, lambda kc=kc: P.matmul(pq[0:nrow, 0:ncol], lhsT=xnT[:, kc, ntok - nrow:ntok],
                                                            rhs=wup[:, kc, n * 512:n * 512 + ncol], start=(kc == 0), stop=(kc == 7)),
                                 [wupg[n], xnT], [pq], inc=(kc == 7))
                        glt = gl[n % 2]
                        S.op(ACT, lambda: A.activation(out=glt[0:nrow, 0:ncol], in_=pq[0:nrow, 0:ncol], func=AF.Copy),
                             [pq], [glt])
                        if not sample:
                            store(POOL, fc_dst[:, n * 512:n * 512 + ncol], glt[0:2, 0:ncol], glt)
                        else:
                            for sq in range(NSS):
                                store(POOL, fc_s[sq, :, n * 512:n * 512 + ncol], glt[TS * sq + TS - 2:TS * sq + TS, 0:ncol], glt)

                def dn_mm(t):
                    for hf in range(2):
                        pdn = ps[(2 * t + hf) % 6]
                        for fc in range(NFC):
                            S.op(PE, lambda fc=fc, hf=hf, pdn=pdn: P.matmul(pdn[0:np_, :], lhsT=hT[:, fc, t * np_:(t + 1) * np_],
                                                                            rhs=wdn[:, fc, hf * 512:(hf + 1) * 512],
                                                                            start=(fc == 0), stop=(fc == NFC - 1)),
                                 [wdn, hTs[fc]], [pdn], inc=(fc == NFC - 1))

                def dn_epi(t):
                    xr_ = xr[t % 2]
                    for hf in range(2):
                        pdn = ps[(2 * t + hf) % 6]
                        S.op(ACT, lambda hf=hf, pdn=pdn: A.activation(out=junkb[0:np_, :], in_=pdn[0:np_, :],
                                                                      func=AF.Square, accum_out=ssq[0:np_, 4 + hf:5 + hf]),
                             [pdn], [junkb, ssq])
                    S.op(DVE, lambda: V.tensor_tensor(out=ssq[0:np_, 6:7], in0=ssq[0:np_, 4:5], in1=ssq[0:np_, 5:6], op=ALU.add),
                         [ssq], [ssq])
                    rms_rstd(ssq[0:np_, 6:7], rstd[0:np_, 6:7], np_, lnt[0:np_, 6:7], ssq, rstd, lnt, D)
                    for hf in range(2):
                        yt_ = ytmp[hf]
                        pdn = ps[(2 * t + hf) % 6]
                        S.op(DVE, lambda hf=hf, yt_=yt_, pdn=pdn: V.scalar_tensor_tensor(
                            out=yt_[0:np_, :], in0=pdn[0:np_, :], scalar=rstd[0:np_, 6:7],
                            in1=gpost2[0:np_, hf * 512:(hf + 1) * 512], op0=ALU.mult, op1=ALU.mult),
                             [pdn, rstd, gpost2], [yt_])
                        S.op(DVE, lambda hf=hf, yt_=yt_: V.tensor_tensor(out=xr_[0:np_, hf * 512:(hf + 1) * 512], in0=yt_[0:np_, :],
                                                                         in1=xr_[0:np_, hf * 512:(hf + 1) * 512], op=ALU.add),
                             [yt_, xr_], [xr_])
                    S.dma(SP, y_dst_fn(t), xr_[0:np_, :], [xr_], [], sembuf=xgo2[t % 2])
                    if xgo2[t % 2] not in out_bufs:
                        out_bufs.append(xgo2[t % 2])

                def dn_pre(t):
                    if nxt is not None:
                        prenorm_a(nxt(t), 128, t % 2)
                    S.dma(SP, xr[t % 2][0:np_, :], x_src_fn(t), [x1buf], [xr[t % 2]])

                if sample and nxt is not None:
                    for t4 in range(4):
                        prenorm_a(nxt(t4), 128, t4 % 2)
                        prenorm_b(t4, 128, t4 % 2)
                    nxt = None
                dn_pre(0)
                dn_mm(0)
                for t in range(nt):
                    if t + 1 < nt:
                        dn_pre(t + 1)
                        dn_mm(t + 1)
                    if nxt is not None:
                        prenorm_b(t, 128, t % 2)
                    dn_epi(t)

            groups = []
            for s in range(DBG["nseq"] if DBG["phaseB"] else 0):
                for g in range(DBG["ngrp"]):
                    tok0 = g * G
                    groups.append((lambda t, s=s, tok0=tok0: x1_p[s, tok0 + t * 128:tok0 + (t + 1) * 128, :],
                                   lambda t, s=s, tok0=tok0: y_p[s, tok0 + t * 128:tok0 + (t + 1) * 128, :],
                                   g == 0, g == DBG["ngrp"] - 1, fc_p[s, :, :]))
            if DBG["sample"] and DBG["phaseB"]:
                prenorm_a(x1_s[0:NST, :], NST, 0)
                prenorm_b(0, NST, 0)
                ffn_group(1, NST, lambda t: x1_s[0:NST, :], lambda t: y_s[0:NST, :], True, True, None, sample=True,
                          nxt=(groups[0][0] if groups else None))
            elif groups:
                for t in range(4):
                    prenorm_a(groups[0][0](t), 128, t % 2)
                    prenorm_b(t, 128, t % 2)
            for i, (xf, yf, fi, la, fcd) in enumerate(groups):
                ffn_group(4, 128, xf, yf, fi, la, fcd, nxt=(groups[i + 1][0] if i + 1 < len(groups) else None))

        for b in out_bufs:
            SP.h.wait_ge(b.dsem, b.dcnt)
    return nc


def _mult(delta):
    delta = np.asarray(delta)
    nn = delta >= 0
    m = (nn & (delta <= 128)).astype(np.float32)
    m += (nn & (delta <= 512) & (delta % 4 == 0))
    m += (nn & (delta <= 2048) & (delta % 16 == 0))
    return m


def _constants():
    bf = ml_dtypes.bfloat16
    c = {}
    c["c_ident"] = np.eye(128, dtype=np.float32).astype(bf)
    jj = np.arange(128)[:, None]
    ii = np.arange(128)[None, :]
    am = np.zeros((128, 19, 128), np.float32)
    for e in range(-3, 16):
        am[:, e + 3, :] = _mult(128 * e + ii - jj)
    c["c_amask"] = am.astype(bf)
    j = np.arange(128)[:, None]
    i = np.arange(128)[None, :]
    c["c_uinc"] = np.where(j <= i, -1.0 / 16.0, 0.0).astype(np.float32)
    c["c_uaft"] = np.where(j > i, -1.0 / 16.0, 0.0).astype(np.float32)
    c["c_caus"] = (j <= i).astype(np.float32).astype(bf)
    c["c_ones"] = np.ones((1, 128), np.float32)
    sm = np.zeros((128, NSS * 16, NST), np.float32)
    tt = np.arange(TS)[None, :]
    for s in range(NSS):
        for jb in range(16):
            sm[:, s * 16 + jb, s * TS:(s + 1) * TS] = _mult(SEQ + tt - (128 * jb + jj))
    c["c_smask"] = sm.astype(bf)
    tok = np.arange(NST)
    same = (tok[:, None] // TS) == (tok[None, :] // TS)
    dl = tok[None, :] - tok[:, None]
    c["c_nmask"] = (np.where(same, _mult(dl), 0.0)).astype(np.float32).astype(bf)
    c["c_suinc"] = np.where(same & (tok[:, None] <= tok[None, :]), -1.0 / 16.0, 0.0).astype(np.float32)
    c["c_suaft"] = np.where(same & (tok[:, None] > tok[None, :]), -1.0 / 16.0, 0.0).astype(np.float32)
    c["c_scaus"] = (same & (tok[:, None] <= tok[None, :])).astype(np.float32).astype(bf)
    ssel = np.zeros((128, NSS, NST), np.float32)
    srow = np.zeros((NST, NSS), np.float32)
    for s in range(NSS):
        ssel[:, s, s * TS:(s + 1) * TS] = 1.0
        srow[s * TS:(s + 1) * TS, s] = 1.0
    c["c_ssel"] = ssel
    c["c_srow"] = srow
    c["c_id8"] = np.eye(8, dtype=np.float32)
    c["c_id32"] = np.eye(32, dtype=np.float32)
    return c


_NC_CACHE = {}


def kernel(x_prompt, x_sample, cache_k_win, cache_v_win, state_gla, state_ffn_conv,
           w_in, w_a2, b_a, g_gla_norm, w_o, g_pre_mix, g_post_mix, g_pre_ffn, g_post_ffn,
           w_up, conv_w, conv_b, w_down):
    f = lambda a: np.ascontiguousarray(np.asarray(a, dtype=np.float32))
    if "nc" not in _NC_CACHE:
        _NC_CACHE["nc"] = build_program()
    nc = _NC_CACHE["nc"]
    consts = _constants()
    shared = {
        "w_in": f(w_in[0]), "w_a2": f(w_a2[0]), "b_a": f(b_a[0]).reshape(1, 256),
        "g_gla": f(g_gla_norm[0]).reshape(1, 512), "w_o": f(w_o[0]),
        "g_pre_mix": f(g_pre_mix[0]).reshape(1, D), "g_post_mix": f(g_post_mix[0]).reshape(1, D),
        "g_pre_ffn": f(g_pre_ffn[0]).reshape(1, D), "g_post_ffn": f(g_post_ffn[0]).reshape(1, D),
        "w_up": f(w_up[0]), "conv_w": f(conv_w[0]), "conv_b": f(conv_b[0]).reshape(1, DFF), "w_down": f(w_down[0]),
    }
    shared.update(consts)
    x_prompt = np.asarray(x_prompt); x_sample = np.asarray(x_sample)
    cache_k_win = np.asarray(cache_k_win); cache_v_win = np.asarray(cache_v_win)
    state_gla = np.asarray(state_gla); state_ffn_conv = np.asarray(state_ffn_conv)
    in_maps = []
    for c in range(NCORES):
        m = dict(shared)
        m["xp"] = f(x_prompt[NSEQ * c:NSEQ * (c + 1)])
        m["xs"] = f(x_sample[NSS * c:NSS * (c + 1)]).reshape(NST, D)
        m["ck"] = f(cache_k_win[0, NSS * c:NSS * (c + 1)]).reshape(NSS, SEQ, 512)
        m["cv"] = f(cache_v_win[0, NSS * c:NSS * (c + 1)]).reshape(NSS, SEQ, 512)
        m["sg"] = f(state_gla[0, NSS * c:NSS * (c + 1)])
        m["sc"] = f(state_ffn_conv[0, NSS * c:NSS * (c + 1)]).reshape(NSS * 2, DFF)
        in_maps.append(m)
    res = run_bass_kernel_spmd(nc, in_maps, core_ids=list(range(NCORES)))
    R = res.results
    cat = lambda k: np.concatenate([np.asarray(r[k], dtype=np.float32) for r in R], axis=0)
    y_prompt = cat("y_p")
    y_sample = cat("y_s").reshape(32, TS, D)
    wk_p = cat("wk_p").reshape(1, 16, SEQ, HA, 64)
    wv_p = cat("wv_p").reshape(1, 16, SEQ, HA, 64)
    gs_p = cat("gs_p").reshape(1, 16, 4, 64, 128)
    fc_p = cat("fc_p").reshape(1, 16, 2, DFF)
    wk_s = cat("wk_s").reshape(1, 32, TS, HA, 64)
    wv_s = cat("wv_s").reshape(1, 32, TS, HA, 64)
    gs_s = cat("gs_s").reshape(1, 32, 4, 64, 128)
    fc_s = cat("fc_s").reshape(1, 32, 2, DFF)
    return (y_prompt, y_sample, wk_p, wv_p, gs_p, fc_p, wk_s, wv_s, gs_s, fc_s)
```

```python
import numpy as np
import ml_dtypes
from contextlib import ExitStack
import concourse.bass as bass
import concourse.mybir as mybir
from concourse.bass_utils import run_bass_kernel_spmd

F32 = mybir.dt.float32
BF16 = mybir.dt.bfloat16
AF = mybir.ActivationFunctionType
ALU = mybir.AluOpType
AX = mybir.AxisListType

NCORES = 8
D = 1024
SEQ = 2048
NSEQ = 2
NSS = 4
TS = 8
NST = NSS * TS
HA = 8
DFF = 2816
NFC = DFF // 128
DIN = 3088
EPS = 1e-6
G = 512
DBG = {"nseq": NSEQ, "ngrp": 4, "phaseB": True, "stage": 9, "gla": True, "gs": 9, "gs2": 9, "sample": True, "npump": 6, "perblk": 0, "look": 1, "glacopy": "dve"}
QA, KA, VA, QB, KB, VB, RB, AL = 0, 512, 1024, 1536, 1792, 2048, 2560, 3072


class Buf:
    __slots__ = ("name", "w", "r", "dsem", "dcnt", "excl")

    def __init__(self, name):
        self.name = name
        self.excl = False
        self.w = None
        self.r = []
        self.dsem = None
        self.dcnt = 0


class Eng:
    def __init__(self, name, h, sem):
        self.name, self.h, self.sem = name, h, sem
        self.count = 0
        self.waited = {}


class Sched:
    def __init__(self, nc, es):
        self.nc, self.es = nc, es
        mk = lambda n, h: Eng(n, h, es.enter_context(nc.semaphore("s_" + n)))
        self.pe = mk("pe", nc.tensor)
        self.act = mk("act", nc.scalar)
        self.dve = mk("dve", nc.vector)
        self.pool = mk("pool", nc.gpsimd)
        self.sp = mk("sp", nc.sync)
        self.semcur = {}
        self.out_tickets = {}
        self.nsem = 5

    def _wait(self, eng, tk, same_ok):
        if tk is None:
            return
        sem, val, en = tk
        if en == eng.name and eng.name == "pe":
            return
        if en == "dma":
            val = max(val, self.semcur[id(sem)][1])
        if eng.waited.get(id(sem), 0) >= val:
            return
        eng.waited[id(sem)] = val
        eng.h.wait_ge(sem, val)

    def _deps(self, eng, reads, writes):
        for b in reads:
            b = b.buf if hasattr(b, "buf") else b
            self._wait(eng, b.w, True)
            if b.excl:
                for tk in b.r:
                    if tk[2] != eng.name:
                        self._wait(eng, tk, True)
        for b in writes:
            b = b.buf if hasattr(b, "buf") else b
            self._wait(eng, b.w, False)
            for tk in b.r:
                self._wait(eng, tk, False)

    def _commit(self, tk, reads, writes):
        for b in reads:
            b = b.buf if hasattr(b, "buf") else b
            b.r.append(tk)
        for b in writes:
            b = b.buf if hasattr(b, "buf") else b
            b.w = tk
            b.r = []

    def op(self, eng, fn, reads=(), writes=(), inc=True):
        self._deps(eng, reads, writes)
        ins = fn()
        if inc:
            eng.count += 1
            ins.then_inc(eng.sem, 1)
            tk = (eng.sem, eng.count, eng.name)
        else:
            tk = (eng.sem, eng.count + 1, eng.name)
        self._commit(tk, reads, writes)
        return tk

    def barrier(self):
        engs = [self.pe, self.act, self.dve, self.pool, self.sp]
        for e in engs:
            for o in engs:
                if o is e or o.count == 0:
                    continue
                if e.waited.get(id(o.sem), 0) < o.count:
                    e.waited[id(o.sem)] = o.count
                    e.h.wait_ge(o.sem, o.count)
            for sem, val in self.semcur.values():
                if e.waited.get(id(sem), 0) < val:
                    e.waited[id(sem)] = val
                    e.h.wait_ge(sem, val)

    def dma(self, eng, out, in_, reads, writes, sembuf=None, **kw):
        sb = sembuf if sembuf is not None else writes[0]
        sb = sb.buf if hasattr(sb, "buf") else sb
        if sb.dsem is None:
            sb.dsem = self.es.enter_context(self.nc.semaphore("d_" + sb.name))
            self.nsem += 1
        saved = []
        for b in writes:
            b = b.buf if hasattr(b, "buf") else b
            if b.w is not None and b.w[2] == "dma" and b.w[0] is sb.dsem:
                saved.append((b, b.w))
                b.w = None
        self._deps(eng, reads, writes)
        for b, w in saved:
            b.w = w
        sb.dcnt += 16
        eng.h.dma_start(out=out, in_=in_, **kw).then_inc(sb.dsem, 16)
        self.semcur[id(sb.dsem)] = (sb.dsem, sb.dcnt)
        tk = (sb.dsem, sb.dcnt, "dma")
        self._commit(tk, reads, writes)
        return tk


class T:
    def __init__(self, S, es, name, shape, dtype=F32, psum=False):
        if psum:
            self.t = es.enter_context(S.nc.psum_tensor(name, shape, dtype))
        else:
            self.t = es.enter_context(S.nc.sbuf_tensor(name, shape, dtype))
        self.buf = Buf(name)
        self.buf.excl = psum
        self.shape = shape

    def __getitem__(self, idx):
        return self.t[idx]


def build_program():
    nc = bass.Bass("TRN2", target_bir_lowering=False)
    dt_in = lambda n, s, d=F32: nc.dram_tensor(n, s, d, kind="ExternalInput").ap()
    dt_out = lambda n, s: nc.dram_tensor(n, s, F32, kind="ExternalOutput").ap()
    xp = dt_in("xp", [NSEQ, SEQ, D])
    xs = dt_in("xs", [NST, D])
    ck = dt_in("ck", [NSS, SEQ, 512])
    cv = dt_in("cv", [NSS, SEQ, 512])
    sg = dt_in("sg", [NSS, 4, 64, 128])
    sc = dt_in("sc", [NSS * 2, DFF])
    w_in = dt_in("w_in", [D, DIN])
    w_a2 = dt_in("w_a2", [16, 256])
    b_a = dt_in("b_a", [1, 256])
    g_gla = dt_in("g_gla", [1, 512])
    w_o = dt_in("w_o", [D, D])
    g_pre_mix = dt_in("g_pre_mix", [1, D])
    g_post_mix = dt_in("g_post_mix", [1, D])
    g_pre_ffn = dt_in("g_pre_ffn", [1, D])
    g_post_ffn = dt_in("g_post_ffn", [1, D])
    w_up = dt_in("w_up", [D, 2 * DFF])
    conv_w = dt_in("conv_w", [3, DFF])
    conv_b = dt_in("conv_b", [1, DFF])
    w_down = dt_in("w_down", [DFF, D])
    c_ident = dt_in("c_ident", [128, 128], BF16)
    c_amask = dt_in("c_amask", [128, 19, 128], BF16)
    c_uinc = dt_in("c_uinc", [128, 128])
    c_uaft = dt_in("c_uaft", [128, 128])
    c_caus = dt_in("c_caus", [128, 128], BF16)
    c_ones = dt_in("c_ones", [1, 128])
    c_smask = dt_in("c_smask", [128, NSS * 16, NST], BF16)
    c_nmask = dt_in("c_nmask", [NST, NST], BF16)
    c_suinc = dt_in("c_suinc", [NST, NST])
    c_suaft = dt_in("c_suaft", [NST, NST])
    c_scaus = dt_in("c_scaus", [NST, NST], BF16)
    c_ssel = dt_in("c_ssel", [128, NSS, NST])
    c_srow = dt_in("c_srow", [NST, NSS])
    c_id8 = dt_in("c_id8", [8, 8])
    c_id32 = dt_in("c_id32", [32, 32])
    y_p = dt_out("y_p", [NSEQ, SEQ, D])
    y_s = dt_out("y_s", [NST, D])
    wk_p = dt_out("wk_p", [NSEQ, SEQ, 512])
    wv_p = dt_out("wv_p", [NSEQ, SEQ, 512])
    gs_p = dt_out("gs_p", [NSEQ, 4, 64, 128])
    fc_p = dt_out("fc_p", [NSEQ, 2, DFF])
    wk_s = dt_out("wk_s", [NST, 512])
    wv_s = dt_out("wv_s", [NST, 512])
    gs_s = dt_out("gs_s", [NSS, 4, 64, 128])
    fc_s = dt_out("fc_s", [NSS, 2, DFF])
    x1_p = nc.dram_tensor("x1_p", [NSEQ, SEQ, D], F32, kind="Internal").ap()
    x1_s = nc.dram_tensor("x1_s", [NST, D], F32, kind="Internal").ap()

    with ExitStack() as es0:
        S = Sched(nc, es0)
        PE, ACT, DVE, POOL, SP = S.pe, S.act, S.dve, S.pool, S.sp
        V, A, P, GP = nc.vector, nc.scalar, nc.tensor, nc.gpsimd
        out_bufs = []

        def store(eng, dst, src_ap, src_t):
            ob = getattr(src_t, "obuf", None)
            if ob is None:
                ob = Buf(src_t.buf.name + "_o")
                src_t.obuf = ob
            S.dma(eng, dst, src_ap, reads=[src_t], writes=[], sembuf=ob)
            if ob not in out_bufs:
                out_bufs.append(ob)

        ident = T(S, es0, "ident", [128, 128], BF16)
        S.dma(SP, ident[:], c_ident[:, :], [], [ident])
        ones_r = T(S, es0, "ones_r", [1, 128])
        S.dma(SP, ones_r[:], c_ones[:, :], [], [ones_r])

        ps = [T(S, es0, "ps%d" % i, [128, 512], F32, psum=True) for i in range(8)]

        class _View:
            def __init__(self, t):
                self.buf = t.buf
                self.v = t[:].bitcast(BF16).rearrange("p (c i) -> p c i", c=8)

            def __getitem__(self, idx):
                return self.v[idx]
        ptp = [_View(ps[6]), _View(ps[7]), _View(ps[0]), _View(ps[1])]

        def rms_rstd(ss_ap, out_ap, n, tmp_ap, ss_t, out_t, tmp_t, nfeat):
            S.op(ACT, lambda: A.activation(out=tmp_ap, in_=ss_ap, func=AF.Ln, scale=1.0 / nfeat, bias=epsb[0:n, 0:1]),
                 [ss_t, epsb], [tmp_t])
            S.op(ACT, lambda: A.activation(out=out_ap, in_=tmp_ap, func=AF.Exp, scale=-0.5), [tmp_t], [out_t])

        epsb = T(S, es0, "epsb", [128, 1])
        S.op(DVE, lambda: V.memset(epsb[:], EPS), [], [epsb])
        oneb = T(S, es0, "oneb", [128, 1])
        S.op(DVE, lambda: V.memset(oneb[:], 1.0), [], [oneb])

        def transpose_tile(src_t, src_ap_fn, np_, dst_t, dst_ap, nchunk, slot):
            pt = ptp[slot]
            for c in range(nchunk):
                S.op(PE, lambda c=c: P.transpose(pt[:, c, 0:np_], src_ap_fn(c), ident[0:np_, 0:np_]),
                     [src_t, ident], [pt], inc=(c == nchunk - 1))
            S.op(ACT, lambda: A.activation(out=dst_ap, in_=pt[:, 0:nchunk, 0:np_], func=AF.Copy), [pt], [dst_t])

        with ExitStack() as es:
            win = T(S, es, "win", [128, 8, DIN], BF16)
            wo = T(S, es, "wo", [128, 8, D], BF16)
            class _Blk:
                def __init__(self, name):
                    self.buf = Buf(name)
            w_in3 = w_in.rearrange("(kc p) n -> p kc n", p=128)
            win_blocks = [(AL, DIN), (QB, VB), (VB, AL), (QA, KA), (KA, VA), (VA, QB)]
            win_bufs = []
            for bi, (c0, c1) in enumerate(win_blocks):
                wb = _Blk("winb%d" % bi)
                win_bufs.append(wb)
                for k0 in range(0, 8, 4):
                    S.dma(POOL, win[:, k0:k0 + 4, c0:c1], w_in3[:, k0:k0 + 4, c0:c1], [], [wb])

            def winb(col0):
                for (c0, c1), wb in zip(win_blocks, win_bufs):
                    if c0 <= col0 < c1:
                        return wb
            for kc in range(8):
                S.dma(POOL, wo[:, kc, :], w_o[kc * 128:(kc + 1) * 128, :], [], [wo])
            wa2 = T(S, es, "wa2", [16, 256])
            S.dma(SP, wa2[:], w_a2[:, :], [], [wa2])
            bar = T(S, es, "bar", [1, 256])
            S.dma(SP, bar[:], b_a[:, :], [], [bar])
            gpre = T(S, es, "gpre", [128, D])
            S.dma(SP, gpre[:], g_pre_mix[0:1, :].partition_broadcast(128), [], [gpre])
            gpost = T(S, es, "gpost", [128, D])
            S.dma(SP, gpost[:], g_post_mix[0:1, :].partition_broadcast(128), [], [gpost])
            ggla = T(S, es, "ggla", [128, 512])
            S.dma(SP, ggla[:], g_gla[0:1, :].partition_broadcast(128), [], [ggla])

            xs1 = [T(S, es, "xs1a_%d" % i, [128, D]) for i in range(2)]
            xr = [T(S, es, "xra_%d" % i, [128, D]) for i in range(2)]
            f_ss = [T(S, es, "fss%d" % i, [128, 1]) for i in range(4)]
            f_ln = [T(S, es, "fln%d" % i, [128, 1]) for i in range(4)]
            f_rs = [T(S, es, "frs%d" % i, [128, 1]) for i in range(4)]
            ssq = T(S, es, "ssq", [128, 8])
            lnt = T(S, es, "lnt", [128, 8])
            rstd = T(S, es, "rstd", [128, 8])
            xnb4 = [T(S, es, "xnb%d" % i, [128, D], BF16) for i in range(4)]
            xnb = xnb4[0]
            xnT = T(S, es, "xnT", [128, 8, G], BF16)
            qTa = T(S, es, "qTa", [128, 4, G], BF16)
            qTb = T(S, es, "qTb", [128, 2, G], BF16)
            kTb = T(S, es, "kTb", [128, 2, G], BF16)
            alT = T(S, es, "alT", [16, G])
            ktok = [T(S, es, "ktok%d" % i, [128, 512]) for i in range(2)]
            vtok = [T(S, es, "vtok%d" % i, [128, 512]) for i in range(1)] * 2
            kbt = T(S, es, "kbt", [128, 4, 256], BF16)
            vbt = T(S, es, "vbt", [128, 4, 512], BF16)
            rbt = T(S, es, "rbt", [128, 4, 512], BF16)
            mixed = T(S, es, "mixed", [128, 4, D], BF16)
            mixT = xnT
            pT = [T(S, es, "pT%d" % i, [128, 512], BF16) for i in range(4)]
            rden = [T(S, es, "rden%d" % i, [128, 4]) for i in range(2)]
            gtmp = T(S, es, "gtmp", [128, 512])
            lzs = [T(S, es, "lz%d" % i, [128, 256]) for i in range(2)]
            ebTs = [T(S, es, "ebT%d" % i, [128, 2, 128]) for i in range(2)]
            enbTs = [T(S, es, "enbT%d" % i, [128, 2, 128]) for i in range(2)]
            qeTds = [T(S, es, "qeTd%d" % i, [128, 2, 2, 128], BF16) for i in range(2)]
            keTs = [T(S, es, "keT%d" % i, [128, 2, 128], BF16) for i in range(2)]
            klbs = [T(S, es, "kl%d" % i, [128, 256], BF16) for i in range(2)]
            attms = [T(S, es, "attm%d" % i, [128, 4, 128], BF16) for i in range(2)]
            osss = [T(S, es, "oss%d" % i, [128, 4]) for i in range(2)]
            olts = [T(S, es, "olt%d" % i, [128, 4]) for i in range(2)]
            orstds = [T(S, es, "orstd%d" % i, [128, 4]) for i in range(2)]
            osqs = [gtmp, T(S, es, "osq1", [128, 512])]
            for i in range(2):
                S.op(POOL, lambda i=i: GP.memset(qeTds[i][:], 0.0), [], [qeTds[i]])
            ytmp = T(S, es, "ytmp", [128, D])

            class _JunkA:
                buf = ytmp.buf

                def __getitem__(self, idx):
                    return ytmp[:].bitcast(BF16)[:, 0:D][idx]
            junk = _JunkA()

            def front_a(x_src, np_, t):
                xs_, xb_ = xs1[t % 2], xnb4[t]
                S.dma(SP, xs_[0:np_, :], x_src, [], [xs_])
                S.op(ACT, lambda: A.activation(out=xb_[0:np_, :], in_=xs_[0:np_, :], func=AF.Square,
                                               accum_out=f_ss[t][0:np_, 0:1]), [xs_], [xb_, f_ss[t]])
                rms_rstd(f_ss[t][0:np_, 0:1], f_rs[t][0:np_, 0:1], np_, f_ln[t][0:np_, 0:1], f_ss[t], f_rs[t], f_ln[t], D)
                S.op(DVE, lambda: V.scalar_tensor_tensor(out=xb_[0:np_, :], in0=xs_[0:np_, :], scalar=f_rs[t][0:np_, 0:1],
                                                         in1=gpre[0:np_, :], op0=ALU.mult, op1=ALU.mult),
                     [xs_, f_rs[t], gpre], [xb_])

            def front_b(np_, t):
                xb_ = xnb4[t]
                transpose_tile(xb_, lambda c: xb_[0:np_, c * 128:(c + 1) * 128], np_, xnT,
                               xnT[:, :, t * np_:(t + 1) * np_], 8, t % 2)

            def mixer_group(nt, np_, x_src_fn, s_idx, g_idx, first_in_seq, k_dst_fn, v_dst_fn, x1_dst_fn,
                            sample=False, front_done=False, nxt=None):
                ntok = nt * np_
                if not front_done:
                    for t in range(nt):
                        front_a(x_src_fn(t), np_, t)
                for t in range(nt):
                    front_b(np_, t)

                if DBG["stage"] < 2:
                    return
                def fproj(col0, ncols, dst_t, dst_ap, bank):
                    for kc in range(8):
                        S.op(PE, lambda kc=kc: P.matmul(ps[bank][0:ncols, 0:ntok], lhsT=win[:, kc, col0:col0 + ncols],
                                                        rhs=xnT[:, kc, 0:ntok], start=(kc == 0), stop=(kc == 7)),
                             [winb(col0), xnT], [ps[bank]], inc=(kc == 7))
                    S.op(ACT, lambda: A.activation(out=dst_ap, in_=ps[bank][0:ncols, 0:ntok], func=AF.Copy),
                         [ps[bank]], [dst_t])

                if DBG["stage"] < 3:
                    return
                def tproj(t, col0, ncols, bank, evac):
                    for kc in range(8):
                        S.op(PE, lambda kc=kc: P.matmul(ps[bank][0:np_, 0:ncols], lhsT=xnT[:, kc, t * np_:(t + 1) * np_],
                                                        rhs=win[:, kc, col0:col0 + ncols], start=(kc == 0), stop=(kc == 7)),
                             [winb(col0), xnT], [ps[bank]], inc=(kc == 7))
                    evac(ps[bank])

                evs = []
                for t in range(nt):
                    jb = (g_idx * 4 + t) if not sample else 0
                    kt, vt = ktok[t % 2], vtok[t % 2]

                    def ev_k(pb, kt=kt, t=t):
                        S.op(ACT, lambda: A.activation(out=kt[0:np_, :], in_=pb[0:np_, 0:512], func=AF.Copy), [pb], [kt])
                        store(POOL, k_dst_fn(t), kt[0:np_, :], kt)

                    def ev_v(pb, vt=vt, t=t, jb=jb):
                        S.op(ACT, lambda: A.activation(out=vt[0:np_, :], in_=pb[0:np_, 0:512], func=AF.Copy), [pb], [vt])
                        S.op(DVE, lambda: V.tensor_copy(out=vaug_cur[0:np_, jb, :, 0:64],
                                                        in_=pb[0:np_, 0:512].rearrange("p (h e) -> p h e", h=HA)),
                             [pb], [vaug_cur])
                        store(POOL, v_dst_fn(t), vt[0:np_, :], vt)

                    def ev_kb(pb, t=t):
                        S.op(ACT, lambda: A.activation(out=kbt[0:np_, t, :], in_=pb[0:np_, 0:256], func=AF.Copy), [pb], [kbt])

                    def ev_vb(pb, t=t):
                        S.op(DVE, lambda: V.tensor_copy(out=vbt[0:np_, t, :], in_=pb[0:np_, 0:512]), [pb], [vbt])

                    def ev_rb(pb, t=t):
                        S.op(ACT, lambda: A.activation(out=gtmp[0:np_, :], in_=pb[0:np_, 0:512], func=AF.Exp, scale=-1.0), [pb], [gtmp])
                        S.op(DVE, lambda: V.tensor_tensor(out=rbt[0:np_, t, :], in0=pb[0:np_, 0:512], in1=ggla[0:np_, :], op=ALU.mult),
                             [pb, ggla], [rbt])
                        S.op(DVE, lambda: V.tensor_scalar_add(out=gtmp[0:np_, :], in0=gtmp[0:np_, :], scalar1=1.0), [gtmp], [gtmp])
                        S.op(DVE, lambda: V.reciprocal(out=gtmp[0:np_, :], in_=gtmp[0:np_, :]), [gtmp], [gtmp])
                        S.op(DVE, lambda: V.tensor_tensor(out=rbt[0:np_, t, :], in0=rbt[0:np_, t, :], in1=gtmp[0:np_, :], op=ALU.mult),
                             [rbt, gtmp], [rbt])

                    evs.append((ev_k, ev_v, ev_kb, ev_vb, ev_rb))

                bk = 0
                fproj(AL, 16, alT, alT[:, 0:ntok], bk); bk = (bk + 1) % 6
                for c in range(2):
                    fproj(QB + c * 128, 128, qTb, qTb[:, c, 0:ntok], bk); bk = (bk + 1) % 6
                for c in range(2):
                    fproj(KB + c * 128, 128, kTb, kTb[:, c, 0:ntok], bk); bk = (bk + 1) % 6
                for t in range(nt):
                    ev_k, ev_v, ev_kb, ev_vb, ev_rb = evs[t]
                    tproj(t, KB, 256, bk, ev_kb); bk = (bk + 1) % 6
                    tproj(t, VB, 512, bk, ev_vb); bk = (bk + 1) % 6
                    tproj(t, RB, 512, bk, ev_rb); bk = (bk + 1) % 6

                def gla_all():
                    gens = [gla_chunk(t, np_, sample) for t in range(nt)]
                    state_done = [False] * nt
                    waiting = [False] * nt
                    started = min(2, nt)
                    active = list(range(started))
                    while active:
                        for t in list(active):
                            if waiting[t]:
                                if t > 0 and not state_done[t - 1]:
                                    continue
                                waiting[t] = False
                            try:
                                r = next(gens[t])
                            except StopIteration:
                                active.remove(t)
                                if started < nt:
                                    active.append(started)
                                    started += 1
                                continue
                            if r == "need_state":
                                waiting[t] = True
                            elif r == "state_done":
                                state_done[t] = True
                            yield
                gla = gla_all()

                def pump(n):
                    for _ in range(n):
                        if next(gla, "done") == "done":
                            return

                npump = 0 if sample else DBG["npump"]
                for c in range(4):
                    fproj(QA + c * 128, 128, qTa, qTa[:, c, 0:ntok], bk); bk = (bk + 1) % 6
                    pump(npump)
                kcol0 = (g_idx * G) if not sample else 0
                for c in range(4):
                    fproj(KA + c * 128, 128, kTa_cur, kTa_cur[:, c, kcol0:kcol0 + ntok], bk); bk = (bk + 1) % 6
                    pump(npump)
                for t in range(nt):
                    ev_k, ev_v, ev_kb, ev_vb, ev_rb = evs[t]
                    tproj(t, KA, 512, bk, ev_k); bk = (bk + 1) % 6
                    pump(npump)
                    tproj(t, VA, 512, bk, ev_v); bk = (bk + 1) % 6
                    pump(npump)

                if not sample:
                    attention_prompt(g_idx, gla, nxt)
                else:
                    attention_sample(gla)
                    for _ in gla:
                        pass
                if DBG["stage"] < 5:
                    return
                for t in range(nt):
                    transpose_tile(mixed, lambda c, t=t: mixed[0:np_, t, c * 128:(c + 1) * 128], np_, mixT,
                                   mixT[:, :, t * np_:(t + 1) * np_], 8, t % 2)

                def wo_mm(t):
                    S.dma(SP, xr[t % 2][0:np_, :], x_src_fn(t), [], [xr[t % 2]])
                    for hf in range(2):
                        pw = ps[(2 * t + hf) % 6]
                        for kc in range(8):
                            S.op(PE, lambda kc=kc, hf=hf, pw=pw: P.matmul(pw[0:np_, :], lhsT=mixT[:, kc, t * np_:(t + 1) * np_],
                                                                          rhs=wo[:, kc, hf * 512:(hf + 1) * 512],
                                                                          start=(kc == 0), stop=(kc == 7)),
                                 [wo, mixT], [pw], inc=(kc == 7))

                def wo_epi(t):
                    for hf in range(2):
                        pw = ps[(2 * t + hf) % 6]
                        S.op(ACT, lambda hf=hf, pw=pw: A.activation(out=junk[0:np_, hf * 512:(hf + 1) * 512], in_=pw[0:np_, :],
                                                                    func=AF.Square, accum_out=ssq[0:np_, 4 + hf:5 + hf]),
                             [pw], [junk, ssq])
                    S.op(DVE, lambda: V.tensor_tensor(out=ssq[0:np_, 6:7], in0=ssq[0:np_, 4:5], in1=ssq[0:np_, 5:6], op=ALU.add),
                         [ssq], [ssq])
                    rms_rstd(ssq[0:np_, 6:7], rstd[0:np_, 6:7], np_, lnt[0:np_, 6:7], ssq, rstd, lnt, D)
                    for hf in range(2):
                        pw = ps[(2 * t + hf) % 6]
                        S.op(DVE, lambda hf=hf, pw=pw: V.scalar_tensor_tensor(out=ytmp[0:np_, hf * 512:(hf + 1) * 512], in0=pw[0:np_, :],
                                                                              scalar=rstd[0:np_, 6:7],
                                                                              in1=gpost[0:np_, hf * 512:(hf + 1) * 512],
                                                                              op0=ALU.mult, op1=ALU.mult),
                             [pw, rstd, gpost], [ytmp])
                    xr_ = xr[t % 2]
                    S.op(DVE, lambda: V.tensor_tensor(out=xr_[0:np_, :], in0=ytmp[0:np_, :], in1=xr_[0:np_, :], op=ALU.add),
                         [ytmp, xr_], [xr_])
                    S.dma(SP, x1_dst_fn(t), xr_[0:np_, :], [xr_], [x1buf], sembuf=xgo[t % 2])

                wo_mm(0)
                for t in range(nt):
                    if t + 1 < nt:
                        wo_mm(t + 1)
                    wo_epi(t)

            def attention_prompt(g_idx, gla, nxt=None):
                nblk = 4 * g_idx + 4
                blocks = [(pr, jb) for pr in range(4) for jb in range(nblk)]
                stb = [ps[4], ps[5], ps[0], ps[1]]
                n_gla = 4 * 52 - 48
                per_blk = DBG["perblk"] or -(-n_gla // len(blocks))

                def geo(jb):
                    d = 4 * g_idx - jb
                    return d, 128 * max(0, -d)

                def A_(b):
                    pr, jb = blocks[b]
                    d, q0 = geo(jb)
                    for e in range(2):
                        st = stb[2 * (b % 2) + e]
                        r0 = 64 * e
                        S.op(PE, lambda st=st, r0=r0: P.matmul(st[:, q0:G], lhsT=kTa[r0:r0 + 64, pr, jb * 128:(jb + 1) * 128],
                                                               rhs=qTa[r0:r0 + 64, pr, q0:G], start=True, stop=True),
                             [kTa, qTa], [st])

                def B_(b):
                    pr, jb = blocks[b]
                    d, q0 = geo(jb)
                    for e in range(2):
                        st = stb[2 * (b % 2) + e]
                        pt = pT[2 * (b % 2) + e]
                        S.op(ACT, lambda st=st, pt=pt: A.activation(out=pt[:, q0:G], in_=st[:, q0:G], func=AF.Exp, scale=0.125),
                             [st], [pt])
                    for e in range(2):
                        pt = pT[2 * (b % 2) + e]
                        S.op(DVE, lambda pt=pt: V.tensor_tensor(
                            out=pt[:, q0:G], in0=pt[:, q0:G],
                            in1=amask[:, d + 3:d + 7, :].rearrange("p a b -> p (a b)")[:, q0:G], op=ALU.mult), [pt, amask], [pt])

                def C_(b):
                    pr, jb = blocks[b]
                    d, q0 = geo(jb)
                    ibs = list(range(max(0, -d), 4))
                    for e in range(2):
                        h = 2 * pr + e
                        oacc = ps[2 + e]
                        pt = pT[2 * (b % 2) + e]
                        for ib in ibs:
                            S.op(PE, lambda ib=ib, oacc=oacc, pt=pt, h=h: P.matmul(
                                oacc[:, ib * 65:(ib + 1) * 65], lhsT=pt[:, ib * 128:(ib + 1) * 128], rhs=vaug[:, jb, h, :],
                                start=(jb == 0 and ib == ibs[0]), stop=(jb == 4 * g_idx + ib), skip_group_check=True),
                                 [pt, vaug], [oacc], inc=(ib == ibs[-1]))
                    if jb == nblk - 1:
                        for e in range(2):
                            h = 2 * pr + e
                            oacc = ps[2 + e]
                            o3 = oacc[:, 0:260].rearrange("p (i e) -> p i e", i=4)
                            rd = rden[e]
                            S.op(DVE, lambda o3=o3, rd=rd: V.reciprocal(out=rd[:, :], in_=o3[:, :, 64]), [oacc], [rd])
                            S.op(DVE, lambda o3=o3, rd=rd, h=h: V.tensor_tensor(
                                out=mixed[:, :, h * 64:(h + 1) * 64], in0=o3[:, :, 0:64],
                                in1=rd[:, :].unsqueeze(2).broadcast_to([128, 4, 64]), op=ALU.mult), [oacc, rd], [mixed])

                nbk = len(blocks)
                fpos = {(nbk * (2 * t + 1)) // 9: t for t in range(4)} if nxt is not None else {}
                A_(0)
                for b in range(nbk):
                    if b + 1 < nbk:
                        A_(b + 1)
                    B_(b)
                    C_(b)
                    if b in fpos:
                        front_a(nxt(fpos[b]), 128, fpos[b])
                    for _ in range(per_blk):
                        if next(gla, "done") == "done":
                            break
                for _ in gla:
                    pass

            def attention_sample(gla):
                oaccs = [ps[2], ps[3]]
                sts = [ps[4], ps[5]]
                first = [True, True]

                def scores(nk, kT_t, kT_fn, mask_t, mask_ap, slot):
                    for h in range(HA):
                        c, r0 = h // 2, 64 * (h % 2)
                        st = sts[h % 2]
                        S.op(PE, lambda c=c, r0=r0, st=st: P.matmul(st[0:nk, c * NST:(c + 1) * NST], lhsT=kT_fn(c, r0),
                                                                    rhs=qTa[r0:r0 + 64, c, 0:NST], start=True, stop=True),
                             [kT_t, qTa], [st], inc=(h >= HA - 2))
                    pt = pTs[slot]
                    for hh in range(2):
                        S.op(ACT, lambda hh=hh: A.activation(
                            out=pt[0:nk, :, :].rearrange("p (c hh) t -> p c hh t", hh=2)[:, :, hh, :],
                            in_=sts[hh][0:nk, 0:4 * NST].rearrange("p (c t) -> p c t", c=4), func=AF.Exp, scale=0.125),
                             [sts[hh]], [pt])
                    S.op(DVE, lambda: V.tensor_tensor(out=pt[0:nk, :, :], in0=pt[0:nk, :, :],
                                                      in1=mask_ap.unsqueeze(1).broadcast_to([nk, HA, NST]), op=ALU.mult),
                         [pt, mask_t], [pt])

                def pv(nk, v_t, v_fn, slot, is_last):
                    pt = pTs[slot]
                    for h in range(HA):
                        oa = oaccs[h // 4]
                        col = (h % 4) * 65
                        S.op(PE, lambda h=h, oa=oa, col=col, fl=first[h // 4]: P.matmul(
                            oa[0:NST, col:col + 65], lhsT=pt[0:nk, h, :], rhs=v_fn(h), start=fl, stop=is_last,
                            skip_group_check=True), [pt, v_t], [oa], inc=(h % 4 == 3))
                        first[h // 4] = False

                S.op(POOL, lambda: GP.memset(vaugc[0][:, :, 64:65], 1.0), [], [vaugc[0]])
                S.op(POOL, lambda: GP.memset(vaugc[1][:, :, 64:65], 1.0), [], [vaugc[1]])
                cblocks = [(sq, jb) for sq in range(NSS) for jb in range(16)]

                def load_d(m):
                    sq, jb = cblocks[2 * m]
                    sl2 = m % 2
                    S.dma(POOL, kcb[sl2][:], ck[sq, jb * 128:(jb + 2) * 128, :].rearrange("(j p) f -> p j f", p=128), [], [kcb[sl2]])
                    S.dma(POOL, vcb[sl2][:], cv[sq, jb * 128:(jb + 2) * 128, :].rearrange("(j p) f -> p j f", p=128), [], [vcb[sl2]])

                def prep(n):
                    sl2, j = (n // 2) % 2, n % 2
                    sl = n % 2
                    kb_, vb_, kt_, va_ = kcb[sl2], vcb[sl2], kTc[sl], vaugc[sl]
                    transpose_tile(kb_, lambda c, kb_=kb_, j=j: kb_[:, j, c * 128:(c + 1) * 128], 128, kt_, kt_[:, :, :], 4, 2 + sl)
                    S.op(DVE, lambda vb_=vb_, va_=va_, j=j: V.tensor_copy(out=va_[:, :, 0:64],
                                                                          in_=vb_[:, j, :].rearrange("p (h e) -> p h e", h=HA)),
                         [vb_], [va_])

                def sc_(n):
                    sq, jb = cblocks[n]
                    kt_ = kTc[n % 2]
                    scores(128, kt_, lambda c, r0, kt_=kt_: kt_[r0:r0 + 64, c, :], smask, smask[:, sq * 16 + jb, :], n % 2)

                def pv_(n):
                    va_ = vaugc[n % 2]
                    pv(128, va_, lambda h, va_=va_: va_[:, h, :], n % 2, False)

                nb_ = len(cblocks)
                nd_ = nb_ // 2
                load_d(0)
                load_d(1)
                prep(0)
                prep(1)
                sc_(0)
                for n in range(nb_):
                    if n + 1 < nb_:
                        sc_(n + 1)
                    pv_(n)
                    if n + 2 < nb_:
                        prep(n + 2)
                    if n % 2 == 1 and (n // 2) + 2 < nd_:
                        load_d(n // 2 + 2)
                    next(gla, None)
                    next(gla, None)
                scores(NST, kTa_cur, lambda c, r0: kTa_cur[r0:r0 + 64, c, 0:NST], nmask, nmask[:, :], 0)
                pv(NST, vaug_cur, lambda h: vaug_cur[0:NST, 0, h, :], 0, True)
                for half in range(2):
                    oa = oaccs[half]
                    o3 = oa[0:NST, 0:260].rearrange("p (i e) -> p i e", i=4)
                    S.op(DVE, lambda o3=o3: V.reciprocal(out=rden[0][0:NST, :], in_=o3[:, :, 64]), [oa], [rden[0]])
                    S.op(DVE, lambda o3=o3, half=half: V.tensor_tensor(
                        out=mixed[0:NST, 0, half * 256:(half + 1) * 256].rearrange("p (i e) -> p i e", i=4),
                        in0=o3[:, :, 0:64], in1=rden[0][0:NST, :].unsqueeze(2).broadcast_to([NST, 4, 64]), op=ALU.mult),
                         [oa, rden[0]], [mixed])

            def gla_chunk(t, np_, sample):
                sl = t % 2
                bank = ps[6 + sl]
                lz, ebT, enbT, qeTd, keT, kl, attm = lzs[sl], ebTs[sl], enbTs[sl], qeTds[sl], keTs[sl], klbs[sl], attms[sl]
                oss, olt, orstd, osq = osss[sl], olts[sl], orstds[sl], osqs[sl]
                ez = ekl = lz
                tok = slice(t * np_, (t + 1) * np_)
                U_inc, U_aft, CA = (suinc, suaft, scaus) if sample else (uinc, uaft, caus)
                pz = bank
                S.op(PE, lambda: P.matmul(pz[0:np_, 0:256], lhsT=alT[0:16, tok], rhs=wa2[0:16, :], start=True, stop=False),
                     [alT, wa2], [pz], inc=False)
                S.op(PE, lambda: P.matmul(pz[0:np_, 0:256], lhsT=ones_r[0:1, 0:np_], rhs=bar[0:1, :], start=False, stop=True),
                     [ones_r, bar], [pz])
                yield
                S.op(ACT, lambda: A.activation(out=ez[0:np_, :], in_=pz[0:np_, 0:256], func=AF.Exp, scale=-1.0), [pz], [ez])
                yield
                S.op(ACT, lambda: A.activation(out=lz[0:np_, :], in_=ez[0:np_, :], func=AF.Ln, bias=oneb[0:np_, 0:1]), [ez, oneb], [lz])
                yield
                pb = bank
                for kc in range(2):
                    S.op(PE, lambda kc=kc: P.matmul(pb[:, kc * 128:kc * 128 + np_], lhsT=lz[0:np_, kc * 128:(kc + 1) * 128],
                                                    rhs=U_inc[0:np_, 0:np_], start=True, stop=True),
                         [lz, U_inc], [pb], inc=False)
                pa = bank
                S.op(PE, lambda: P.matmul(pa[0:np_, 256:512], lhsT=U_aft[0:np_, 0:np_], rhs=lz[0:np_, :], start=True, stop=True),
                     [lz, U_aft], [pa])
                yield
                pb3 = pb[:, 0:256].rearrange("p (c i) -> p c i", c=2)[:, :, 0:np_]
                S.op(ACT, lambda: A.activation(out=ebT[:, :, 0:np_], in_=pb3, func=AF.Exp), [pb], [ebT])
                yield
                S.op(ACT, lambda: A.activation(out=enbT[:, :, 0:np_], in_=pb3, func=AF.Exp, scale=-1.0), [pb], [enbT])
                yield
                S.op(ACT, lambda: A.activation(out=ekl[0:np_, :], in_=pa[0:np_, 256:512], func=AF.Exp), [pa], [ekl])
                yield
                for hh in range(2):
                    r0 = 64 * hh
                    S.op(DVE, lambda hh=hh, r0=r0: V.scalar_tensor_tensor(
                        out=qeTd[r0:r0 + 64, :, hh, 0:np_], in0=qTb[r0:r0 + 64, :, tok], scalar=0.125,
                        in1=ebT[r0:r0 + 64, :, 0:np_], op0=ALU.mult, op1=ALU.mult), [qTb, ebT], [qeTd])
                    yield
                S.op(DVE, lambda: V.tensor_tensor(out=keT[:, :, 0:np_], in0=kTb[:, :, tok], in1=enbT[:, :, 0:np_], op=ALU.mult),
                     [kTb, enbT], [keT])
                yield
                S.op(DVE, lambda: V.tensor_tensor(out=kl[0:np_, :], in0=kbt[0:np_, t, :], in1=ekl[0:np_, :], op=ALU.mult),
                     [kbt, ekl], [kl])
                yield
                pat = bank
                if np_ == 128:
                    for c in range(2):
                        S.op(PE, lambda c=c: P.matmul(pat[:, c * 256:(c + 1) * 256], lhsT=keT[:, c, :],
                                                      rhs=qeTd[:, c, :, :].rearrange("p hh i -> p (hh i)"), start=True, stop=True),
                             [keT, qeTd], [pat], inc=(c == 1))
                else:
                    for h in range(4):
                        S.op(PE, lambda h=h: P.matmul(pat[0:np_, h * 128:h * 128 + np_], lhsT=keT[:, h // 2, 0:np_],
                                                      rhs=qeTd[:, h // 2, h % 2, 0:np_], start=True, stop=True),
                             [keT, qeTd], [pat], inc=(h == 3))
                yield
                S.op(DVE, lambda: V.tensor_tensor(
                    out=attm[0:np_, :, 0:np_], in0=pat[0:np_, :].rearrange("p (h i) -> p h i", h=4)[:, :, 0:np_],
                    in1=CA[0:np_, 0:np_].unsqueeze(1).broadcast_to([np_, 4, np_]), op=ALU.mult), [pat, CA], [attm])
                yield
                yield "need_state"
                po = bank
                if not sample:
                    for h in range(4):
                        c, r0 = h // 2, 64 * (h % 2)
                        S.op(PE, lambda h=h: P.matmul(po[0:np_, h * 128:(h + 1) * 128], lhsT=attm[0:np_, h, 0:np_],
                                                      rhs=vbt[0:np_, t, h * 128:(h + 1) * 128], start=True, stop=False),
                             [attm, vbt], [po], inc=False)
                        S.op(PE, lambda h=h, c=c, r0=r0: P.matmul(po[0:np_, h * 128:(h + 1) * 128], lhsT=qeTd[r0:r0 + 64, c, h % 2, 0:np_],
                                                                  rhs=Sbf[r0:r0 + 64, c, :], start=False, stop=True),
                             [qeTd, Sbf], [po], inc=(h == 3))
                        yield
                else:
                    for sq in range(NSS):
                        S.op(DVE, lambda sq=sq: V.tensor_tensor(out=qeTs[:, sq, :, :],
                                                                in0=qeTd[:, :, :, 0:NST].rearrange("p c hh i -> p (c hh) i"),
                                                                in1=ssel[:, sq, :].unsqueeze(1).broadcast_to([128, 4, NST]),
                                                                op=ALU.mult), [qeTd, ssel], [qeTs])
                        yield
                    for h in range(4):
                        c, r0 = h // 2, 64 * (h % 2)
                        S.op(PE, lambda h=h: P.matmul(po[0:NST, h * 128:(h + 1) * 128], lhsT=attm[:, h, 0:NST],
                                                      rhs=vbt[:, t, h * 128:(h + 1) * 128], start=True, stop=False),
                             [attm, vbt], [po], inc=False)
                        for sq in range(NSS):
                            S.op(PE, lambda h=h, c=c, r0=r0, sq=sq: P.matmul(
                                po[0:NST, h * 128:(h + 1) * 128], lhsT=qeTs[r0:r0 + 64, sq, h, :], rhs=Sbf_s[r0:r0 + 64, sq, c, :],
                                start=False, stop=(sq == NSS - 1)), [qeTs, Sbf_s], [po], inc=(h == 3 and sq == NSS - 1))
                        yield
                if DBG["glacopy"] == "act":
                    S.op(ACT, lambda: A.activation(out=osq[0:np_, :], in_=po[0:np_, :], func=AF.Copy), [po], [osq])
                else:
                    S.op(DVE, lambda: V.tensor_copy(out=osq[0:np_, :], in_=po[0:np_, :]), [po], [osq])
                yield
                pd = bank
                if not sample:
                    for c in range(2):
                        S.op(PE, lambda c=c: P.matmul(pd[:, c * 256:(c + 1) * 256], lhsT=kl[0:np_, c * 128:(c + 1) * 128],
                                                      rhs=vbt[0:np_, t, c * 256:(c + 1) * 256], start=True, stop=True),
                             [kl, vbt], [pd], inc=(c == 1))
                    yield
                    for c in range(2):
                        for hh in range(2):
                            r0 = 64 * hh
                            S.op(DVE, lambda c=c, hh=hh, r0=r0: V.scalar_tensor_tensor(
                                out=Sst[r0:r0 + 64, c, :], in0=Sst[r0:r0 + 64, c, :], scalar=ebT[r0:r0 + 64, c, np_ - 1:np_],
                                in1=pd[r0:r0 + 64, c * 256 + hh * 128:c * 256 + (hh + 1) * 128], op0=ALU.mult, op1=ALU.add),
                                 [Sst, ebT, pd], [Sst])
                            yield
                    if DBG["glacopy"] == "act":
                        S.op(ACT, lambda: A.activation(out=Sbf[:], in_=Sst[:], func=AF.Copy), [Sst], [Sbf])
                    else:
                        S.op(DVE, lambda: V.tensor_copy(out=Sbf[:], in_=Sst[:]), [Sst], [Sbf])
                    yield "state_done"
                else:
                    for sq in range(NSS):
                        S.op(DVE, lambda sq=sq: V.tensor_scalar(out=kls[0:NST, :], in0=kl[0:NST, :], scalar1=srow[0:NST, sq:sq + 1],
                                                                scalar2=None, op0=ALU.mult), [kl, srow], [kls])
                        yield
                        for c in range(2):
                            S.op(PE, lambda c=c: P.matmul(pd[:, c * 256:(c + 1) * 256], lhsT=kls[0:NST, c * 128:(c + 1) * 128],
                                                          rhs=vbt[0:NST, t, c * 256:(c + 1) * 256], start=True, stop=True),
                                 [kls, vbt], [pd], inc=(c == 1))
                        yield
                        for c in range(2):
                            for hh in range(2):
                                r0 = 64 * hh
                                S.op(DVE, lambda c=c, hh=hh, r0=r0, sq=sq: V.scalar_tensor_tensor(
                                    out=Sst_s[r0:r0 + 64, sq, c, :], in0=Sst_s[r0:r0 + 64, sq, c, :],
                                    scalar=ebT[r0:r0 + 64, c, TS * sq + TS - 1:TS * sq + TS],
                                    in1=pd[r0:r0 + 64, c * 256 + hh * 128:c * 256 + (hh + 1) * 128], op0=ALU.mult, op1=ALU.add),
                                     [Sst_s, ebT, pd], [Sst_s])
                                yield
                for h in range(4):
                    S.op(ACT, lambda h=h: A.activation(out=lz[0:np_, 0:128], in_=osq[0:np_, h * 128:(h + 1) * 128], func=AF.Square,
                                                       accum_out=oss[0:np_, h:h + 1]), [osq], [lz, oss])
                    yield
                rms_rstd(oss[0:np_, :], orstd[0:np_, :], np_, olt[0:np_, :], oss, orstd, olt, 128)
                yield
                for h in range(4):
                    S.op(DVE, lambda h=h: V.scalar_tensor_tensor(out=mixed[0:np_, t, 512 + h * 128:512 + (h + 1) * 128],
                                                                 in0=osq[0:np_, h * 128:(h + 1) * 128], scalar=orstd[0:np_, h:h + 1],
                                                                 in1=rbt[0:np_, t, h * 128:(h + 1) * 128], op0=ALU.mult, op1=ALU.mult),
                         [osq, orstd, rbt], [mixed])
                    yield

            x1buf = Buf("x1dram")
            xgo = [Buf("xgo0"), Buf("xgo1")]
            with ExitStack() as esp:
                amask = T(S, esp, "amask", [128, 19, 128], BF16)
                S.dma(SP, amask[:], c_amask[:, :, :], [], [amask])
                uinc = T(S, esp, "uinc", [128, 128])
                S.dma(SP, uinc[:], c_uinc[:, :], [], [uinc])
                uaft = T(S, esp, "uaft", [128, 128])
                S.dma(SP, uaft[:], c_uaft[:, :], [], [uaft])
                caus = T(S, esp, "caus", [128, 128], BF16)
                S.dma(SP, caus[:], c_caus[:, :], [], [caus])
                kTa = T(S, esp, "kTa", [128, 4, SEQ], BF16)
                vaug = T(S, esp, "vaug", [128, 16, HA, 65], BF16)
                S.op(POOL, lambda: GP.memset(vaug[:, :, :, 64:65], 1.0), [], [vaug])
                Sst = T(S, esp, "Sst", [128, 2, 128])
                Sbf = T(S, esp, "Sbf", [128, 2, 128], BF16)
                kTa_cur, vaug_cur = kTa, vaug
                glist = [(sq_, g_) for sq_ in range(DBG["nseq"]) for g_ in range(DBG["ngrp"])]

                def xsrc(sq_, g_):
                    return lambda t: xp[sq_, g_ * G + t * 128:g_ * G + (t + 1) * 128, :]
                for gi, (sq_, g_) in enumerate(glist):
                    if g_ == 0:
                        S.op(DVE, lambda: V.memset(Sst[:], 0.0), [], [Sst])
                        S.op(DVE, lambda: V.memset(Sbf[:], 0.0), [], [Sbf])
                    tok0 = g_ * G
                    nxt = xsrc(*glist[gi + 1]) if gi + 1 < len(glist) else None
                    mixer_group(4, 128, xsrc(sq_, g_), sq_, g_, g_ == 0,
                                lambda t, s=sq_, tok0=tok0: wk_p[s, tok0 + t * 128:tok0 + (t + 1) * 128, :],
                                lambda t, s=sq_, tok0=tok0: wv_p[s, tok0 + t * 128:tok0 + (t + 1) * 128, :],
                                lambda t, s=sq_, tok0=tok0: x1_p[s, tok0 + t * 128:tok0 + (t + 1) * 128, :],
                                front_done=(gi > 0), nxt=nxt)
                    if g_ == DBG["ngrp"] - 1:
                        for c in range(2):
                            for hh in range(2):
                                store(POOL, gs_p[sq_, 2 * c + hh, :, :], Sst[64 * hh:64 * hh + 64, c, :], Sst)

            S.barrier()
            if DBG["sample"]:
                ess = es.enter_context(ExitStack())
                smask = T(S, ess, "smask", [128, NSS * 16, NST], BF16)
                S.dma(SP, smask[:], c_smask[:, :, :], [], [smask])
                nmask = T(S, ess, "nmask", [NST, NST], BF16)
                S.dma(SP, nmask[:], c_nmask[:, :], [], [nmask])
                suinc = T(S, ess, "suinc", [NST, NST])
                S.dma(SP, suinc[:], c_suinc[:, :], [], [suinc])
                suaft = T(S, ess, "suaft", [NST, NST])
                S.dma(SP, suaft[:], c_suaft[:, :], [], [suaft])
                scaus = T(S, ess, "scaus", [NST, NST], BF16)
                S.dma(SP, scaus[:], c_scaus[:, :], [], [scaus])
                ssel = T(S, ess, "ssel", [128, NSS, NST])
                S.dma(SP, ssel[:], c_ssel[:, :, :], [], [ssel])
                srow = T(S, ess, "srow", [NST, NSS])
                S.dma(SP, srow[:], c_srow[:, :], [], [srow])
                Sst_s = T(S, ess, "Sst_s", [128, NSS, 2, 128])
                Sbf_s = T(S, ess, "Sbf_s", [128, NSS, 2, 128], BF16)
                qeTs = T(S, ess, "qeTs", [128, NSS, 4, NST], BF16)
                kls = T(S, ess, "kls", [NST, 256], BF16)
                kcb = [T(S, ess, "kcb%d" % i, [128, 2, 512], BF16) for i in range(2)]
                vcb = [T(S, ess, "vcb%d" % i, [128, 2, 512], BF16) for i in range(2)]
                kTc = [T(S, ess, "kTc%d" % i, [128, 4, 128], BF16) for i in range(2)]
                vaugc = [T(S, ess, "vaugc%d" % i, [128, HA, 65], BF16) for i in range(2)]
                pTs = [T(S, ess, "pTs%d" % i, [128, HA, NST], BF16) for i in range(2)]
                kTs = T(S, ess, "kTs", [128, 4, NST], BF16)
                vaug_s = T(S, ess, "vaug_s", [128, 1, HA, 65], BF16)
                S.op(POOL, lambda: GP.memset(vaug_s[:, :, :, 64:65], 1.0), [], [vaug_s])
                kTa_cur, vaug_cur = kTs, vaug_s
                S.op(DVE, lambda: V.memset(attms[0][:], 0.0), [], [attms[0]])
                S.op(DVE, lambda: V.memset(vbt[:], 0.0), [], [vbt])
                for sq in range(NSS):
                    for c in range(2):
                        for hh in range(2):
                            S.dma(SP, Sst_s[64 * hh:64 * hh + 64, sq, c, :], sg[sq, 2 * c + hh, :, :], [], [Sst_s])
                S.op(ACT, lambda: A.activation(out=Sbf_s[:], in_=Sst_s[:], func=AF.Copy), [Sst_s], [Sbf_s])
                mixer_group(1, NST, lambda t: xs[0:NST, :], 0, 0, True,
                            lambda t: wk_s[0:NST, :], lambda t: wv_s[0:NST, :], lambda t: x1_s[0:NST, :], sample=True)
                for sq in range(NSS):
                    for c in range(2):
                        for hh in range(2):
                            store(POOL, gs_s[sq, 2 * c + hh, :, :], Sst_s[64 * hh:64 * hh + 64, sq, c, :], Sst_s)

        S.barrier()
        with ExitStack() as es:
            wup = T(S, es, "wup", [128, 8, 2 * DFF], BF16)
            wdn = T(S, es, "wdn", [128, NFC, D], BF16)
            class _Blk2:
                def __init__(self, name):
                    self.buf = Buf(name)
            w_up3 = w_up.rearrange("(kc p) n -> p kc n", p=128)
            wupg = [_Blk2("wupg%d" % i) for i in range(6)]
            wupu = [_Blk2("wupu%d" % i) for i in range(6)]
            for i in range(6):
                ncol = min(512, DFF - i * 512)
                S.dma(POOL, wup[:, :, i * 512:i * 512 + ncol], w_up3[:, :, i * 512:i * 512 + ncol], [], [wupg[i]])
                S.dma(POOL, wup[:, :, DFF + i * 512:DFF + i * 512 + ncol], w_up3[:, :, DFF + i * 512:DFF + i * 512 + ncol], [], [wupu[i]])
            for fc in range(NFC):
                S.dma(POOL, wdn[:, fc, :], w_down[fc * 128:(fc + 1) * 128, :], [], [wdn])
            gpre2 = T(S, es, "gpre2", [128, D])
            S.dma(SP, gpre2[:], g_pre_ffn[0:1, :].partition_broadcast(128), [], [gpre2])
            gpost2 = T(S, es, "gpost2", [128, D])
            S.dma(SP, gpost2[:], g_post_ffn[0:1, :].partition_broadcast(128), [], [gpost2])
            cw = T(S, es, "cw", [128, 4, NFC])
            cwt = T(S, es, "cwt", [NFC, 4, 128])
            id32 = T(S, es, "id32", [32, 32])
            S.dma(SP, id32[:], c_id32[:, :], [], [id32])
            S.dma(SP, cwt[:, 0:3, :], conv_w.rearrange("j (c p) -> c j p", p=128), [], [cwt])
            S.dma(SP, cwt[:, 3, :], conv_b[0, :].rearrange("(c p) -> c p", p=128), [], [cwt])
            for j in range(4):
                S.op(PE, lambda j=j: P.matmul(ps[0][:, j * 32:j * 32 + NFC], lhsT=cwt[0:NFC, j, :], rhs=id32[0:NFC, 0:NFC],
                                              start=True, stop=True), [cwt, id32], [ps[0]], inc=(j == 3))
            S.op(ACT, lambda: A.activation(out=cw[:], in_=ps[0][:, 0:128].rearrange("p (j c) -> p j c", j=4)[:, :, 0:NFC],
                                           func=AF.Copy), [ps[0]], [cw])

            xs1 = [T(S, es, "xs1_%d" % i, [128, D]) for i in range(2)]
            xr = [T(S, es, "xr_%d" % i, [128, D]) for i in range(2)]
            xnbs = [T(S, es, "xnb2_%d" % i, [128, D], BF16) for i in range(2)]
            st_ss = [T(S, es, "pss%d" % i, [128, 1]) for i in range(2)]
            st_ln = [T(S, es, "pln%d" % i, [128, 1]) for i in range(2)]
            st_rs = [T(S, es, "prs%d" % i, [128, 1]) for i in range(2)]
            ssq = T(S, es, "ssq2", [128, 8])
            lnt = T(S, es, "lnt2", [128, 8])
            rstd = T(S, es, "rstd2", [128, 8])
            xnT = T(S, es, "xnT2", [128, 8, G], BF16)
            hT = T(S, es, "hT", [128, NFC, G], BF16)

            class _Sub:
                def __init__(self, fc):
                    self.buf = Buf("hT%d" % fc)
            hTs = [_Sub(fc) for fc in range(NFC)]
            junkb = T(S, es, "junkb", [128, 512], BF16)
            gbuf = [T(S, es, "gbuf%d" % i, [128, G + 2]) for i in range(1)] * 2
            cbuf = [T(S, es, "cbuf%d" % i, [128, G]) for i in range(1)] * 2
            gel = [T(S, es, "gel%d" % i, [128, G]) for i in range(1)] * 2
            carry = T(S, es, "carry", [128, NFC, 2])
            ytmp = [T(S, es, "ytmp2_%d" % i, [128, 512]) for i in range(2)]
            gl = [T(S, es, "gl%d" % i, [NST, 512]) for i in range(1)] * 2
            schist = T(S, es, "schist", [128, NFC, 8])
            gbs = T(S, es, "gbs", [128, NSS, TS + 2])
            id8 = T(S, es, "id8", [8, 8])
            S.dma(SP, id8[:], c_id8[:, :], [], [id8])
            xgo2 = [Buf("xgo2_%d" % i) for i in range(2)]

            def prenorm_a(x_src, np_, slot):
                xs_, xb_ = xs1[slot], xnbs[slot]
                S.dma(SP, xs_[0:np_, :], x_src, [x1buf], [xs_])
                S.op(ACT, lambda: A.activation(out=xb_[0:np_, :], in_=xs_[0:np_, :], func=AF.Square,
                                               accum_out=st_ss[slot][0:np_, 0:1]), [xs_], [xb_, st_ss[slot]])
                rms_rstd(st_ss[slot][0:np_, 0:1], st_rs[slot][0:np_, 0:1], np_, st_ln[slot][0:np_, 0:1],
                         st_ss[slot], st_rs[slot], st_ln[slot], D)
                S.op(DVE, lambda: V.scalar_tensor_tensor(out=xb_[0:np_, :], in0=xs_[0:np_, :], scalar=st_rs[slot][0:np_, 0:1],
                                                         in1=gpre2[0:np_, :], op0=ALU.mult, op1=ALU.mult),
                     [xs_, st_rs[slot], gpre2], [xb_])

            def prenorm_b(t, np_, slot):
                xb_ = xnbs[slot]
                transpose_tile(xb_, lambda c: xb_[0:np_, c * 128:(c + 1) * 128], np_, xnT,
                               xnT[:, :, t * np_:(t + 1) * np_], 8, slot)

            def ffn_group(nt, np_, x_src_fn, y_dst_fn, first_in_seq, last_in_seq, fc_dst, sample=False, nxt=None):
                ntok = nt * np_
                if sample:
                    for n in range(6):
                        ncol = min(512, DFF - n * 512)
                        glt = gl[0]
                        S.dma(SP, glt[0:8, 0:ncol], sc[:, n * 512:n * 512 + ncol], [], [glt])
                        nf = ncol // 128
                        for j in range(nf):
                            S.op(PE, lambda j=j: P.matmul(ps[0][:, j * 8:(j + 1) * 8], lhsT=glt[0:8, j * 128:(j + 1) * 128],
                                                          rhs=id8[0:8, 0:8], start=True, stop=True), [glt, id8], [ps[0]],
                                 inc=(j == nf - 1))
                        S.op(ACT, lambda n=n, nf=nf: A.activation(out=schist[:, 4 * n:4 * n + nf, :],
                                                                  in_=ps[0][:, 0:nf * 8].rearrange("p (f r) -> p f r", r=8),
                                                                  func=AF.Copy), [ps[0]], [schist])
                if first_in_seq:
                    S.op(POOL, lambda: GP.memset(carry[:], 0.0), [], [carry])
                for fc in range(NFC):
                    pg, pu = ps[(2 * fc) % 6], ps[(2 * fc + 1) % 6]
                    gb, cb_, ge = gbuf[fc % 2], cbuf[fc % 2], gel[fc % 2]
                    for kc in range(8):
                        S.op(PE, lambda kc=kc: P.matmul(pg[:, 0:ntok], lhsT=wup[:, kc, fc * 128:(fc + 1) * 128],
                                                        rhs=xnT[:, kc, 0:ntok], start=(kc == 0), stop=(kc == 7)),
                             [wupg[fc // 4], xnT], [pg], inc=(kc == 7))
                    for kc in range(8):
                        S.op(PE, lambda kc=kc: P.matmul(pu[:, 0:ntok], lhsT=wup[:, kc, DFF + fc * 128:DFF + (fc + 1) * 128],
                                                        rhs=xnT[:, kc, 0:ntok], start=(kc == 0), stop=(kc == 7)),
                             [wupu[fc // 4], xnT], [pu], inc=(kc == 7))
                    if not sample:
                        S.op(POOL, lambda: GP.tensor_copy(out=gb[:, 0:2], in_=carry[:, fc, :]), [carry], [gb])
                        S.op(ACT, lambda: A.activation(out=gb[:, 2:2 + ntok], in_=pg[:, 0:ntok], func=AF.Copy), [pg], [gb])
                        S.op(POOL, lambda: GP.tensor_copy(out=carry[:, fc, :], in_=gb[:, ntok:ntok + 2]), [gb], [carry])
                        g2, g1, g0, co = gb[:, 2:2 + ntok], gb[:, 1:1 + ntok], gb[:, 0:ntok], cb_[:, 0:ntok]
                        gsrc = gb
                    else:
                        S.op(POOL, lambda: GP.tensor_copy(out=gbs[:, :, 0:2], in_=schist[:, fc, :].rearrange("p (s j) -> p s j", j=2)),
                             [schist], [gbs])
                        S.op(ACT, lambda: A.activation(out=gbs[:, :, 2:2 + TS], in_=pg[:, 0:NST].rearrange("p (s t) -> p s t", t=TS),
                                                       func=AF.Copy), [pg], [gbs])
                        g2, g1, g0 = gbs[:, :, 2:2 + TS], gbs[:, :, 1:1 + TS], gbs[:, :, 0:TS]
                        co = cb_[:, 0:NST].rearrange("p (s t) -> p s t", t=TS)
                        gsrc = gbs
                    S.op(DVE, lambda: V.tensor_scalar(out=co, in0=g2, scalar1=cw[:, 2, fc:fc + 1],
                                                      scalar2=cw[:, 3, fc:fc + 1], op0=ALU.mult, op1=ALU.add),
                         [gsrc, cw], [cb_])
                    S.op(DVE, lambda: V.scalar_tensor_tensor(out=co, in0=g1, scalar=cw[:, 1, fc:fc + 1],
                                                             in1=co, op0=ALU.mult, op1=ALU.add),
                         [gsrc, cw, cb_], [cb_])
                    S.op(DVE, lambda: V.scalar_tensor_tensor(out=co, in0=g0, scalar=cw[:, 0, fc:fc + 1],
                                                             in1=co, op0=ALU.mult, op1=ALU.add),
                         [gsrc, cw, cb_], [cb_])
                    S.op(ACT, lambda: A.activation(out=ge[:, 0:ntok], in_=cb_[:, 0:ntok], func=AF.Gelu), [cb_], [ge])
                    S.op(DVE, lambda: V.tensor_tensor(out=hT[:, fc, 0:ntok], in0=ge[:, 0:ntok], in1=pu[:, 0:ntok], op=ALU.mult),
                         [ge, pu], [hTs[fc]])
                if last_in_seq:
                    nrow = NST if sample else 2
                    for n in range(6):
                        ncol = min(512, DFF - n * 512)
                        pq = ps[n % 2]
                        for kc in range(8):
                            S.op(PE, lambda kc=kc: P.matmul(pq[0:nrow, 0:ncol], lhsT=xnT[:, kc, ntok - nrow:ntok],
                                                            rhs=wup[:, kc, n * 512:n * 512 + ncol], start=(kc == 0), stop=(kc == 7)),
                                 [wupg[n], xnT], [pq], inc=(kc == 7))
                        glt = gl[n % 2]
                        S.op(ACT, lambda: A.activation(out=glt[0:nrow, 0:ncol], in_=pq[0:nrow, 0:ncol], func=AF.Copy),
                             [pq], [glt])
                        if not sample:
                            store(POOL, fc_dst[:, n * 512:n * 512 + ncol], glt[0:2, 0:ncol], glt)
                        else:
                            for sq in range(NSS):
                                store(POOL, fc_s[sq, :, n * 512:n * 512 + ncol], glt[TS * sq + TS - 2:TS * sq + TS, 0:ncol], glt)

                def dn_mm(t):
                    for hf in range(2):
                        pdn = ps[(2 * t + hf) % 6]
                        for fc in range(NFC):
                            S.op(PE, lambda fc=fc, hf=hf, pdn=pdn: P.matmul(pdn[0:np_, :], lhsT=hT[:, fc, t * np_:(t + 1) * np_],
                                                                            rhs=wdn[:, fc, hf * 512:(hf + 1) * 512],
                                                                            start=(fc == 0), stop=(fc == NFC - 1)),
                                 [wdn, hTs[fc]], [pdn], inc=(fc == NFC - 1))

                def dn_epi(t):
                    xr_ = xr[t % 2]
                    for hf in range(2):
                        pdn = ps[(2 * t + hf) % 6]
                        S.op(ACT, lambda hf=hf, pdn=pdn: A.activation(out=junkb[0:np_, :], in_=pdn[0:np_, :],
                                                                      func=AF.Square, accum_out=ssq[0:np_, 4 + hf:5 + hf]),
                             [pdn], [junkb, ssq])
                    S.op(DVE, lambda: V.tensor_tensor(out=ssq[0:np_, 6:7], in0=ssq[0:np_, 4:5], in1=ssq[0:np_, 5:6], op=ALU.add),
                         [ssq], [ssq])
                    rms_rstd(ssq[0:np_, 6:7], rstd[0:np_, 6:7], np_, lnt[0:np_, 6:7], ssq, rstd, lnt, D)
                    for hf in range(2):
                        yt_ = ytmp[hf]
                        pdn = ps[(2 * t + hf) % 6]
                        S.op(DVE, lambda hf=hf, yt_=yt_, pdn=pdn: V.scalar_tensor_tensor(
                            out=yt_[0:np_, :], in0=pdn[0:np_, :], scalar=rstd[0:np_, 6:7],
                            in1=gpost2[0:np_, hf * 512:(hf + 1) * 512], op0=ALU.mult, op1=ALU.mult),
                             [pdn, rstd, gpost2], [yt_])
                        S.op(DVE, lambda hf=hf, yt_=yt_: V.tensor_tensor(out=xr_[0:np_, hf * 512:(hf + 1) * 512], in0=yt_[0:np_, :],
                                                                         in1=xr_[0:np_, hf * 512:(hf + 1) * 512], op=ALU.add),
                             [yt_, xr_], [xr_])
                    S.dma(SP, y_dst_fn(t), xr_[0:np_, :], [xr_], [], sembuf=xgo2[t % 2])
                    if xgo2[t % 2] not in out_bufs:
                        out_bufs.append(xgo2[t % 2])

                def dn_pre(t):
                    if nxt is not None:
                        prenorm_a(nxt(t), 128, t % 2)
                    S.dma(SP, xr[t % 2][0:np_, :], x_src_fn(t), [x1buf], [xr[t % 2]])

                if sample and nxt is not None:
                    for t4 in range(4):
                        prenorm_a(nxt(t4), 128, t4 % 2)
                        prenorm_b(t4, 128, t4 % 2)
                    nxt = None
                dn_pre(0)
                dn_mm(0)
                for t in range(nt):
                    if t + 1 < nt:
                        dn_pre(t + 1)
                        dn_mm(t + 1)
                    if nxt is not None:
                        prenorm_b(t, 128, t % 2)
                    dn_epi(t)

            groups = []
            for s in range(DBG["nseq"] if DBG["phaseB"] else 0):
                for g in range(DBG["ngrp"]):
                    tok0 = g * G
                    groups.append((lambda t, s=s, tok0=tok0: x1_p[s, tok0 + t * 128:tok0 + (t + 1) * 128, :],
                                   lambda t, s=s, tok0=tok0: y_p[s, tok0 + t * 128:tok0 + (t + 1) * 128, :],
                                   g == 0, g == DBG["ngrp"] - 1, fc_p[s, :, :]))
            if DBG["sample"] and DBG["phaseB"]:
                prenorm_a(x1_s[0:NST, :], NST, 0)
                prenorm_b(0, NST, 0)
                ffn_group(1, NST, lambda t: x1_s[0:NST, :], lambda t: y_s[0:NST, :], True, True, None, sample=True,
                          nxt=(groups[0][0] if groups else None))
            elif groups:
                for t in range(4):
                    prenorm_a(groups[0][0](t), 128, t % 2)
                    prenorm_b(t, 128, t % 2)
            for i, (xf, yf, fi, la, fcd) in enumerate(groups):
                ffn_group(4, 128, xf, yf, fi, la, fcd, nxt=(groups[i + 1][0] if i + 1 < len(groups) else None))

        for b in out_bufs:
            SP.h.wait_ge(b.dsem, b.dcnt)
    return nc


def _mult(delta):
    delta = np.asarray(delta)
    nn = delta >= 0
    m = (nn & (delta <= 128)).astype(np.float32)
    m += (nn & (delta <= 512) & (delta % 4 == 0))
    m += (nn & (delta <= 2048) & (delta % 16 == 0))
    return m


def _constants():
    bf = ml_dtypes.bfloat16
    c = {}
    c["c_ident"] = np.eye(128, dtype=np.float32).astype(bf)
    jj = np.arange(128)[:, None]
    ii = np.arange(128)[None, :]
    am = np.zeros((128, 19, 128), np.float32)
    for e in range(-3, 16):
        am[:, e + 3, :] = _mult(128 * e + ii - jj)
    c["c_amask"] = am.astype(bf)
    j = np.arange(128)[:, None]
    i = np.arange(128)[None, :]
    c["c_uinc"] = np.where(j <= i, -1.0 / 16.0, 0.0).astype(np.float32)
    c["c_uaft"] = np.where(j > i, -1.0 / 16.0, 0.0).astype(np.float32)
    c["c_caus"] = (j <= i).astype(np.float32).astype(bf)
    c["c_ones"] = np.ones((1, 128), np.float32)
    sm = np.zeros((128, NSS * 16, NST), np.float32)
    tt = np.arange(TS)[None, :]
    for s in range(NSS):
        for jb in range(16):
            sm[:, s * 16 + jb, s * TS:(s + 1) * TS] = _mult(SEQ + tt - (128 * jb + jj))
    c["c_smask"] = sm.astype(bf)
    tok = np.arange(NST)
    same = (tok[:, None] // TS) == (tok[None, :] // TS)
    dl = tok[None, :] - tok[:, None]
    c["c_nmask"] = (np.where(same, _mult(dl), 0.0)).astype(np.float32).astype(bf)
    c["c_suinc"] = np.where(same & (tok[:, None] <= tok[None, :]), -1.0 / 16.0, 0.0).astype(np.float32)
    c["c_suaft"] = np.where(same & (tok[:, None] > tok[None, :]), -1.0 / 16.0, 0.0).astype(np.float32)
    c["c_scaus"] = (same & (tok[:, None] <= tok[None, :])).astype(np.float32).astype(bf)
    ssel = np.zeros((128, NSS, NST), np.float32)
    srow = np.zeros((NST, NSS), np.float32)
    for s in range(NSS):
        ssel[:, s, s * TS:(s + 1) * TS] = 1.0
        srow[s * TS:(s + 1) * TS, s] = 1.0
    c["c_ssel"] = ssel
    c["c_srow"] = srow
    c["c_id8"] = np.eye(8, dtype=np.float32)
    c["c_id32"] = np.eye(32, dtype=np.float32)
    return c


_NC_CACHE = {}


def kernel(x_prompt, x_sample, cache_k_win, cache_v_win, state_gla, state_ffn_conv,
           w_in, w_a2, b_a, g_gla_norm, w_o, g_pre_mix, g_post_mix, g_pre_ffn, g_post_ffn,
           w_up, conv_w, conv_b, w_down):
    f = lambda a: np.ascontiguousarray(np.asarray(a, dtype=np.float32))
    if "nc" not in _NC_CACHE:
        _NC_CACHE["nc"] = build_program()
    nc = _NC_CACHE["nc"]
    consts = _constants()
    shared = {
        "w_in": f(w_in[0]), "w_a2": f(w_a2[0]), "b_a": f(b_a[0]).reshape(1, 256),
        "g_gla": f(g_gla_norm[0]).reshape(1, 512), "w_o": f(w_o[0]),
        "g_pre_mix": f(g_pre_mix[0]).reshape(1, D), "g_post_mix": f(g_post_mix[0]).reshape(1, D),
        "g_pre_ffn": f(g_pre_ffn[0]).reshape(1, D), "g_post_ffn": f(g_post_ffn[0]).reshape(1, D),
        "w_up": f(w_up[0]), "conv_w": f(conv_w[0]), "conv_b": f(conv_b[0]).reshape(1, DFF), "w_down": f(w_down[0]),
    }
    shared.update(consts)
    x_prompt = np.asarray(x_prompt); x_sample = np.asarray(x_sample)
    cache_k_win = np.asarray(cache_k_win); cache_v_win = np.asarray(cache_v_win)
    state_gla = np.asarray(state_gla); state_ffn_conv = np.asarray(state_ffn_conv)
    in_maps = []
    for c in range(NCORES):
        m = dict(shared)
        m["xp"] = f(x_prompt[NSEQ * c:NSEQ * (c + 1)])
        m["xs"] = f(x_sample[NSS * c:NSS * (c + 1)]).reshape(NST, D)
        m["ck"] = f(cache_k_win[0, NSS * c:NSS * (c + 1)]).reshape(NSS, SEQ, 512)
        m["cv"] = f(cache_v_win[0, NSS * c:NSS * (c + 1)]).reshape(NSS, SEQ, 512)
        m["sg"] = f(state_gla[0, NSS * c:NSS * (c + 1)])
        m["sc"] = f(state_ffn_conv[0, NSS * c:NSS * (c + 1)]).reshape(NSS * 2, DFF)
        in_maps.append(m)
    res = run_bass_kernel_spmd(nc, in_maps, core_ids=list(range(NCORES)))
    R = res.results
    cat = lambda k: np.concatenate([np.asarray(r[k], dtype=np.float32) for r in R], axis=0)
    y_prompt = cat("y_p")
    y_sample = cat("y_s").reshape(32, TS, D)
    wk_p = cat("wk_p").reshape(1, 16, SEQ, HA, 64)
    wv_p = cat("wv_p").reshape(1, 16, SEQ, HA, 64)
    gs_p = cat("gs_p").reshape(1, 16, 4, 64, 128)
    fc_p = cat("fc_p").reshape(1, 16, 2, DFF)
    wk_s = cat("wk_s").reshape(1, 32, TS, HA, 64)
    wv_s = cat("wv_s").reshape(1, 32, TS, HA, 64)
    gs_s = cat("gs_s").reshape(1, 32, 4, 64, 128)
    fc_s = cat("fc_s").reshape(1, 32, 2, DFF)
    return (y_prompt, y_sample, wk_p, wv_p, gs_p, fc_p, wk_s, wv_s, gs_s, fc_s)
```
